# Optimizing a Trainium2 kernel written in Bass

```python
import jax
import jax.numpy as jnp
from jax import lax
import numpy as np

D_MODEL = 1024
BATCH = 8
SEQ = 2048
DEPTH = 2
DEC_BATCH = 128
DEC_SEQ = 4
PAST_LEN = 16384
PAGE_SIZE = 128

N_EVEN = (DEPTH + 1) // 2
N_ODD = DEPTH // 2
N_SUB = 3
EPS = 1e-6
A_HEADS = 8
A_KV_HEADS = 2
A_HEAD_DIM = 64
A_GROUP = A_HEADS // A_KV_HEADS
WINDOW = 128
ROPE_THETA = 500000.0
ROT_DIM = A_HEAD_DIM // 4
B_WIDTH = D_MODEL // 2
B_BLOCKS = 8
B_BLOCK_DIM = B_WIDTH // B_BLOCKS
CONV_W = 4
RG_C = 8.0
C_HEADS = 8
C_KEY_DIM = D_MODEL // C_HEADS
C_VAL_DIM = D_MODEL // C_HEADS
C_CHUNK = 32
D_FF = ((8 * D_MODEL // 3 + 127) // 128) * 128
A_Q = A_HEADS * A_HEAD_DIM
A_KV = A_KV_HEADS * A_HEAD_DIM
EVEN_IN = A_Q + 2 * A_KV + 2 * B_WIDTH
EVEN_MIX = A_Q + B_WIDTH
C_HK = C_HEADS * C_KEY_DIM
ODD_MIX = C_HEADS * C_VAL_DIM
ODD_IN = 2 * C_HK + 2 * ODD_MIX

kernel_name = 'hybrid_swa_rglru_hgrn2_macaron_step'


def _rms_norm(x, gain):
    xf = x.astype(jnp.float32)
    inv = lax.rsqrt(jnp.mean(xf * xf, axis=-1, keepdims=True) + EPS)
    return (xf * inv).astype(x.dtype) * gain.astype(x.dtype)


def _pre(x, c, g_pre, ada_w, ada_b):
    mod = jax.nn.silu(c) @ ada_w + ada_b
    shift, scale, gate = jnp.split(mod[:, None, :], 3, axis=-1)
    return _rms_norm(x, g_pre) * (1 + scale) + shift, gate


def _post(x, y, gate, g_post, res_w):
    return x + res_w * (1 + gate) * _rms_norm(y, g_post)


def _swiglu(h, w_in, w_out):
    g, u = jnp.split(h @ w_in, 2, axis=-1)
    return (jax.nn.silu(g) * u) @ w_out


def _rope_partial(x, pos):
    inv_freq = ROPE_THETA ** (-jnp.arange(0, ROT_DIM, 2, dtype=jnp.float32) / ROT_DIM)
    ang = pos.astype(jnp.float32)[:, None] * inv_freq[None, :]
    cos = jnp.cos(ang)[None, :, None, :]
    sin = jnp.sin(ang)[None, :, None, :]
    half = ROT_DIM // 2
    x1 = x[..., :half].astype(jnp.float32)
    x2 = x[..., half:ROT_DIM].astype(jnp.float32)
    rot = jnp.concatenate([x1 * cos - x2 * sin, x2 * cos + x1 * sin], axis=-1).astype(x.dtype)
    return jnp.concatenate([rot, x[..., ROT_DIM:]], axis=-1)


def _sink_attention(q, k, v, q_pos, k_pos, sinks):
    B, N, Tq, H, Dh = q.shape
    qg = q.reshape(B, N, Tq, A_KV_HEADS, A_GROUP, Dh)
    s = jnp.einsum('bnqkgd,bnskd->bnkgqs', qg, k).astype(jnp.float32) / np.float32(np.sqrt(Dh))
    rel = q_pos[:, :, None] - k_pos[:, None, :]
    mask = (rel >= 0) & (rel < WINDOW) & (k_pos[:, None, :] >= 0)
    s = jnp.where(mask[None, :, None, None], s, -jnp.inf)
    sink = jnp.broadcast_to(sinks.astype(jnp.float32).reshape(1, 1, A_KV_HEADS, A_GROUP, 1, 1), s.shape[:-1] + (1,))
    p = jax.nn.softmax(jnp.concatenate([s, sink], axis=-1), axis=-1)[..., :-1].astype(v.dtype)
    o = jnp.einsum('bnkgqs,bnskd->bnqkgd', p, v)
    return o.reshape(B, N, Tq, H, Dh)


def _swa_prompt(q, k, v, sinks):
    B, T, H, Dh = q.shape
    N = T // WINDOW
    qb = q.reshape(B, N, WINDOW, H, Dh)
    kb = k.reshape(B, N, WINDOW, A_KV_HEADS, Dh)
    vb = v.reshape(B, N, WINDOW, A_KV_HEADS, Dh)
    prev = lambda a: jnp.pad(a, ((0, 0), (1, 0), (0, 0), (0, 0), (0, 0)))[:, :-1]
    kk = jnp.concatenate([prev(kb), kb], axis=2)
    vv = jnp.concatenate([prev(vb), vb], axis=2)
    pos = jnp.arange(T, dtype=jnp.int32).reshape(N, WINDOW)
    k_pos = jnp.concatenate([pos - WINDOW, pos], axis=1)
    return _sink_attention(qb, kk, vv, pos, k_pos, sinks).reshape(B, T, H, Dh)


def _swa_sample(q, k, v, prefix_k, prefix_v, pos0, sinks):
    B, T, H, Dh = q.shape
    P = prefix_k.shape[1]
    kk = jnp.concatenate([prefix_k.astype(k.dtype), k], axis=1)
    vv = jnp.concatenate([prefix_v.astype(v.dtype), v], axis=1)
    q_pos = pos0 + jnp.arange(T, dtype=jnp.int32)
    k_pos = jnp.concatenate([pos0 - P + jnp.arange(P, dtype=jnp.int32), q_pos])
    o = _sink_attention(q[:, None], kk[:, None], vv[:, None], q_pos[None], k_pos[None], sinks)
    return o[:, 0], kk[:, -P:], vv[:, -P:]


def _lin_combine(left, right):
    a1, b1 = left
    a2, b2 = right
    return a1 * a2, a2 * b1 + b2


def _rglru_branch(xr, conv_buf, h0, conv_w, conv_b, wa, ba, wx, bx, lam):
    B, T, W = xr.shape
    xp = jnp.concatenate([conv_buf.astype(xr.dtype), xr], axis=1)
    xc = conv_b
    for j in range(CONV_W):
        xc = xc + xp[:, j:j + T] * conv_w[j]
    new_buf = xp[:, T:]
    xb = xc.reshape(B, T, B_BLOCKS, B_BLOCK_DIM)
    r = jax.nn.sigmoid(jnp.einsum('btkd,kde->btke', xb, wa).reshape(B, T, W) + ba)
    i = jax.nn.sigmoid(jnp.einsum('btkd,kde->btke', xb, wx).reshape(B, T, W) + bx)
    log_a = -RG_C * r.astype(jnp.float32) * jax.nn.softplus(-lam.astype(jnp.float32))
    a = jnp.exp(log_a)
    mult = jnp.sqrt(jnp.maximum(1.0 - jnp.exp(2.0 * log_a), 0.0))
    b = mult * (i * xc).astype(jnp.float32)
    b = b.at[:, 0].add(a[:, 0] * h0.astype(jnp.float32))
    _, hs = lax.associative_scan(_lin_combine, (a, b), axis=1)
    return hs.astype(xr.dtype), new_buf, hs[:, -1]


def _even_mixer(h, pos0, prefix_k, prefix_v, conv_buf, h0, w_in, w_out, sinks,
                conv_w, conv_b, wa, ba, wx, bx, lam):
    B, T, _ = h.shape
    q, k, v, xg, xr = jnp.split(h @ w_in, [A_Q, A_Q + A_KV, A_Q + 2 * A_KV, A_Q + 2 * A_KV + B_WIDTH], axis=-1)
    pos = pos0 + jnp.arange(T, dtype=jnp.int32)
    q = _rope_partial(q.reshape(B, T, A_HEADS, A_HEAD_DIM), pos)
    k = _rope_partial(k.reshape(B, T, A_KV_HEADS, A_HEAD_DIM), pos)
    v = v.reshape(B, T, A_KV_HEADS, A_HEAD_DIM)
    if prefix_k is None:
        o_a = _swa_prompt(q, k, v, sinks)
        win = min(WINDOW, PAST_LEN)
        k_win, v_win = k[:, -win:], v[:, -win:]
    else:
        o_a, k_win, v_win = _swa_sample(q, k, v, prefix_k, prefix_v, pos0, sinks)
    o_b, new_buf, h_last = _rglru_branch(xr, conv_buf, h0, conv_w, conv_b, wa, ba, wx, bx, lam)
    o_b = jax.nn.gelu(xg, approximate=True) * o_b
    y = jnp.concatenate([o_a.reshape(B, T, A_Q), o_b], axis=-1) @ w_out
    return y, k_win, v_win, new_buf, h_last


def _hgrn2_chunked(q, log_f, k, v, s0):
    B, T, H, K = q.shape
    V = v.shape[-1]
    L = min(C_CHUNK, T)
    N = -(-T // L)
    pad = N * L - T

    def chunks(a):
        a = jnp.pad(a.astype(jnp.float32), ((0, 0), (0, pad), (0, 0), (0, 0)))
        return jnp.moveaxis(a.reshape(B, N, L, H, a.shape[-1]), 1, 0)

    qc, kc, vc = chunks(q), chunks(k), chunks(v)
    bc = jnp.cumsum(chunks(log_f), axis=2)
    causal = jnp.tril(jnp.ones((L, L), dtype=bool))[None, :, :, None, None]

    def step(S, xs):
        qi, ki, vi, bi = xs
        diff = bi[:, :, None] - bi[:, None, :]
        dec = jnp.where(causal, jnp.exp(jnp.minimum(diff, 0.0)), 0.0)
        scores = jnp.einsum('bthk,bshk,btshk->bhts', qi, ki, dec)
        o = jnp.einsum('bhts,bshv->bthv', scores, vi) + jnp.einsum('bthk,bhkv->bthv', qi * jnp.exp(bi), S)
        b_last = bi[:, -1]
        S_new = jnp.exp(b_last)[..., None] * S + jnp.einsum('bshk,bshv->bhkv', ki * jnp.exp(b_last[:, None] - bi), vi)
        return S_new, o

    s_last, o = lax.scan(step, s0.astype(jnp.float32), (qc, kc, vc, bc))
    o = jnp.moveaxis(o, 0, 1).reshape(B, N * L, H, V)[:, :T]
    return o, s_last


def _lower_bound(lb_logits, layer):
    gam = jax.nn.softmax(lb_logits.astype(jnp.float32), axis=0)
    return jnp.cumsum(gam, axis=0)[layer] - gam[0]


def _odd_mixer(h, s0, lb, w_in, w_out, gnorm):
    B, T, _ = h.shape
    q, fz, v, g = jnp.split(h @ w_in, [C_HK, 2 * C_HK, 2 * C_HK + ODD_MIX], axis=-1)
    f = lb + (1.0 - lb) * jax.nn.sigmoid(fz.astype(jnp.float32))
    shp = (B, T, C_HEADS, C_KEY_DIM)
    o, s_last = _hgrn2_chunked(q.reshape(shp), jnp.log(f).reshape(shp), (1.0 - f).reshape(shp),
                               v.reshape(B, T, C_HEADS, C_VAL_DIM), s0)
    o = _rms_norm(o, gnorm) * jax.nn.silu(g.reshape(B, T, C_HEADS, C_VAL_DIM).astype(jnp.float32))
    y = o.reshape(B, T, ODD_MIX).astype(h.dtype) @ w_out
    return y, s_last


def _trunk(x, c, pos0, k_win, v_win, conv_st, h_st, s_st, W):
    B = x.shape[0]
    new_k, new_v, new_conv, new_h, new_s = [], [], [], [], []
    for l in range(DEPTH):
        h, gate = _pre(x, c, W['norm_pre'][l, 0], W['ada_w'][l, 0], W['ada_b'][l, 0])
        x = _post(x, _swiglu(h, W['ffn1_w_in'][l], W['ffn1_w_out'][l]), gate, W['norm_post'][l, 0], 0.5)
        h, gate = _pre(x, c, W['norm_pre'][l, 1], W['ada_w'][l, 1], W['ada_b'][l, 1])
        if l % 2 == 0:
            e = l // 2
            if k_win is None:
                pk, pv = None, None
                cb = jnp.zeros((B, CONV_W - 1, B_WIDTH), x.dtype)
                h0 = jnp.zeros((B, B_WIDTH), jnp.float32)
            else:
                pk, pv, cb, h0 = k_win[e], v_win[e], conv_st[e], h_st[e]
            y, kw, vw, cbn, hl = _even_mixer(h, pos0, pk, pv, cb, h0, W['even_w_in'][e], W['even_w_out'][e],
                                            W['attn_sinks'][e], W['rg_conv_w'][e], W['rg_conv_b'][e],
                                            W['rg_wa'][e], W['rg_ba'][e], W['rg_wx'][e], W['rg_bx'][e],
                                            W['rg_lambda'][e])
            new_k.append(kw)
            new_v.append(vw)
            new_conv.append(cbn)
            new_h.append(hl)
        else:
            o = l // 2
            s0 = jnp.zeros((B, C_HEADS, C_KEY_DIM, C_VAL_DIM), jnp.float32) if s_st is None else s_st[o]
            lb = _lower_bound(W['hgrn_lb_logits'], l)
            y, sl = _odd_mixer(h, s0, lb, W['odd_w_in'][o], W['odd_w_out'][o], W['hgrn_gnorm'][o])
            new_s.append(sl)
        x = _post(x, y, gate, W['norm_post'][l, 1], 1.0)
        h, gate = _pre(x, c, W['norm_pre'][l, 2], W['ada_w'][l, 2], W['ada_b'][l, 2])
        x = _post(x, _swiglu(h, W['ffn2_w_in'][l], W['ffn2_w_out'][l]), gate, W['norm_post'][l, 2], 0.5)
    return (x, jnp.stack(new_k), jnp.stack(new_v), jnp.stack(new_conv), jnp.stack(new_h), jnp.stack(new_s))


def setup_inputs(seed: int = 0) -> dict:
    key = jax.random.key(seed)
    ks = iter(jax.random.split(key, 48))
    nrm = lambda shape, scale: scale * jax.random.normal(next(ks), shape, jnp.float32)
    win = min(WINDOW, PAST_LEN)
    D = D_MODEL
    a0 = jax.random.uniform(next(ks), (N_EVEN, B_WIDTH), jnp.float32, minval=0.9, maxval=0.999)
    s_a = a0 ** (1.0 / RG_C)
    rg_lambda = jnp.log(s_a) - jnp.log1p(-s_a)
    return {
        'x_prompt': nrm((BATCH, SEQ, D), 1.0),
        'x_sample': nrm((DEC_BATCH, DEC_SEQ, D), 1.0),
        'c_prompt': nrm((BATCH, D), 1.0),
        'c_sample': nrm((DEC_BATCH, D), 1.0),
        'cache_k_win': nrm((N_EVEN, DEC_BATCH, win, A_KV_HEADS, A_HEAD_DIM), 1.0),
        'cache_v_win': nrm((N_EVEN, DEC_BATCH, win, A_KV_HEADS, A_HEAD_DIM), 1.0),
        'state_conv_rglru': nrm((N_EVEN, DEC_BATCH, CONV_W - 1, B_WIDTH), 1.0),
        'state_h_rglru': nrm((N_EVEN, DEC_BATCH, B_WIDTH), 0.5),
        'state_s_hgrn': nrm((N_ODD, DEC_BATCH, C_HEADS, C_KEY_DIM, C_VAL_DIM), 0.5),
        'norm_pre': 1.0 + nrm((DEPTH, N_SUB, D), 0.05),
        'norm_post': 1.0 + nrm((DEPTH, N_SUB, D), 0.05),
        'ada_w': nrm((DEPTH, N_SUB, D, 3 * D), 0.2 * D ** -0.5),
        'ada_b': nrm((DEPTH, N_SUB, 3 * D), 0.01),
        'ffn1_w_in': nrm((DEPTH, D, 2 * D_FF), D ** -0.5),
        'ffn1_w_out': nrm((DEPTH, D_FF, D), D_FF ** -0.5),
        'ffn2_w_in': nrm((DEPTH, D, 2 * D_FF), D ** -0.5),
        'ffn2_w_out': nrm((DEPTH, D_FF, D), D_FF ** -0.5),
        'even_w_in': nrm((N_EVEN, D, EVEN_IN), D ** -0.5),
        'even_w_out': nrm((N_EVEN, EVEN_MIX, D), EVEN_MIX ** -0.5),
        'attn_sinks': nrm((N_EVEN, A_HEADS), 1.0),
        'rg_conv_w': nrm((N_EVEN, CONV_W, B_WIDTH), CONV_W ** -0.5),
        'rg_conv_b': nrm((N_EVEN, B_WIDTH), 0.01),
        'rg_wa': nrm((N_EVEN, B_BLOCKS, B_BLOCK_DIM, B_BLOCK_DIM), B_BLOCK_DIM ** -0.5),
        'rg_ba': nrm((N_EVEN, B_WIDTH), 0.01),
        'rg_wx': nrm((N_EVEN, B_BLOCKS, B_BLOCK_DIM, B_BLOCK_DIM), B_BLOCK_DIM ** -0.5),
        'rg_bx': nrm((N_EVEN, B_WIDTH), 0.01),
        'rg_lambda': rg_lambda,
        'odd_w_in': nrm((N_ODD, D, ODD_IN), D ** -0.5),
        'odd_w_out': nrm((N_ODD, ODD_MIX, D), ODD_MIX ** -0.5),
        'hgrn_lb_logits': nrm((DEPTH, C_HK), 0.5),
        'hgrn_gnorm': 1.0 + nrm((N_ODD, C_VAL_DIM), 0.05),
    }


def reference(x_prompt, x_sample, c_prompt, c_sample, cache_k_win, cache_v_win, state_conv_rglru,
              state_h_rglru, state_s_hgrn, norm_pre, norm_post, ada_w, ada_b, ffn1_w_in, ffn1_w_out,
              ffn2_w_in, ffn2_w_out, even_w_in, even_w_out, attn_sinks, rg_conv_w, rg_conv_b, rg_wa,
              rg_ba, rg_wx, rg_bx, rg_lambda, odd_w_in, odd_w_out, hgrn_lb_logits, hgrn_gnorm):
    W = dict(norm_pre=norm_pre, norm_post=norm_post, ada_w=ada_w, ada_b=ada_b,
             ffn1_w_in=ffn1_w_in, ffn1_w_out=ffn1_w_out, ffn2_w_in=ffn2_w_in, ffn2_w_out=ffn2_w_out,
             even_w_in=even_w_in, even_w_out=even_w_out, attn_sinks=attn_sinks,
             rg_conv_w=rg_conv_w, rg_conv_b=rg_conv_b, rg_wa=rg_wa, rg_ba=rg_ba, rg_wx=rg_wx,
             rg_bx=rg_bx, rg_lambda=rg_lambda, odd_w_in=odd_w_in, odd_w_out=odd_w_out,
             hgrn_lb_logits=hgrn_lb_logits, hgrn_gnorm=hgrn_gnorm)
    y_prompt, k_win_p, v_win_p, conv_p, h_p, s_p = _trunk(x_prompt, c_prompt, 0, None, None, None, None, None, W)
    y_sample, k_win_s, v_win_s, conv_s, h_s, s_s = _trunk(x_sample, c_sample, PAST_LEN, cache_k_win, cache_v_win,
                                                         state_conv_rglru, state_h_rglru, state_s_hgrn, W)
    return (y_prompt, y_sample, k_win_p, v_win_p, conv_p, h_p, s_p, k_win_s, v_win_s, conv_s, h_s, s_s)
```

```python
import numpy as np
from contextlib import ExitStack
import concourse.bass as bass
import concourse.mybir as mybir
from concourse.bass_utils import run_bass_kernel_spmd

F32 = mybir.dt.float32
BF16 = mybir.dt.bfloat16
AF = mybir.ActivationFunctionType
ALU = mybir.AluOpType

ENGS = ['pe', 'act', 'dve', 'pool', 'sp']
D = 1024
DFF = 2816
NP = 2048
NS = 64
TG = 1088
NCORES = 8
EPS = 1e-6


class Sched:
    EPOCH = 3000

    def __init__(self, nc):
        self.nc = nc
        self.q = {e: [] for e in ENGS}
        self.cnt = {e: 0 for e in ENGS}
        self.sems = {}
        self.waited = {e: {} for e in ENGS}
        self.lastw = {}
        self.readers = {}
        self.dcount = {}
        self.nsem = 0
        self.pool_out = []
        self.pool_sum = 0

    def sem(self, key):
        if key not in self.sems:
            self.sems[key] = self.nc.alloc_semaphore(name="s%d" % self.nsem)
            self.nsem += 1
        return self.sems[key]

    def _filter(self, eng, deps):
        out = {}
        wd = self.waited[eng]
        for key, val in deps:
            if eng == 'pe' and key[0] == 'e' and key[1] == 'pe':
                continue
            if wd.get(key, 0) >= val:
                continue
            if out.get(key, 0) < val:
                out[key] = val
        for k, v in out.items():
            wd[k] = v
        return list(out.items())

    def _deps(self, eng, reads, writes):
        deps = []
        for r in reads:
            if r in self.lastw:
                deps.append(self.lastw[r])
        for w in writes:
            if w in self.lastw:
                deps.append(self.lastw[w])
            deps.extend(self.readers.get(w, ()))
        return self._filter(eng, deps)

    def _book(self, me, reads, writes):
        for r in reads:
            self.readers.setdefault(r, []).append(me)
        for w in writes:
            self.lastw[w] = me
            self.readers[w] = []

    def op(self, eng, fn, reads=(), writes=()):
        waits = self._deps(eng, reads, writes)
        idx = self.cnt[eng]
        self.cnt[eng] += 1
        key = ('e', eng, idx // self.EPOCH)
        self.sem(key)
        val = idx % self.EPOCH + 1
        self.q[eng].append((waits, fn, key, 1))
        self._book((key, val), reads, writes)

    POOL_DESC_LIMIT = 1 << 40

    def dma(self, eng, fn, semname, reads=(), writes=(), ndesc=1024):
        waits = self._deps(eng, reads, writes)
        if eng == 'pool':
            extra = []
            while self.pool_out and self.pool_sum + ndesc > self.POOL_DESC_LIMIT:
                k0, v0, n0 = self.pool_out.pop(0)
                self.pool_sum -= n0
                extra.append((k0, v0))
            if extra:
                waits = waits + self._filter(eng, extra)
        key = ('d', semname)
        self.sem(key)
        self.dcount[key] = self.dcount.get(key, 0) + 16
        self.q[eng].append((waits, fn, key, 16))
        self._book((key, self.dcount[key]), reads, writes)
        if eng == 'pool':
            self.pool_out.append((key, self.dcount[key], ndesc))
            self.pool_sum += ndesc

    def raw(self, eng, fn):
        self.q[eng].append(([], fn, None, 0))

    def _now(self, engs):
        deps = []
        for e in engs:
            if self.cnt[e] > 0:
                idx = self.cnt[e] - 1
                deps.append((('e', e, idx // self.EPOCH), idx % self.EPOCH + 1))
        return deps

    def barrier(self, engs=('pe', 'act', 'dve'), dma_prefix=('o_', 'l_'), wait_engs=None):
        deps = self._now(list(engs) + ['pool'])
        for key, v in self.dcount.items():
            if key[1].startswith(dma_prefix):
                deps.append((key, v))
        if wait_engs is None:
            wait_engs = engs
        for e in list(wait_engs) + ['sp']:
            w = self._filter(e, deps)
            if w:
                self.q[e].append((w, None, None, 0))

    def finish(self, eng='sp'):
        deps = self._now(ENGS)
        for key, v in self.dcount.items():
            deps.append((key, v))
        self.q[eng].append((deps, None, None, 0))

    def emit(self, block):
        sems = self.sems
        qs = self.q

        def run(engobj, lst):
            for waits, fn, key, inc in lst:
                for k, v in waits:
                    engobj.wait_ge(sems[k], v)
                if fn is not None:
                    ins = fn(engobj)
                    if key is not None:
                        ins.then_inc(sems[key], inc)

        @block.tensor
        def _(e):
            run(e, qs['pe'])

        @block.scalar
        def _(e):
            run(e, qs['act'])

        @block.vector
        def _(e):
            run(e, qs['dve'])

        @block.gpsimd
        def _(e):
            run(e, qs['pool'])

        @block.sync
        def _(e):
            run(e, qs['sp'])


class Rot:
    def __init__(self, n):
        self.n = n
        self.i = 0
        self.held = set()

    def __call__(self):
        for _ in range(self.n):
            v = self.i
            self.i = (self.i + 1) % self.n
            if v not in self.held:
                return v
        raise RuntimeError("all slots held")

    def hold(self, *vs):
        self.held.update(vs)

    def release(self, *vs):
        self.held.difference_update(vs)


def build_nc(do_even=True, do_odd=True, do_ffn=True, groups=(0, 1), estop=9, ostop=9):
    nc = bass.Bass("TRN2", target_bir_lowering=False)

    def din(name, shape):
        return nc.dram_tensor(name, list(shape), F32, kind="ExternalInput").ap()

    def dout(name, shape):
        return nc.dram_tensor(name, list(shape), F32, kind="ExternalOutput").ap()

    xT = din("xT", [D, NP + NS])
    cT = din("cT", [D, 17])
    cachek = din("cachek", [16, 128, 128])
    cachev = din("cachev", [16, 128, 128])
    convS = din("convS", [128, 4, 16, 3])
    h0S = din("h0S", [128, 4, 16])
    s0S = din("s0S", [16, 8, 128, 128])
    normpre = din("normpre", [128, 6, 8])
    normpost = din("normpost", [128, 6, 8])
    adab = din("adab", [128, 6, 24])
    ada_w = din("ada_w", [2, 3, D, 3 * D])
    ffn_w_in = [din("ffn1_w_in", [2, D, 2 * DFF]), din("ffn2_w_in", [2, D, 2 * DFF])]
    ffn_w_out = [din("ffn1_w_out", [2, DFF, D]), din("ffn2_w_out", [2, DFF, D])]
    even_w_in = din("even_w_in", [D, 1792])
    even_w_out = din("even_w_out", [D, D])
    sinkT = din("sinkT", [128, 4])
    convw = din("convw", [128, 4, 4])
    convb = din("convb", [128, 4])
    rg_wa = din("rg_wa", [8, 64, 64])
    rg_wx = din("rg_wx", [8, 64, 64])
    rgba = din("rgba", [128, 4])
    rgbx = din("rgbx", [128, 4])
    rglam = din("rglam", [128, 4])
    odd_w_in = din("odd_w_in", [D, 4096])
    odd_w_out = din("odd_w_out", [D, D])
    lbl = din("lbl", [128, 2, 8])
    gnorm = din("gnorm", [128, 1])
    ident_d = din("ident", [128, 128])
    amask_d = din("amask", [128, 256])
    hmask_d = din("hmask", [128, 128])
    smask_d = din("smask", [64, 64])
    mc_d = din("mcmask", [128, 4])
    rmask_d = din("rmask", [128, 512 + 64])
    onehot_d = din("onehot", [64, 16])
    ropec_d = din("ropec", [128, NP + NS])
    ropes_d = din("ropes", [128, NP + NS])

    yT = dout("yT", [D, NP + NS])
    kwp = dout("kwp", [128, 128])
    vwp = dout("vwp", [128, 128])
    convp = dout("convp", [128, 4, 3])
    hp = dout("hp", [128, 4])
    sp_o = dout("sp_o", [8, 128, 128])
    kws = dout("kws", [16, 128, 128])
    vws = dout("vws", [16, 128, 128])
    convs = dout("convs", [128, 4, 16, 3])
    hs_o = dout("hs_o", [128, 4, 16])
    ss_o = dout("ss_o", [16, 8, 128, 128])

    es = ExitStack()

    def sb(name, shape, dt):
        return es.enter_context(nc.sbuf_tensor("s_" + name, list(shape), dt))

    with es:
        s = Sched(nc)
        x = sb("x", [128, 8, TG], F32)
        h = sb("h", [128, 8, TG], BF16)
        ARW = 21120
        arena = sb("arena", [128, ARW], F32)
        NWS = 4
        wst = sb("wst", [128, NWS, 4096], BF16)
        NT32 = 6
        t32 = sb("t32", [128, NT32, 512], F32)
        NT16 = 4
        t16 = sb("t16", [128, NT16, 512], BF16)
        NINV = 2
        inv = sb("inv", [128, NINV, 512], F32)
        Amod = sb("Amod", [128, 6, 8, 17], F32)
        Smod = sb("Smod", [128, 6, 8, 17], F32)
        Gmod = sb("Gmod", [128, 6, 8, 17], F32)
        ones_bf = sb("ones_bf", [128, 128], BF16)
        ident_f = sb("ident_f", [128, 128], F32)
        ident_b = sb("ident_b", [128, 128], BF16)
        epsc = sb("epsc", [128, 1], F32)
        onec = sb("onec", [128, 1], F32)
        amask = sb("amask", [128, 256], BF16)
        hmask = sb("hmask", [128, 128], BF16)
        smask = sb("smask", [64, 64], BF16)
        mcm = sb("mcm", [128, 4], BF16)
        rmask = sb("rmask", [128, 576], F32)
        onehot = sb("onehot", [64, 16], F32)
        npre = sb("npre", [128, 6, 8], F32)
        npost = sb("npost", [128, 6, 8], F32)
        adabs = sb("adabs", [128, 6, 24], F32)
        sinkexp = sb("sinkexp", [128, 4], F32)
        cw = sb("cw", [128, 4, 4], F32)
        cb = sb("cb", [128, 4], F32)
        hba = sb("hba", [128, 4], F32)
        hbx = sb("hbx", [128, 4], F32)
        hcoef = sb("hcoef", [128, 4], F32)
        BDa = sb("BDa", [128, 4, 128], BF16)
        BDx = sb("BDx", [128, 4, 128], BF16)
        hc0 = sb("hc0", [128, 8], F32)
        hc1 = sb("hc1", [128, 8], F32)
        gn = sb("gn", [128, 1], F32)
        convc = sb("convc", [128, 4, 3], F32)
        hcar = sb("hcar", [128, 4], F32)
        sm1 = sb("sm1", [128, 2, 8], F32)
        ps = es.enter_context(nc.psum_tensor("ps", [128, 8, 512], F32))

        nb = Rot(8)
        n32 = Rot(NT32)
        n16 = Rot(NT16)
        ninv = Rot(NINV)
        nws = Rot(NWS)

        def mm(out, lhsT, rhs, start, stop, reads, writes, **kw):
            s.op('pe', lambda e: e.matmul(out, lhsT=lhsT, rhs=rhs, start=start, stop=stop, **kw), reads, writes)

        def tr(out, in_, ident, reads, writes):
            s.op('pe', lambda e: e.transpose(out=out, in_=in_, identity=ident), reads, writes)

        SET6 = (AF.Ln, AF.Exp, AF.Square, AF.Copy, AF.Identity)
        actstate = {'cur6': False}

        def act(out, in_, func, reads, writes, bias=None, scale=None, force6=False):
            if func not in SET6:
                actstate['cur6'] = False
            kw = {}
            if bias is not None:
                kw['bias'] = bias
            if scale is not None:
                kw['scale'] = scale
            s.op('act', lambda e: e.activation(out=out, in_=in_, func=func, **kw), reads, writes)

        def tt(out, in0, in1, op, reads, writes, eng='dve'):
            s.op(eng, lambda e: e.tensor_tensor(out=out, in0=in0, in1=in1, op=op), reads, writes)

        def ts(out, in0, s1, s2, op0, op1, reads, writes, eng='dve'):
            s.op(eng, lambda e: e.tensor_scalar(out=out, in0=in0, scalar1=s1, scalar2=s2, op0=op0, op1=op1), reads, writes)

        def stt(out, in0, scalar, in1, op0, op1, reads, writes):
            s.op('dve', lambda e: e.scalar_tensor_tensor(out=out, in0=in0, scalar=scalar, in1=in1, op0=op0, op1=op1), reads, writes)

        def cp(out, in_, reads, writes, eng='dve'):
            if eng == 'act':
                s.op('act', lambda e: e.activation(out=out, in_=in_, func=AF.Copy), reads, writes)
            else:
                s.op(eng, lambda e: e.tensor_copy(out, in_), reads, writes)

        def recip(out, in_, reads, writes, scratch=None):
            if scratch is None:
                s.op('dve', lambda e: e.reciprocal(out, in_), reads, writes)
            else:
                s.op('dve', lambda e: e.reciprocal_approx_accurate(out, in_, scratch), reads, writes)

        def memset(ap, val, writes, eng='dve'):
            s.op(eng, lambda e: e.memset(ap, val), (), writes)

        def scan(out, d0, d1, init, reads, writes):
            s.op('dve', lambda e: e.tensor_tensor_scan(out=out, data0=d0, data1=d1, initial=init, op0=ALU.mult, op1=ALU.add), reads, writes)

        def dma(eng, out, in_, sem, reads=(), writes=()):
            nd = 1
            for d_ in list(out.shape)[:-1]:
                nd *= d_
            s.dma(eng, lambda e: e.dma_start(out=out, in_=in_), sem, reads, writes, ndesc=nd)

        def av(off_words, shape, dt):
            n = 1
            for d_ in shape[1:]:
                n *= d_
            if dt == BF16:
                assert n % 2 == 0
                w = n // 2
                v = arena[:, off_words:off_words + w].bitcast(BF16)
            else:
                w = n
                v = arena[:, off_words:off_words + w]
            assert off_words + w <= ARW, (off_words, w, ARW)
            if len(shape) == 3:
                v = v.rearrange("p (a b) -> p a b", a=shape[1])
            elif len(shape) == 4:
                v = v.rearrange("p (a b c) -> p a b c", a=shape[1], b=shape[2])
            return v[0:shape[0]], off_words + w

        def wslot(i, shape):
            v = wst[:, i, :]
            n = 1
            for d_ in shape[1:]:
                n *= d_
            v = v[:, 0:n]
            if len(shape) == 3:
                v = v.rearrange("p (a b) -> p a b", a=shape[1])
            elif len(shape) == 4:
                v = v.rearrange("p (a b c) -> p a b c", a=shape[1], b=shape[2])
            return v

        K = 'const'
        for (dst, src) in [(ident_f, ident_d), (rmask, rmask_d), (onehot, onehot_d), (npre, normpre), (npost, normpost),
                           (adabs, adab), (cw, convw), (cb, convb), (gn, gnorm)]:
            dma('sp', dst[:], src, 'l_c_%s' % src.name, writes=[K])
        for (dst, src) in [(ident_b, ident_d), (amask, amask_d), (hmask, hmask_d), (smask, smask_d), (mcm, mc_d)]:
            dma('pool', dst[:], src, 'l_cb_%s' % src.name, writes=[K])
        memset(ones_bf[:], 1.0, [K])
        memset(epsc[:], EPS, [K])
        memset(onec[:], 1.0, [K])
        memset(BDa[:], 0.0, ['BD'])
        memset(BDx[:], 0.0, ['BD'])
        for k in range(8):
            c, hf = k // 2, k % 2
            dma('pool', BDa[hf * 64:(hf + 1) * 64, c, hf * 64:(hf + 1) * 64], rg_wa[k], 'l_bd', reads=(), writes=['BD'])
            dma('pool', BDx[hf * 64:(hf + 1) * 64, c, hf * 64:(hf + 1) * 64], rg_wx[k], 'l_bd', reads=(), writes=['BD'])
        dma('sp', sinkexp[:], sinkT, 'l_p1', writes=['p_sink'])
        act(sinkexp[:], sinkexp[:], AF.Exp, ['p_sink'], ['p_sink'])
        dma('sp', hba[:], rgba, 'l_p2', writes=['p_hba'])
        ts(hba[:], hba[:], 0.5, None, ALU.mult, ALU.bypass, ['p_hba'], ['p_hba'])
        dma('sp', hbx[:], rgbx, 'l_p3', writes=['p_hbx'])
        ts(hbx[:], hbx[:], 0.5, None, ALU.mult, ALU.bypass, ['p_hbx'], ['p_hbx'])
        dma('sp', hcoef[:], rglam, 'l_p4', writes=['p_hc'])
        act(hcoef[:], hcoef[:], AF.Exp, ['p_hc'], ['p_hc'], scale=-1.0)
        act(hcoef[:], hcoef[:], AF.Ln, ['p_hc', K], ['p_hc'], bias=onec[:], scale=1.0)
        ts(hcoef[:], hcoef[:], -4.0, None, ALU.mult, ALU.bypass, ['p_hc'], ['p_hc'])
        dma('sp', sm1[:], lbl, 'l_p5', writes=['p_lb'])
        tt(hc0[:], sm1[:, 1, :], sm1[:, 0, :], ALU.subtract, ['p_lb'], ['p_hc0'])
        act(hc0[:], hc0[:], AF.Tanh, ['p_hc0'], ['p_hc0'], scale=0.5)
        ts(hc1[:], hc0[:], -0.25, 0.25, ALU.mult, ALU.add, ['p_hc0'], ['p_hc1'])
        ts(hc0[:], hc0[:], 0.25, 0.75, ALU.mult, ALU.add, ['p_hc0', 'p_hc1'], ['p_hc0'])
        s.barrier()
        PK = [K, 'BD', 'p_sink', 'p_hba', 'p_hbx', 'p_hc', 'p_hc0', 'p_hc1']

        sc, o = av(0, [128, 8, 17], BF16)
        cts, o = av(o, [128, 8, 17], F32)
        modr, o = av(o, [128, 6, 24, 17], F32)
        dma('sp', cts, cT.rearrange("(c p) s -> p c s", p=128), 'l_p6', writes=['cts'])
        act(sc, cts, AF.Silu, ['cts'], ['sc'])
        for sub in range(6):
            l, k3 = sub // 3, sub % 3
            bank = nb()
            for blk in range(6):
                sl = nws()
                wv = wslot(sl, [128, 8, 512])
                dma('pool', wv, ada_w[l, k3, :, blk * 512:(blk + 1) * 512].rearrange("(c p) f -> p c f", p=128),
                    'w%d' % sl, writes=[('ws', sl)])
                for fl in range(4):
                    fc = blk * 4 + fl
                    for kc in range(8):
                        mm(ps[:, bank, fc * 17:(fc + 1) * 17], wv[:, kc, fl * 128:(fl + 1) * 128], sc[:, kc, :],
                           kc == 0, kc == 7, [('ws', sl), 'sc'], [('ps', bank)])
            tt(modr[:, sub, :, :], ps[:, bank, 0:408].rearrange("p (a b) -> p a b", a=24),
               adabs[:, sub, :].unsqueeze(2).to_broadcast([128, 24, 17]), ALU.add, [('ps', bank), K], [('modr', sub)])
            stt(Amod[:, sub], modr[:, sub, 8:16, :], 1.0, npre[:, sub, :].unsqueeze(2).to_broadcast([128, 8, 17]),
                ALU.add, ALU.mult, [('modr', sub), K], ['mods'])
            cp(Smod[:, sub], modr[:, sub, 0:8, :], [('modr', sub)], ['mods'])
            stt(Gmod[:, sub], modr[:, sub, 16:24, :], 1.0, npost[:, sub, :].unsqueeze(2).to_broadcast([128, 8, 17]),
                ALU.add, ALU.mult, [('modr', sub), K], ['mods'])
            if k3 != 1:
                ts(Gmod[:, sub], Gmod[:, sub], 0.5, None, ALU.mult, ALU.bypass, ['mods'], ['mods'])
        s.barrier()

        def rms_inv(src_fn, rkeys, n, scale):
            bank = nb()
            for c in range(8):
                q = n16()
                if True:
                    act(t16[:, q, :n], src_fn(c), AF.Square, rkeys(c), [('t16', q)])
                else:
                    tt(t16[:, q, :n], src_fn(c), src_fn(c), ALU.mult, rkeys(c), [('t16', q)], eng='pool')
                mm(ps[:, bank, :n], ones_bf[:], t16[:, q, :n], c == 0, c == 7, [K, ('t16', q)], [('ps', bank)])
            r = n32()
            iv = ninv()
            act(t32[:, r, :n], ps[:, bank, :n], AF.Ln, [('ps', bank), K], [('t32', r)], bias=epsc[:], scale=scale, force6=True)
            act(inv[:, iv, :n], t32[:, r, :n], AF.Exp, [('t32', r)], [('inv', iv)], scale=-0.5, force6=True)
            return iv

        def expand_mod(src, sub):
            r = n32()
            cp(t32[:, r, :].rearrange("p (c s i) -> p c s i", c=8, s=16),
               src[:, sub, :, 1:17].unsqueeze(3).to_broadcast([128, 8, 16, 4]), ['mods'], [('t32', r)])
            return r

        def prenorm_tile(sub, ti, tile):
            t0, n, kind = tile
            iv = rms_inv(lambda c: x[:, c, t0:t0 + n], lambda c: [('x', ti, c)], n, 1.0 / D)
            if kind == 'p':
                for c in range(8):
                    r = n32()
                    stt(t32[:, r, :n], x[:, c, t0:t0 + n], Amod[:, sub, c, 0:1], inv[:, iv, :n], ALU.mult, ALU.mult,
                        [('x', ti, c), 'mods', ('inv', iv)], [('t32', r)])
                    act(h[:, c, t0:t0 + n], t32[:, r, :n], AF.Identity, [('t32', r), 'mods'], [('h', ti, c)],
                        bias=Smod[:, sub, c, 0:1], scale=1.0)
            else:
                r = n32()
                v = t32[:, r, :].rearrange("p (c t) -> p c t", c=8)
                allx = [('x', ti, c) for c in range(8)]
                tt(v, x[:, :, t0:t0 + n], inv[:, iv, :n].unsqueeze(1).to_broadcast([128, 8, n]), ALU.mult,
                   allx + [('inv', iv)], [('t32', r)])
                ra = expand_mod(Amod, sub)
                tt(v, v, t32[:, ra, :].rearrange("p (c t) -> p c t", c=8), ALU.mult, [('t32', r), ('t32', ra)], [('t32', r)])
                rs = expand_mod(Smod, sub)
                tt(h[:, :, t0:t0 + n], v, t32[:, rs, :].rearrange("p (c t) -> p c t", c=8), ALU.add, [('t32', r), ('t32', rs)],
                   [('h', ti, c) for c in range(8)])

        def postnorm(sub, ti, tile, yb, ykey):
            t0, n, kind = tile
            iv = rms_inv(yb, lambda c: [ykey(c)], n, 1.0 / D)
            if kind == 'p':
                for c in range(8):
                    r = n32()
                    stt(t32[:, r, :n], yb(c), Gmod[:, sub, c, 0:1], inv[:, iv, :n], ALU.mult, ALU.mult,
                        [ykey(c), 'mods', ('inv', iv)], [('t32', r)])
                    tt(x[:, c, t0:t0 + n], x[:, c, t0:t0 + n], t32[:, r, :n], ALU.add, [('x', ti, c), ('t32', r)], [('x', ti, c)], eng='pool')
            else:
                rg = expand_mod(Gmod, sub)
                gv = t32[:, rg, :].rearrange("p (c t) -> p c t", c=8)
                n32.hold(rg)
                for c in range(8):
                    r = n32()
                    tt(t32[:, r, :n], yb(c), inv[:, iv, :n], ALU.mult, [ykey(c), ('inv', iv)], [('t32', r)])
                    tt(t32[:, r, :n], t32[:, r, :n], gv[:, c, :], ALU.mult, [('t32', r), ('t32', rg)], [('t32', r)])
                    tt(x[:, c, t0:t0 + n], x[:, c, t0:t0 + n], t32[:, r, :n], ALU.add, [('x', ti, c), ('t32', r)], [('x', ti, c)], eng='pool')
                n32.release(rg)

        def ffn(l, which, sub, tiles, hook, pre):
            a_, o = av(0, [128, 22, TG], BF16)
            yb_, o = av(o, [128, 8, TG], F32)
            w_in = ffn_w_in[which][l]
            w_out = ffn_w_out[which][l]
            for blk in range(11):
                sl = nws()
                wv = wslot(sl, [128, 2, 8, 256])
                for gu in range(2):
                    c0 = gu * DFF + blk * 256
                    dma('pool', wv[:, gu], w_in[:, c0:c0 + 256].rearrange("(c p) f -> p c f", p=128), 'w%d' % sl,
                        writes=[('ws', sl)])
                for ti, (t0, n, kind) in enumerate(tiles):
                    if blk == 0:
                        pre(ti, tiles[ti])
                    for jj in range(2):
                        j = blk * 2 + jj
                        bg, bu = nb(), nb()
                        for gu, bk in ((0, bg), (1, bu)):
                            for kc in range(8):
                                mm(ps[:, bk, :n], wv[:, gu, kc, jj * 128:(jj + 1) * 128], h[:, kc, t0:t0 + n], kc == 0, kc == 7,
                                   [('ws', sl), ('h', ti, kc)], [('ps', bk)])
                        r = n32()
                        act(t32[:, r, :n], ps[:, bg, :n], AF.Silu, [('ps', bg)], [('t32', r)])
                        tt(a_[:, j, t0:t0 + n], t32[:, r, :n], ps[:, bu, :n], ALU.mult, [('t32', r), ('ps', bu)], [('a', j, ti)])
            jblocks = [(0, 4), (4, 4), (8, 4), (12, 4), (16, 4), (20, 2)]
            for bi, (j0, nj) in enumerate(jblocks):
                sl = nws()
                wv = wslot(sl, [128, 4, 1024])
                dma('pool', wv[:, 0:nj, :], w_out[j0 * 128:(j0 + nj) * 128, :].rearrange("(j p) f -> p j f", p=128), 'w%d' % sl,
                    writes=[('ws', sl)])
                for ti, (t0, n, kind) in enumerate(tiles):
                    for dc in range(8):
                        bank = nb()
                        for jl in range(nj):
                            mm(ps[:, bank, :n], wv[:, jl, dc * 128:(dc + 1) * 128], a_[:, j0 + jl, t0:t0 + n], jl == 0, jl == nj - 1,
                               [('ws', sl), ('a', j0 + jl, ti)], [('ps', bank)])
                        if bi == 0:
                            cp(yb_[:, dc, t0:t0 + n], ps[:, bank, :n], [('ps', bank)], [('yb', ti, dc)], eng='act')
                        else:
                            tt(yb_[:, dc, t0:t0 + n], yb_[:, dc, t0:t0 + n], ps[:, bank, :n], ALU.add, [('yb', ti, dc), ('ps', bank)], [('yb', ti, dc)])
            for ti, tile in enumerate(tiles):
                t0, n, kind = tile
                postnorm(sub, ti, tile, lambda c: yb_[:, c, t0:t0 + n], lambda c: ('yb', ti, c))
                hook(ti, tile)

        def outproj(sub, tiles, w_dram_chunk, src_fn, src_keys, ybt, hook):
            sls = []
            for half in range(2):
                sl = nws()
                wv = wslot(sl, [128, 8, 512])
                for (dst, ap) in w_dram_chunk(wv, half):
                    dma('pool', dst, ap, 'w%d' % sl, writes=[('ws', sl)])
                sls.append((sl, wv))
            for ti, tile in enumerate(tiles):
                t0, n, kind = tile
                for half in range(2):
                    sl, wv = sls[half]
                    for dcl in range(4):
                        dc = half * 4 + dcl
                        bank = nb()
                        for kc in range(8):
                            mm(ps[:, bank, :n], wv[:, kc, dcl * 128:(dcl + 1) * 128], src_fn(kc, t0, n), kc == 0, kc == 7,
                               [('ws', sl)] + src_keys(kc, ti), [('ps', bank)])
                        cp(ybt[:, dc, :n], ps[:, bank, :n], [('ps', bank)], [('ybt', dc)], eng='act')
                postnorm(sub, ti, tile, lambda c: ybt[:, c, :n], lambda c: ('ybt', c))
                hook(ti, tile)

        def even_mixer(sub, g, tiles, hook, pre):
            has_s = (g == 0)
            o = 0
            oa, o = av(o, [128, 4, TG], BF16)
            ob, o = av(o, [128, 4, TG], BF16)
            oP = o
            qr, o = av(oP, [128, 4, TG], BF16)
            kr, o = av(o, [128, 1216], BF16)
            krf, o = av(o, [128, 192], F32)
            vtok, o = av(o, [128, 10, 128], BF16)
            vwf, o = av(o, [128, 2, 128], F32)
            cosT, o = av(o, [128, TG], F32)
            sinT, o = av(o, [128, TG], F32)
            kcache, o = av(o, [128, 2, 128], F32)
            vcache, o = av(o, [128, 2, 128], F32)
            KT, o = av(o, [128, 2, 128], BF16)
            Vb, o = av(o, [128, 16, 128], BF16)
            ktr, o = av(o, [64, 128], F32)
            gl, o = av(oP, [128, 4, TG], BF16)
            xr, o = av(o, [128, 4, 3 + 1024], F32)
            xrs, o = av(o, [128, 4, 16, 7], F32)
            xc, o = av(o, [128, TG], F32)
            xcb, o = av(o, [128, TG], BF16)
            ac, o = av(o, [128, TG], F32)
            bc, o = av(o, [128, TG], F32)
            hsb, o = av(o, [128, TG], F32)
            h0s, o = av(o, [128, 4, 16], F32)
            hsS, o = av(o, [128, 4, 16], F32)
            ybt, _ = av(oP, [128, 8, 512], F32)

            dma('sp', cosT[:, 0:1024], ropec_d[:, g * 1024:(g + 1) * 1024], 'l_rope', writes=['rope'])
            dma('sp', sinT[:, 0:1024], ropes_d[:, g * 1024:(g + 1) * 1024], 'l_rope', writes=['rope'])
            if has_s:
                dma('sp', cosT[:, 1024:1088], ropec_d[:, NP:NP + NS], 'l_rope', writes=['rope'])
                dma('sp', sinT[:, 1024:1088], ropes_d[:, NP:NP + NS], 'l_rope', writes=['rope'])
            W = even_w_in
            Wv_ = W.rearrange("(c p) f -> p c f", p=128)
            slq, slqs, slk, slst = nws(), nws(), nws(), nws()
            wq = wslot(slq, [128, 8, 512])
            wqs = wslot(slqs, [128, 8, 512])
            wk = wslot(slk, [128, 8, 384])
            wstg = wslot(slst, [128, 8, 512])
            dma('pool', wstg, Wv_[:, :, 0:512], 'w%d' % slst, writes=[('ws', slst)])
            dma('pool', wk[:, :, 0:128], Wv_[:, :, 512:640], 'w%d' % slk, writes=[('ws', slk)])
            dma('pool', wk[:, :, 256:384], Wv_[:, :, 640:768], 'w%d' % slk, writes=[('ws', slk)])
            src5 = wstg.rearrange("p c (hf j d) -> p c hf j d", hf=2, j=4)
            dq5 = wq.rearrange("p c (j hf d) -> p c j hf d", j=4, hf=2)
            dqs5 = wqs.rearrange("p c (j hf d) -> p c j hf d", j=4, hf=2)
            for hf in range(2):
                cp(dq5[:, :, :, hf, :], src5[:, :, hf, :, :], [('ws', slst)], [('ws', slq)], eng='pool')
                cp(dqs5[:, :, :, hf, 0:8], src5[:, :, hf, :, 8:16], [('ws', slst)], [('ws', slqs)], eng='pool')
                cp(dqs5[:, :, :, hf, 8:16], src5[:, :, hf, :, 0:8], [('ws', slst)], [('ws', slqs)], eng='pool')
                cp(dqs5[:, :, :, hf, 16:64], src5[:, :, hf, :, 16:64], [('ws', slst)], [('ws', slqs)], eng='pool')
            ks4 = wk[:, :, 0:128].rearrange("p c (kv d) -> p c kv d", kv=2)
            kd4 = wk[:, :, 128:256].rearrange("p c (kv d) -> p c kv d", kv=2)
            cp(kd4[:, :, :, 0:8], ks4[:, :, :, 8:16], [('ws', slk)], [('ws', slk)], eng='pool')
            cp(kd4[:, :, :, 8:16], ks4[:, :, :, 0:8], [('ws', slk)], [('ws', slk)], eng='pool')
            cp(kd4[:, :, :, 16:64], ks4[:, :, :, 16:64], [('ws', slk)], [('ws', slk)], eng='pool')

            def kcol(t0):
                return 128 + t0
            for ti, (t0, n, kind) in enumerate(tiles):
                pre(ti, tiles[ti])
                hk = [('h', ti, kc) for kc in range(8)]
                for j in range(4):
                    b1, b2 = nb(), nb()
                    for kc in range(8):
                        mm(ps[:, b1, :n], wq[:, kc, j * 128:(j + 1) * 128], h[:, kc, t0:t0 + n], kc == 0, kc == 7,
                           [('ws', slq), ('h', ti, kc)], [('ps', b1)])
                    for kc in range(8):
                        mm(ps[:, b2, :n], wqs[:, kc, j * 128:(j + 1) * 128], h[:, kc, t0:t0 + n], kc == 0, kc == 7,
                           [('ws', slqs), ('h', ti, kc)], [('ps', b2)])
                    r1, r2 = n32(), n32()
                    tt(t32[:, r1, :n], ps[:, b1, :n], cosT[:, t0:t0 + n], ALU.mult, [('ps', b1), 'rope'], [('t32', r1)])
                    tt(t32[:, r2, :n], ps[:, b2, :n], sinT[:, t0:t0 + n], ALU.mult, [('ps', b2), 'rope'], [('t32', r2)])
                    tt(qr[:, j, t0:t0 + n], t32[:, r1, :n], t32[:, r2, :n], ALU.add, [('t32', r1), ('t32', r2)], [('qr', ti, j)])
                b1, b2 = nb(), nb()
                for kc in range(8):
                    mm(ps[:, b1, :n], wk[:, kc, 0:128], h[:, kc, t0:t0 + n], kc == 0, kc == 7, [('ws', slk), ('h', ti, kc)], [('ps', b1)])
                for kc in range(8):
                    mm(ps[:, b2, :n], wk[:, kc, 128:256], h[:, kc, t0:t0 + n], kc == 0, kc == 7, [('ws', slk), ('h', ti, kc)], [('ps', b2)])
                r1, r2 = n32(), n32()
                tt(t32[:, r1, :n], ps[:, b1, :n], cosT[:, t0:t0 + n], ALU.mult, [('ps', b1), 'rope'], [('t32', r1)])
                tt(t32[:, r2, :n], ps[:, b2, :n], sinT[:, t0:t0 + n], ALU.mult, [('ps', b2), 'rope'], [('t32', r2)])
                tt(kr[:, kcol(t0):kcol(t0) + n], t32[:, r1, :n], t32[:, r2, :n], ALU.add, [('t32', r1), ('t32', r2)], [('kr', ti)])
                if kind == 's':
                    tt(krf[:, 128:192], t32[:, r1, :n], t32[:, r2, :n], ALU.add, [('t32', r1), ('t32', r2)], ['krf_s'])
                elif g == 1 and ti == 1:
                    tt(krf[:, 0:128], t32[:, r1, 384:512], t32[:, r2, 384:512], ALU.add, [('t32', r1), ('t32', r2)], ['krf_p'])
                nblk = (n + 127) // 128
                for bl in range(nblk):
                    nt = min(128, n - bl * 128)
                    blk = (t0 // 128 + bl) if kind == 'p' else 8
                    bank = nb()
                    for kc in range(8):
                        mm(ps[:nt, bank, 0:128], h[:, kc, t0 + bl * 128:t0 + bl * 128 + nt], wk[:, kc, 256:384], kc == 0, kc == 7,
                           [('ws', slk), ('h', ti, kc)], [('ps', bank)])
                    cp(vtok[:nt, 1 + blk, :], ps[:nt, bank, 0:128], [('ps', bank)], [('vtok', 1 + blk)], eng='act')
                    if kind == 's':
                        cp(vwf[:nt, 1, :], ps[:nt, bank, 0:128], [('ps', bank)], ['vwf_s'], eng='act')
                    elif g == 1 and blk == 7:
                        cp(vwf[:, 0, :], ps[:, bank, 0:128], [('ps', bank)], ['vwf_p'], eng='act')
            if estop <= 1:
                return
            anorm = (int(estop * 100 + 0.5) % 10) != 5 and estop >= 2
            alvl = (int(estop * 100 + 0.5) % 10) if estop < 2 else 9
            if g == 1:
                cp(kr[:, 0:128], kcar[:], ['kcar'], [('krc',)])
                cp(vtok[:, 0, :], vcar[:], ['vcar'], [('vtok', 0)])
            def tile_of(col):
                return col // 512
            for b in range(8 if estop >= 2 else int((estop - 1) * 10 + 0.5)):
                has_prev = not (g == 0 and b == 0)
                tq = tile_of(b * 128)
                kkeys = [('kr', tile_of(b * 128))]
                if has_prev:
                    kkeys.append(('kr', tile_of((b - 1) * 128)) if b > 0 else ('krc',))
                bo, bd = nb(), nb()
                nb.hold(bo, bd)
                for jp in range(2):
                    banks = [nb(), nb()]
                    for hf in range(2):
                        p0 = hf * 64
                        for jl in range(2):
                            j = jp * 2 + jl
                            qa = qr[p0:p0 + 64, j, b * 128:(b + 1) * 128]
                            mm(ps[:, banks[hf], (jl * 2) * 128:(jl * 2 + 1) * 128], kr[p0:p0 + 64, 128 * (1 + b):128 * (2 + b)], qa, True, True,
                               kkeys + [('qr', tq, j)], [('ps', banks[hf])])
                            if has_prev:
                                mm(ps[:, banks[hf], (jl * 2 + 1) * 128:(jl * 2 + 2) * 128], kr[p0:p0 + 64, 128 * b:128 * (1 + b)], qa, True, True,
                                   kkeys + [('qr', tq, j)], [('ps', banks[hf])])
                    for hf in range(2):
                        p0 = hf * 64
                        bank = banks[hf]
                        q = n16()
                        if alvl < 1:
                            continue
                        if has_prev:
                            act(t16[:, q, :], ps[:, bank, :], AF.Exp, [('ps', bank)], [('t16', q)], scale=0.125)
                            ev = t16[:, q, :].rearrange("p (a b) -> p a b", a=2)
                            tt(ev, ev, amask[:].unsqueeze(1).to_broadcast([128, 2, 256]), ALU.mult, [('t16', q), K], [('t16', q)])
                        else:
                            ev4 = t16[:, q, :].rearrange("p (a b c) -> p a b c", a=2, b=2)
                            pv4 = ps[:, bank, :].rearrange("p (a b c) -> p a b c", a=2, b=2)
                            act(ev4[:, :, 0, :], pv4[:, :, 0, :], AF.Exp, [('ps', bank)], [('t16', q)], scale=0.125)
                            tt(ev4[:, :, 0, :], ev4[:, :, 0, :], amask[:, 0:128].unsqueeze(1).to_broadcast([128, 2, 128]), ALU.mult,
                               [('t16', q), K], [('t16', q)])
                        for jl in range(2 if alvl >= 2 else 0):
                            j = jp * 2 + jl
                            ed = t16[:, q, (jl * 2) * 128:(jl * 2 + 1) * 128]
                            ep = t16[:, q, (jl * 2 + 1) * 128:(jl * 2 + 2) * 128]
                            mm(ps[p0:p0 + 64, bo, j * 128:(j + 1) * 128], vtok[:, 1 + b, p0:p0 + 64], ed, True, not has_prev,
                               [('vtok', 1 + b), ('t16', q)], [('ps', bo)])
                            if has_prev:
                                mm(ps[p0:p0 + 64, bo, j * 128:(j + 1) * 128], vtok[:, b, p0:p0 + 64], ep, False, True,
                                   [('vtok', b), ('t16', q)], [('ps', bo)])
                            if alvl < 3:
                                continue
                            mm(ps[p0:p0 + 64, bd, j * 128:(j + 1) * 128], ones_bf[:, 0:64], ed, True, not has_prev, [K, ('t16', q)], [('ps', bd)])
                            if has_prev:
                                mm(ps[p0:p0 + 64, bd, j * 128:(j + 1) * 128], ones_bf[:, 0:64], ep, False, True, [K, ('t16', q)], [('ps', bd)])
                nb.release(bo, bd)
                if not anorm:
                    continue
                r = n32()
                rv = t32[:, r, :].rearrange("p (a b) -> p a b", a=4)
                tt(rv, ps[:, bd, :].rearrange("p (a b) -> p a b", a=4), sinkexp[:].unsqueeze(2).to_broadcast([128, 4, 128]), ALU.add,
                   [('ps', bd), 'p_sink'], [('t32', r)])
                r2 = n32()
                recip(t32[:, r2, :], t32[:, r, :], [('t32', r)], [('t32', r2)])
                tt(oa[:, :, b * 128:(b + 1) * 128], ps[:, bo, :].rearrange("p (a b) -> p a b", a=4),
                   t32[:, r2, :].rearrange("p (a b) -> p a b", a=4), ALU.mult, [('ps', bo), ('t32', r2)], [('oa', b // 4)])
            if g == 0:
                cp(kcar[:], kr[:, 128 * 8:128 * 9], [('kr', 1)], ['kcar'])
                cp(vcar[:], vtok[:, 8, :], [('vtok', 8)], ['vcar'])
            else:
                dma('sp', kwp, krf[:, 0:128], 'o_kwp', reads=['krf_p'])
                dma('sp', vwp, vwf[:, 0, :], 'o_vwp', reads=['vwf_p'])
            if estop <= 2:
                return
            if has_s:
                TS = 1024
                dma('sp', kws[:, 0:124, :], cachek[:, 4:128, :], 'o_kws')
                dma('sp', vws[:, 0:124, :], cachev[:, 4:128, :], 'o_vws')
                bsc = [nb(), nb()]
                nb.hold(*bsc)
                for sq in range(16):
                    rr = sq % 2
                    dma('sp', kcache[:, rr, :], cachek[sq], 'l_kc%d' % rr, writes=[('kcache', rr)])
                    dma('sp', vcache[:, rr, :], cachev[sq], 'l_vc%d' % rr, writes=[('vcache', rr)])
                    bt = nb()
                    tr(ps[:, bt, 0:128], kcache[:, rr, :], ident_f[:], [('kcache', rr), K], [('ps', bt)])
                    cp(KT[:, rr, :], ps[:, bt, 0:128], [('ps', bt)], [('KT', rr)], eng='act')
                    cp(Vb[:, sq, :], vcache[:, rr, :], [('vcache', rr)], [('Vb', sq)], eng='dve')
                    for hf in range(2):
                        p0 = hf * 64
                        for j in range(4):
                            c0 = (sq * 4 + j) * 4
                            mm(ps[:, bsc[hf], c0:c0 + 4], KT[p0:p0 + 64, rr, :], qr[p0:p0 + 64, j, TS + sq * 4:TS + sq * 4 + 4], True, True,
                               [('KT', rr), ('qr', 2, j)], [('ps', bsc[hf])])
                qc = [n16(), n16()]
                for hf in range(2):
                    act(t16[:, qc[hf], 0:256], ps[:, bsc[hf], 0:256], AF.Exp, [('ps', bsc[hf])], [('t16', qc[hf])], scale=0.125)
                    ecv = t16[:, qc[hf], 0:256].rearrange("p (a b) -> p a b", b=4)
                    tt(ecv, ecv, mcm[:].unsqueeze(1).to_broadcast([128, 64, 4]), ALU.mult, [('t16', qc[hf]), K], [('t16', qc[hf])])
                nb.release(*bsc)
                bn = [nb(), nb()]
                for hf in range(2):
                    p0 = hf * 64
                    for j in range(4):
                        mm(ps[0:64, bn[hf], j * 64:(j + 1) * 64], kr[p0:p0 + 64, 128 + TS:128 + TS + 64], qr[p0:p0 + 64, j, TS:TS + 64], True, True,
                           [('kr', 2), ('qr', 2, j)], [('ps', bn[hf])])
                qn = [n16(), n16()]
                for hf in range(2):
                    act(t16[0:64, qn[hf], 0:256], ps[0:64, bn[hf], 0:256], AF.Exp, [('ps', bn[hf])], [('t16', qn[hf])], scale=0.125)
                    env = t16[0:64, qn[hf], 0:256].rearrange("p (a b) -> p a b", a=4)
                    tt(env, env, smask[:].unsqueeze(1).to_broadcast([64, 4, 64]), ALU.mult, [('t16', qn[hf]), K], [('t16', qn[hf])])
                bo, bd = nb(), nb()
                for j in range(4):
                    for hf in range(2):
                        p0 = hf * 64
                        en_ = t16[0:64, qn[hf], j * 64:(j + 1) * 64]
                        mm(ps[p0:p0 + 64, bo, j * 64:(j + 1) * 64], vtok[0:64, 9, p0:p0 + 64], en_, True, False,
                           [('vtok', 9), ('t16', qn[hf])], [('ps', bo)], skip_group_check=True)
                        mm(ps[p0:p0 + 64, bd, j * 64:(j + 1) * 64], ones_bf[0:64, 0:64], en_, True, False, [K, ('t16', qn[hf])], [('ps', bd)],
                           skip_group_check=True)
                        for sq in range(16):
                            c1 = (sq * 4 + j) * 4
                            ec_ = t16[:, qc[hf], c1:c1 + 4]
                            mm(ps[p0:p0 + 64, bo, j * 64 + sq * 4:j * 64 + sq * 4 + 4], Vb[:, sq, p0:p0 + 64], ec_, False, True,
                               [('Vb', sq), ('t16', qc[hf])], [('ps', bo)], skip_group_check=True)
                            mm(ps[p0:p0 + 64, bd, j * 64 + sq * 4:j * 64 + sq * 4 + 4], ones_bf[:, 0:64], ec_, False, True,
                               [K, ('t16', qc[hf])], [('ps', bd)], skip_group_check=True)
                r = n32()
                rv = t32[:, r, 0:256].rearrange("p (a b) -> p a b", a=4)
                tt(rv, ps[:, bd, 0:256].rearrange("p (a b) -> p a b", a=4), sinkexp[:].unsqueeze(2).to_broadcast([128, 4, 64]), ALU.add,
                   [('ps', bd), 'p_sink'], [('t32', r)])
                r2 = n32()
                recip(t32[:, r2, 0:256], t32[:, r, 0:256], [('t32', r)], [('t32', r2)])
                tt(oa[:, :, TS:TS + 64], ps[:, bo, 0:256].rearrange("p (a b) -> p a b", a=4),
                   t32[:, r2, 0:256].rearrange("p (a b) -> p a b", a=4), ALU.mult, [('ps', bo), ('t32', r2)], [('oa', 2)])
                bt = nb()
                tr(ps[0:64, bt, 0:128], krf[:, 128:192], ident_f[:], ['krf_s', K], [('ps', bt)])
                cp(ktr[:], ps[0:64, bt, 0:128], [('ps', bt)], ['ktr'], eng='act')
                for sq in range(16):
                    dma('sp', kws[sq, 124:128, :], ktr[sq * 4:sq * 4 + 4, :], 'o_kws2', reads=['ktr'])
                    dma('sp', vws[sq, 124:128, :], vwf[sq * 4:sq * 4 + 4, 1, :], 'o_vws2', reads=['vwf_s'])

            if estop <= 3:
                return
            s.barrier()
            slg, slr = nws(), nws()
            wg = wslot(slg, [128, 8, 512])
            wr = wslot(slr, [128, 8, 512])
            dma('pool', wg, Wv_[:, :, 768:1280], 'w%d' % slg, writes=[('ws', slg)])
            dma('pool', wr, Wv_[:, :, 1280:1792], 'w%d' % slr, writes=[('ws', slr)])
            if g == 0:
                memset(xr[:, :, 0:3], 0.0, ['xr_c'])
                dma('sp', xrs[:, :, :, 0:3], convS, 'l_cs1', writes=['xrs_c'])
                dma('sp', h0s, h0S, 'l_cs2', writes=['h0s'])
            else:
                cp(xr[:, :, 0:3], convc[:], ['convc'], ['xr_c'])
            for ti, (t0, n, kind) in enumerate(tiles):
                for j in range(4):
                    bank = nb()
                    for kc in range(8):
                        mm(ps[:, bank, :n], wg[:, kc, j * 128:(j + 1) * 128], h[:, kc, t0:t0 + n], kc == 0, kc == 7,
                           [('ws', slg), ('h', ti, kc)], [('ps', bank)])
                    r1, r2 = n32(), n32()
                    act(t32[:, r1, :n], ps[:, bank, :n], AF.Square, [('ps', bank)], [('t32', r1)])
                    ts(t32[:, r1, :n], t32[:, r1, :n], 0.044715, 1.0, ALU.mult, ALU.add, [('t32', r1)], [('t32', r1)])
                    tt(t32[:, r2, :n], t32[:, r1, :n], ps[:, bank, :n], ALU.mult, [('t32', r1), ('ps', bank)], [('t32', r2)])
                    act(t32[:, r1, :n], t32[:, r2, :n], AF.Tanh, [('t32', r2)], [('t32', r1)], scale=0.7978845608028654)
                    stt(gl[:, j, t0:t0 + n], t32[:, r1, :n], 1.0, ps[:, bank, :n], ALU.add, ALU.mult, [('t32', r1), ('ps', bank)], [('gl', ti, j)])
                    bank = nb()
                    for kc in range(8):
                        mm(ps[:, bank, :n], wr[:, kc, j * 128:(j + 1) * 128], h[:, kc, t0:t0 + n], kc == 0, kc == 7,
                           [('ws', slr), ('h', ti, kc)], [('ps', bank)])
                    if kind == 'p':
                        cp(xr[:, j, 3 + t0:3 + t0 + n], ps[:, bank, :n], [('ps', bank)], [('xr', j, ti)], eng='act')
                    else:
                        cp(xrs[:, j, :, 3:7], ps[:, bank, 0:64].rearrange("p (a b) -> p a b", b=4), [('ps', bank)], [('xr', j, ti)], eng='act')
            nt_ = len(tiles)
            for c in range(4):
                xk = [('xr', c, ti) for ti in range(nt_)] + ['xr_c', 'xrs_c']
                ts(xc[:, 0:1024], xr[:, c, 3:1027], cw[:, c, 3:4], cb[:, c:c + 1], ALU.mult, ALU.add, xk + [K], ['xc'])
                for jj in range(3):
                    stt(xc[:, 0:1024], xr[:, c, jj:jj + 1024], cw[:, c, jj:jj + 1], xc[:, 0:1024], ALU.mult, ALU.add, xk + [K, 'xc'], ['xc'])
                if has_s:
                    xcs = xc[:, 1024:1088].rearrange("p (a b) -> p a b", b=4)
                    ts(xcs, xrs[:, c, :, 3:7], cw[:, c, 3:4], cb[:, c:c + 1], ALU.mult, ALU.add, xk + [K], ['xcs'])
                    for jj in range(3):
                        stt(xcs, xrs[:, c, :, jj:jj + 4], cw[:, c, jj:jj + 1], xcs, ALU.mult, ALU.add, xk + [K, 'xcs'], ['xcs'])
                ntot = 1088 if has_s else 1024
                cp(xcb[:, 0:ntot], xc[:, 0:ntot], ['xc', 'xcs'], ['xcb'], eng='act')
                for ti, (t0, n, kind) in enumerate(tiles):
                    b1, b2 = nb(), nb()
                    mm(ps[:, b1, :n], BDa[:, c, :], xcb[:, t0:t0 + n], True, True, ['BD', 'xcb'], [('ps', b1)])
                    mm(ps[:, b2, :n], BDx[:, c, :], xcb[:, t0:t0 + n], True, True, ['BD', 'xcb'], [('ps', b2)])
                    r1, r2, r3 = n32(), n32(), n32()
                    act(t32[:, r1, :n], ps[:, b1, :n], AF.Tanh, [('ps', b1), 'p_hba'], [('t32', r1)], bias=hba[:, c:c + 1], scale=0.5)
                    act(ac[:, t0:t0 + n], t32[:, r1, :n], AF.Exp, [('t32', r1), 'p_hc'], [('ac', ti)], bias=hcoef[:, c:c + 1], scale=hcoef[:, c:c + 1])
                    act(t32[:, r2, :n], ps[:, b2, :n], AF.Tanh, [('ps', b2), 'p_hbx'], [('t32', r2)], bias=hbx[:, c:c + 1], scale=0.5)
                    tt(t32[:, r1, :n], ac[:, t0:t0 + n], ac[:, t0:t0 + n], ALU.mult, [('ac', ti), ('t32', r1)], [('t32', r1)])
                    ts(t32[:, r1, :n], t32[:, r1, :n], -1.0, 1.0, ALU.mult, ALU.add, [('t32', r1)], [('t32', r1)])
                    ts(t32[:, r1, :n], t32[:, r1, :n], 0.0, None, ALU.max, ALU.bypass, [('t32', r1)], [('t32', r1)])
                    act(t32[:, r3, :n], t32[:, r1, :n], AF.Sqrt, [('t32', r1)], [('t32', r3)])
                    stt(t32[:, r2, :n], t32[:, r2, :n], 1.0, xc[:, t0:t0 + n], ALU.add, ALU.mult, [('t32', r2), 'xc', 'xcs'], [('t32', r2)])
                    stt(bc[:, t0:t0 + n], t32[:, r2, :n], 0.5, t32[:, r3, :n], ALU.mult, ALU.mult, [('t32', r2), ('t32', r3)], [('bc', ti)])
                allab = [('ac', ti) for ti in range(nt_)] + [('bc', ti) for ti in range(nt_)]
                if has_s:
                    a0 = ac[:, 1024:1088].rearrange("p (a b) -> p a b", b=4)[:, :, 0]
                    b0 = bc[:, 1024:1088].rearrange("p (a b) -> p a b", b=4)[:, :, 0]
                    r = n32()
                    tt(t32[:, r, 0:16], a0, h0s[:, c, :], ALU.mult, allab + ['h0s'], [('t32', r)])
                    tt(b0, b0, t32[:, r, 0:16], ALU.add, allab + [('t32', r)], [('bc', 2)])
                    memset(a0, 0.0, [('ac', 2)])
                init = 0.0 if g == 0 else hcar[:, c:c + 1]
                scan(hsb[:, 0:1024], ac[:, 0:1024], bc[:, 0:1024], init, allab + ['hcar'], ['hsb'])
                if has_s:
                    scan(hsb[:, 1024:1088], ac[:, 1024:1088], bc[:, 1024:1088], 0.0, allab, ['hsbs'])
                stt(ob[:, c, 0:ntot], gl[:, c, 0:ntot], 0.5, hsb[:, 0:ntot], ALU.mult, ALU.mult,
                    [('gl', ti, c) for ti in range(nt_)] + ['hsb', 'hsbs'], [('ob', c)])
                cp(hcar[:, c:c + 1], hsb[:, 1023:1024], ['hsb'], ['hcar'])
                if has_s:
                    cp(hsS[:, c, :], hsb[:, 1024:1088].rearrange("p (a b) -> p a b", b=4)[:, :, 3], ['hsbs'], ['hsS'])
            cp(convc[:], xr[:, :, 1024:1027], [('xr', c, 1) for c in range(4)], ['convc'])
            if g == 1:
                dma('sp', hp, hcar[:], 'o_hp', reads=['hcar'])
                dma('sp', convp, convc[:], 'o_cp', reads=['convc'])
            if has_s:
                dma('sp', hs_o, hsS, 'o_hs', reads=['hsS'])
                dma('sp', convs, xrs[:, :, :, 4:7], 'o_cs', reads=[('xr', c, 2) for c in range(4)])
            if estop <= 4:
                return
            s.barrier()
            Wo = even_w_out

            def wchunk(wv, half):
                cs = slice(half * 512, (half + 1) * 512)
                out = []
                for hf in range(2):
                    out.append((wv[hf * 64:(hf + 1) * 64, 0:4, :],
                                Wo[hf * 256:(hf + 1) * 256, cs].rearrange("(j d) f -> d j f", j=4)))
                out.append((wv[:, 4:8, :], Wo[512:1024, cs].rearrange("(c p) f -> p c f", p=128)))
                return out

            def src(kc, t0, n):
                return oa[:, kc, t0:t0 + n] if kc < 4 else ob[:, kc - 4, t0:t0 + n]

            def srck(kc, ti):
                return [('oa', ti)] if kc < 4 else [('ob', kc - 4)]
            outproj(sub, tiles, wchunk, src, srck, ybt, hook)

        def odd_mixer(sub, g, tiles, hook, pre):
            has_s = (g == 0)
            o = 0
            qe, o = av(o, [128, 8, TG], BF16)
            ybt, _ = av(0, [128, 8, 512], F32)
            ke, o = av(o, [128, 8, TG], BF16)
            sg, o = av(o, [128, 8, TG], BF16)
            vtk, o = av(o, [128, 9, 1024], BF16)
            keT, o = av(o, [128, 2, 1024], BF16)
            Sbf, o = av(o, [128, 8, 128], BF16)
            tmpS, o = av(o, [128, 8, 128], F32)
            ebl, o = av(o, [128, 8, 32], F32)
            wsc, o = av(o, [128, 8, 32], F32)
            ebr, o = av(o, [128, 8, 32], F32)
            ebls, o = av(o, [128, 8, 16], F32)
            zT = h
            W = odd_w_in
            Wv_ = W.rearrange("(c p) f -> p c f", p=128)
            for hd in range(8):
                sl = nws()
                wv = wslot(sl, [128, 2, 8, 128])
                dma('pool', wv[:, 0], Wv_[:, :, hd * 128:(hd + 1) * 128], 'w%d' % sl, writes=[('ws', sl)])
                dma('pool', wv[:, 1], Wv_[:, :, 1024 + hd * 128:1024 + (hd + 1) * 128], 'w%d' % sl, writes=[('ws', sl)])
                for ti, (t0, n, kind) in enumerate(tiles):
                    if hd == 0:
                        pre(ti, tiles[ti])
                    bq, bf = nb(), nb()
                    for kc in range(8):
                        mm(ps[:, bq, :n], wv[:, 0, kc, :], h[:, kc, t0:t0 + n], kc == 0, kc == 7, [('ws', sl), ('h', ti, kc)], [('ps', bq)])
                    for kc in range(8):
                        mm(ps[:, bf, :n], wv[:, 1, kc, :], h[:, kc, t0:t0 + n], kc == 0, kc == 7, [('ws', sl), ('h', ti, kc)], [('ps', bf)])
                    rA, rB, rC, rD = n32(), n32(), n32(), n32()
                    A_, B_, C_, D_ = t32[:, rA, :n], t32[:, rB, :n], t32[:, rC, :n], t32[:, rD, :n]
                    act(A_, ps[:, bf, :n], AF.Tanh, [('ps', bf)], [('t32', rA)], scale=0.5)
                    ts(B_, A_, hc1[:, hd:hd + 1], hc0[:, hd:hd + 1], ALU.mult, ALU.add, [('t32', rA), 'p_hc0', 'p_hc1'], [('t32', rB)])
                    act(C_, B_, AF.Ln, [('t32', rB)], [('t32', rC)])
                    ts(B_, B_, -1.0, 1.0, ALU.mult, ALU.add, [('t32', rB), ('t32', rC)], [('t32', rB)])
                    if kind == 'p':
                        scan(A_, rmask[:, 0:n], C_, 0.0, [K, ('t32', rC), ('t32', rA)], [('t32', rA)])
                        nch = n // 32
                        ch0 = t0 // 32
                        bv = A_.rearrange("p (c l) -> p c l", l=32)
                        tt(D_.rearrange("p (c l) -> p c l", l=32), bv, bv[:, :, 15].unsqueeze(2).to_broadcast([128, nch, 32]), ALU.subtract,
                           [('t32', rA)], [('t32', rD)])
                        act(ebl[:, hd, ch0:ch0 + nch], bv[:, :, 31], AF.Exp, [('t32', rA)], [('ebl', hd, ti)])
                        act(ebr[:, hd, ch0:ch0 + nch], bv[:, :, 15], AF.Exp, [('t32', rA)], [('ebr', hd, ti)])
                        act(wsc[:, hd, ch0:ch0 + nch], D_.rearrange("p (c l) -> p c l", l=32)[:, :, 31], AF.Exp, [('t32', rD)], [('wsc', hd, ti)])
                        dsrc, dk = D_, ('t32', rD)
                    else:
                        scan(A_, rmask[:, 512:512 + n], C_, 0.0, [K, ('t32', rC), ('t32', rA)], [('t32', rA)])
                        act(ebls[:, hd, :], A_.rearrange("p (c l) -> p c l", l=4)[:, :, 3], AF.Exp, [('t32', rA)], [('ebls', hd)])
                        dsrc, dk = A_, ('t32', rA)
                    act(C_, dsrc, AF.Exp, [dk, ('t32', rC)], [('t32', rC)])
                    tt(qe[:, hd, t0:t0 + n], ps[:, bq, :n], C_, ALU.mult, [('ps', bq), ('t32', rC)], [('qe', hd, ti)])
                    act(C_, dsrc, AF.Exp, [dk, ('t32', rC)], [('t32', rC)], scale=-1.0)
                    tt(ke[:, hd, t0:t0 + n], B_, C_, ALU.mult, [('t32', rB), ('t32', rC)], [('ke', hd, ti)])
            for half in range(2):
                sl = nws()
                wv = wslot(sl, [128, 8, 512])
                dma('pool', wv, Wv_[:, :, 2048 + half * 512:2048 + (half + 1) * 512], 'w%d' % sl, writes=[('ws', sl)])
                for ti, (t0, n, kind) in enumerate(tiles):
                    nblk = (n + 127) // 128
                    for bl in range(nblk):
                        nt = min(128, n - bl * 128)
                        blk = (t0 // 128 + bl) if kind == 'p' else 8
                        bank = nb()
                        for kc in range(8):
                            mm(ps[:nt, bank, :], h[:, kc, t0 + bl * 128:t0 + bl * 128 + nt], wv[:, kc, :], kc == 0, kc == 7,
                               [('ws', sl), ('h', ti, kc)], [('ps', bank)])
                        cp(vtk[:nt, blk, half * 512:(half + 1) * 512], ps[:nt, bank, :], [('ps', bank)], [('vtk', blk, half)], eng='act')
            for half in range(2):
                sl = nws()
                wv = wslot(sl, [128, 8, 512])
                dma('pool', wv, Wv_[:, :, 3072 + half * 512:3072 + (half + 1) * 512], 'w%d' % sl, writes=[('ws', sl)])
                for ti, (t0, n, kind) in enumerate(tiles):
                    for hl in range(4):
                        hd = half * 4 + hl
                        bank = nb()
                        for kc in range(8):
                            mm(ps[:, bank, :n], wv[:, kc, hl * 128:(hl + 1) * 128], h[:, kc, t0:t0 + n], kc == 0, kc == 7,
                               [('ws', sl), ('h', ti, kc)], [('ps', bank)])
                        act(sg[:, hd, t0:t0 + n], ps[:, bank, :n], AF.Silu, [('ps', bank)], [('sg', hd, ti)])
            s.barrier()
            allh = [('h', ti, c) for ti in range(len(tiles)) for c in range(8)]

            def finish_o(banks, ncols, zdst_fn, sg_fn, sgkeys, zkeys):
                for (bk, h0_, nh) in banks:
                    w = nh * ncols
                    q = n16()
                    act(t16[:, q, :w], ps[:, bk, :w], AF.Square, [('ps', bk)], [('t16', q)])
                    b2 = nb()
                    mm(ps[:, b2, :w], ones_bf[:], t16[:, q, :w], True, True, [K, ('t16', q)], [('ps', b2)])
                    r = n32()
                    act(t32[:, r, :w], ps[:, b2, :w], AF.Sqrt, [('ps', b2), K], [('t32', r)], bias=epsc[:], scale=1.0 / 128)
                    r2 = n32()
                    recip(t32[:, r2, :w], t32[:, r, :w], [('t32', r)], [('t32', r2)])
                    tt(t32[:, r, :w], ps[:, bk, :w], t32[:, r2, :w], ALU.mult, [('ps', bk), ('t32', r2), ('t32', r)], [('t32', r)])
                    stt(zdst_fn(h0_, nh), t32[:, r, :w].rearrange("p (a b) -> p a b", a=nh), gn[:, 0:1], sg_fn(h0_, nh), ALU.mult, ALU.mult,
                        [('t32', r), K] + sgkeys + allh, zkeys)

            if has_s:
                TS = 1024
                bs = nb()
                for hd in range(8):
                    mm(ps[0:64, bs, hd * 64:(hd + 1) * 64], ke[:, hd, TS:TS + 64], qe[:, hd, TS:TS + 64], True, True,
                       [('ke', hd, 2), ('qe', hd, 2)], [('ps', bs)])
                qp = n16()
                tt(t16[0:64, qp, :].rearrange("p (a b) -> p a b", a=8), ps[0:64, bs, :].rearrange("p (a b) -> p a b", a=8),
                   smask[:].unsqueeze(1).to_broadcast([64, 8, 64]), ALU.mult, [('ps', bs), K], [('t16', qp)])
                btr = nb()
                ptb = ps[:, btr, :].bitcast(BF16)
                for hd in range(8):
                    tr(ptb[0:64, hd * 128:(hd + 1) * 128], ke[:, hd, TS:TS + 64], ident_b[:], [('ke', hd, 2), K], [('ps', btr)])
                cp(keT[0:64, 0, :], ptb[0:64, :], [('ps', btr)], [('keT', 0)], eng='act')
                bo = nb()
                nb.hold(bo)
                for hd in range(8):
                    mm(ps[:, bo, hd * 64:(hd + 1) * 64], vtk[0:64, 8, hd * 128:(hd + 1) * 128], t16[0:64, qp, hd * 64:(hd + 1) * 64], hd == 0, False,
                       [('vtk', 8, hd // 4), ('t16', qp)], [('ps', bo)], skip_group_check=True)
                for sq in range(16):
                    rr = sq % 2
                    S0 = Sst if rr == 0 else tmpS
                    skey = 'Sst' if rr == 0 else 'tmpS'
                    dma('sp', S0, s0S[sq].rearrange("h k v -> k h v"), 'l_s0%d' % rr, writes=[skey])
                    cp(Sbf, S0, [skey], ['Sbf'], eng='act')
                    for hd in range(8):
                        c0 = hd * 64 + sq * 4
                        mm(ps[:, bo, c0:c0 + 4], Sbf[:, hd, :], qe[:, hd, TS + sq * 4:TS + sq * 4 + 4], False, True,
                           ['Sbf', ('qe', hd, 2)], [('ps', bo)], skip_group_check=True)
                    ts(keT[0:64, 1, :], keT[0:64, 0, :], onehot[:, sq:sq + 1], None, ALU.mult, ALU.bypass, [('keT', 0), K], [('keT', 1)])
                    u0, u1 = nb(), nb()
                    for hd in range(8):
                        ub = u0 if hd < 4 else u1
                        mm(ps[:, ub, (hd % 4) * 128:(hd % 4 + 1) * 128], keT[0:64, 1, hd * 128:(hd + 1) * 128], vtk[0:64, 8, hd * 128:(hd + 1) * 128],
                           True, True, [('keT', 1), ('vtk', 8, hd // 4)], [('ps', ub)])
                    for half, ub in ((0, u0), (1, u1)):
                        sv = S0[:, half * 4:(half + 1) * 4, :]
                        tt(sv, sv, ps[:, ub, :].rearrange("p (a b) -> p a b", a=4), ALU.add, [skey, ('ps', ub)], [skey])
                        tt(sv, sv, ebls[:, half * 4:(half + 1) * 4, sq:sq + 1].to_broadcast([128, 4, 128]), ALU.mult,
                           [skey] + [('ebls', hh) for hh in range(8)], [skey])
                    dma('sp', ss_o[sq].rearrange("h k v -> k h v"), S0, 'o_ss%d' % rr, reads=[skey])
                finish_o([(bo, 0, 8)], 64,
                         lambda h0_, nh: zT[:, h0_:h0_ + nh, TS:TS + 64],
                         lambda h0_, nh: sg[:, h0_:h0_ + nh, TS:TS + 64],
                         [('sg', hh, 2) for hh in range(8)], [('z', 2)])
                nb.release(bo)
                s.barrier(dma_prefix=('o_ss', 'l_s0'))
                memset(Sst, 0.0, ['Sst', ('Sst', 0), ('Sst', 1)])
            HS = [slice(0, 4), slice(4, 8)]
            for b in range(8):
                ti = b // 4
                cols = slice(b * 128, (b + 1) * 128)
                ek = [('ebr', hh, ti) for hh in range(8)] + [('ebl', hh, ti) for hh in range(8)] + [('wsc', hh, ti) for hh in range(8)]
                btr = nb()
                ptb = ps[:, btr, :].bitcast(BF16)
                for half in range(2):
                    qk = n16()
                    kdv = t16[:, qk, :].rearrange("p (h c l) -> p h c l", h=4, c=4)
                    tt(kdv, ke[:, HS[half], cols].rearrange("p h (c l) -> p h c l", c=4),
                       wsc[:, HS[half], b * 4:b * 4 + 4].unsqueeze(3).to_broadcast([128, 4, 4, 32]), ALU.mult,
                       [('ke', hh, ti) for hh in range(half * 4, half * 4 + 4)] + ek, [('t16', qk)], eng='pool')
                    for hl in range(4):
                        hd = half * 4 + hl
                        tr(ptb[:, hd * 128:(hd + 1) * 128], t16[:, qk, hl * 128:(hl + 1) * 128], ident_b[:], [('t16', qk), K], [('ps', btr)])
                kslot = b % 2
                cp(keT[:, kslot, :], ptb[:, :], [('ps', btr)], [('keT', kslot)], eng='act')
                pbanks = []
                for half in range(2):
                    bk = nb()
                    for hl in range(4):
                        hd = half * 4 + hl
                        mm(ps[:, bk, hl * 128:(hl + 1) * 128], ke[:, hd, cols], qe[:, hd, cols], True, True, [('ke', hd, ti), ('qe', hd, ti)], [('ps', bk)])
                    qp = n16()
                    tt(t16[:, qp, :].rearrange("p (a b) -> p a b", a=4), ps[:, bk, :].rearrange("p (a b) -> p a b", a=4),
                       hmask[:].unsqueeze(1).to_broadcast([128, 4, 128]), ALU.mult, [('ps', bk), K], [('t16', qp)])
                    pbanks.append(qp)
                obanks = [nb(), nb()]
                nb.hold(*obanks)
                for hd in range(8):
                    ob_ = obanks[hd // 4]
                    qp = pbanks[hd // 4]
                    hl = hd % 4
                    mm(ps[:, ob_, hl * 128:(hl + 1) * 128], vtk[:, b, hd * 128:(hd + 1) * 128], t16[:, qp, hl * 128:(hl + 1) * 128], hl == 0, False,
                       [('vtk', b, hd // 4), ('t16', qp)], [('ps', ob_)], skip_group_check=True)
                for c in range(4):
                    ch = b * 4 + c
                    for half in range(2):
                        tt(Sbf[:, HS[half], :], Sst[:, HS[half], :], ebr[:, HS[half], ch:ch + 1].to_broadcast([128, 4, 128]), ALU.mult,
                           [('Sst', half), 'Sst'] + ek, [('Sbf', half)])
                    us = [nb(), nb()]
                    for half in range(2):
                        for hl in range(4):
                            hd = half * 4 + hl
                            c0 = hl * 128 + c * 32
                            mm(ps[:, obanks[half], c0:c0 + 32], Sbf[:, hd, :], qe[:, hd, b * 128 + c * 32:b * 128 + (c + 1) * 32], False, True,
                               [('Sbf', half), ('qe', hd, ti)], [('ps', obanks[half])], skip_group_check=True)
                        for hl in range(4):
                            hd = half * 4 + hl
                            mm(ps[:, us[half], hl * 128:(hl + 1) * 128], keT[32 * c:32 * (c + 1), kslot, hd * 128:(hd + 1) * 128],
                               vtk[32 * c:32 * (c + 1), b, hd * 128:(hd + 1) * 128], True, True, [('keT', kslot), ('vtk', b, hd // 4)], [('ps', us[half])],
                               tile_position=(32 * c, 0))
                    for half in range(2):
                        tt(Sst[:, HS[half], :], Sst[:, HS[half], :], ebl[:, HS[half], ch:ch + 1].to_broadcast([128, 4, 128]), ALU.mult,
                           [('Sst', half), 'Sst'] + ek, [('Sst', half)], eng='pool')
                    for half in range(2):
                        tt(Sst[:, HS[half], :], Sst[:, HS[half], :], ps[:, us[half], :].rearrange("p (a b) -> p a b", a=4), ALU.add,
                           [('Sst', half), ('ps', us[half])], [('Sst', half)])
                finish_o([(obanks[0], 0, 4), (obanks[1], 4, 4)], 128,
                         lambda h0_, nh: zT[:, h0_:h0_ + nh, cols],
                         lambda h0_, nh: sg[:, h0_:h0_ + nh, cols],
                         [('sg', hh, ti) for hh in range(8)], [('z', ti)])
                nb.release(*obanks)
            if g == 1:
                dma('sp', sp_o.rearrange("h k v -> k h v"), Sst, 'o_sp', reads=['Sst', ('Sst', 0), ('Sst', 1)])
            s.barrier()
            Wo = odd_w_out

            def wchunk(wv, half):
                return [(wv, Wo[:, half * 512:(half + 1) * 512].rearrange("(c p) f -> p c f", p=128))]
            outproj(sub, tiles, wchunk, lambda kc, t0, n: zT[:, kc, t0:t0 + n], lambda kc, ti: [('z', ti), ('h', ti, kc)], ybt, hook)

        kcar = sb("kcar", [128, 128], BF16)
        vcar = sb("vcar", [128, 128], BF16)
        Sst = sb("Sst", [128, 8, 128], F32)[:]

        for g in groups:
            if g == 0:
                tiles = [(0, 512, 'p'), (512, 512, 'p'), (1024, 64, 's')]
            else:
                tiles = [(0, 512, 'p'), (512, 512, 'p')]
            xv = xT.rearrange("(c p) t -> p c t", p=128)
            for ti, (t0, n, kind) in enumerate(tiles):
                src_c0 = g * 1024 + t0 if kind == 'p' else NP
                dma('sp', x[:, :, t0:t0 + n], xv[:, :, src_c0:src_c0 + n], 'l_x%d' % ti, writes=[('x', ti, c) for c in range(8)])
            yv = yT.rearrange("(c p) t -> p c t", p=128)
            for l in range(2):
                for k3 in range(3):
                    sub = l * 3 + k3

                    def hook(ti, tile, sub=sub):
                        t0, n, kind = tile
                        if sub == 5:
                            dst_c0 = g * 1024 + t0 if kind == 'p' else NP
                            dma('sp', yv[:, :, dst_c0:dst_c0 + n], x[:, :, t0:t0 + n], 'o_y%d' % ti,
                                reads=[('x', ti, c) for c in range(8)])
                    enabled = do_ffn if k3 != 1 else (do_even if l == 0 else do_odd)

                    def pre(ti, tile, sub=sub):
                        prenorm_tile(sub, ti, tile)
                    if not enabled:
                        for ti, tile in enumerate(tiles):
                            pre(ti, tile)
                            hook(ti, tile)
                    elif k3 == 0 or k3 == 2:
                        ffn(l, 0 if k3 == 0 else 1, sub, tiles, hook, pre)
                    elif l == 0:
                        even_mixer(sub, g, tiles, hook, pre)
                    else:
                        odd_mixer(sub, g, tiles, hook, pre)
                    s.barrier(wait_engs=('act', 'dve'))
            s.barrier()
        s.finish()
        global _LAST_COUNTS
        _LAST_COUNTS = (dict(s.cnt), {e: len(v) for e, v in s.q.items()}, s.nsem)
        with nc.Block() as block:
            s.emit(block)
    return nc


def _consts():
    c = {}
    c['ident'] = np.eye(128, dtype=np.float32)
    k = np.arange(128)[:, None]
    q = np.arange(128)[None, :]
    am = np.zeros((128, 2, 128), np.float32)
    am[:, 0, :] = (k <= q)
    am[:, 1, :] = (k > q)
    c['amask'] = am.reshape(128, 256)
    c['hmask'] = ((k // 32 == q // 32) & (k <= q)).astype(np.float32)
    k6 = np.arange(64)[:, None]
    q6 = np.arange(64)[None, :]
    c['smask'] = ((k6 // 4 == q6 // 4) & (k6 <= q6)).astype(np.float32)
    c['mcmask'] = (np.arange(128)[:, None] > np.arange(4)[None, :]).astype(np.float32)
    rm = np.ones((128, 576), np.float32)
    rm[:, 0:512:32] = 0.0
    rm[:, 512:576:4] = 0.0
    c['rmask'] = rm
    c['onehot'] = (np.arange(64)[:, None] // 4 == np.arange(16)[None, :]).astype(np.float32)
    pos = np.concatenate([np.arange(NP, dtype=np.int32), np.tile(16384 + np.arange(4, dtype=np.int32), 16)]).astype(np.float32)
    inv_freq = (np.float32(500000.0) ** (-(np.arange(0, 16, 2, dtype=np.float32)) / np.float32(16))).astype(np.float32)
    ang = (pos[None, :] * inv_freq[:, None]).astype(np.float32)
    cs = np.cos(ang).astype(np.float32)
    sn = np.sin(ang).astype(np.float32)
    rc = np.ones((64, NP + NS), np.float32)
    rs = np.zeros((64, NP + NS), np.float32)
    rc[0:8] = cs
    rc[8:16] = cs
    rs[0:8] = -sn
    rs[8:16] = sn
    c['ropec'] = np.concatenate([rc, rc], 0)
    c['ropes'] = np.concatenate([rs, rs], 0)
    return c


_NC_CACHE = {}


def kernel(x_prompt, x_sample, c_prompt, c_sample, cache_k_win, cache_v_win, state_conv_rglru,
           state_h_rglru, state_s_hgrn, norm_pre, norm_post, ada_w, ada_b, ffn1_w_in, ffn1_w_out,
           ffn2_w_in, ffn2_w_out, even_w_in, even_w_out, attn_sinks, rg_conv_w, rg_conv_b, rg_wa,
           rg_ba, rg_wx, rg_bx, rg_lambda, odd_w_in, odd_w_out, hgrn_lb_logits, hgrn_gnorm, _flags=None):
    f = lambda a: np.ascontiguousarray(np.asarray(a, dtype=np.float32))
    flags = dict(_flags or {})
    ncores_run = flags.pop('ncores', NCORES)
    key = tuple(sorted(flags.items()))
    if key not in _NC_CACHE:
        _NC_CACHE[key] = build_nc(**flags)
    nc = _NC_CACHE[key]
    consts = _consts()

    def fm(v, nchunk):
        v = np.asarray(v, np.float32)
        lead = v.shape[:-1]
        v = v.reshape(lead + (nchunk, 128))
        v = np.moveaxis(v, -1, 0)
        return np.ascontiguousarray(v)

    shared = {
        'normpre': fm(np.asarray(norm_pre).reshape(6, D), 8),
        'normpost': fm(np.asarray(norm_post).reshape(6, D), 8),
        'adab': fm(np.asarray(ada_b).reshape(6, 3 * D), 24),
        'ada_w': f(ada_w),
        'ffn1_w_in': f(ffn1_w_in), 'ffn2_w_in': f(ffn2_w_in), 'ffn1_w_out': f(ffn1_w_out), 'ffn2_w_out': f(ffn2_w_out),
        'even_w_in': f(even_w_in)[0], 'even_w_out': f(even_w_out)[0],
        'convw': np.ascontiguousarray(np.moveaxis(np.asarray(rg_conv_w, np.float32)[0].reshape(4, 4, 128), 2, 0).transpose(0, 2, 1)),
        'convb': fm(np.asarray(rg_conv_b)[0], 4),
        'rg_wa': f(rg_wa)[0], 'rg_wx': f(rg_wx)[0],
        'rgba': fm(np.asarray(rg_ba)[0], 4), 'rgbx': fm(np.asarray(rg_bx)[0], 4), 'rglam': fm(np.asarray(rg_lambda)[0], 4),
        'odd_w_in': f(odd_w_in)[0], 'odd_w_out': f(odd_w_out)[0],
        'lbl': fm(np.asarray(hgrn_lb_logits), 8),
        'gnorm': f(np.asarray(hgrn_gnorm)[0].reshape(128, 1)),
    }
    sk = np.asarray(attn_sinks, np.float32)[0]
    sT = np.zeros((128, 4), np.float32)
    sT[0:64, :] = sk[0:4][None, :]
    sT[64:128, :] = sk[4:8][None, :]
    shared['sinkT'] = sT
    shared.update(consts)
    cwv = np.asarray(rg_conv_w, np.float32)[0].reshape(4, 4, 128)
    shared['convw'] = np.ascontiguousarray(cwv.transpose(2, 1, 0))

    xp = np.asarray(x_prompt, np.float32)
    xs = np.asarray(x_sample, np.float32)
    in_maps = []
    for i in range(NCORES):
        m = dict(shared)
        sl = slice(16 * i, 16 * i + 16)
        xt = np.concatenate([xp[i], xs[sl].reshape(64, D)], 0)
        m['xT'] = np.ascontiguousarray(xt.T)
        ct = np.concatenate([np.asarray(c_prompt, np.float32)[i:i + 1], np.asarray(c_sample, np.float32)[sl]], 0)
        m['cT'] = np.ascontiguousarray(ct.T)
        m['cachek'] = f(np.asarray(cache_k_win)[0, sl].reshape(16, 128, 128))
        m['cachev'] = f(np.asarray(cache_v_win)[0, sl].reshape(16, 128, 128))
        cs_ = np.asarray(state_conv_rglru, np.float32)[0, sl]
        m['convS'] = np.ascontiguousarray(cs_.reshape(16, 3, 4, 128).transpose(3, 2, 0, 1))
        hs_ = np.asarray(state_h_rglru, np.float32)[0, sl]
        m['h0S'] = np.ascontiguousarray(hs_.reshape(16, 4, 128).transpose(2, 1, 0))
        m['s0S'] = f(np.asarray(state_s_hgrn)[0, sl])
        in_maps.append(m)
    res = run_bass_kernel_spmd(nc, in_maps[:ncores_run], core_ids=list(range(ncores_run)))
    R = list(res.results)
    while len(R) < NCORES:
        R.append(R[0])
    y_prompt = np.stack([R[i]['yT'][:, :NP].T for i in range(NCORES)], 0)
    y_sample = np.concatenate([R[i]['yT'][:, NP:].T.reshape(16, 4, D) for i in range(NCORES)], 0)
    k_win_p = np.stack([R[i]['kwp'].T.reshape(128, 2, 64) for i in range(NCORES)], 0)[None]
    v_win_p = np.stack([R[i]['vwp'].reshape(128, 2, 64) for i in range(NCORES)], 0)[None]
    conv_p = np.stack([R[i]['convp'].transpose(2, 1, 0).reshape(3, 512) for i in range(NCORES)], 0)[None]
    h_p = np.stack([R[i]['hp'].T.reshape(512) for i in range(NCORES)], 0)[None]
    s_p = np.stack([R[i]['sp_o'] for i in range(NCORES)], 0)[None]
    k_win_s = np.concatenate([R[i]['kws'].reshape(16, 128, 2, 64) for i in range(NCORES)], 0)[None]
    v_win_s = np.concatenate([R[i]['vws'].reshape(16, 128, 2, 64) for i in range(NCORES)], 0)[None]
    conv_s = np.concatenate([R[i]['convs'].transpose(2, 3, 1, 0).reshape(16, 3, 512) for i in range(NCORES)], 0)[None]
    h_s = np.concatenate([R[i]['hs_o'].transpose(2, 1, 0).reshape(16, 512) for i in range(NCORES)], 0)[None]
    s_s = np.concatenate([R[i]['ss_o'] for i in range(NCORES)], 0)[None]
    outs = (y_prompt, y_sample, k_win_p, v_win_p, conv_p, h_p, s_p, k_win_s, v_win_s, conv_s, h_s, s_s)
    return tuple(np.ascontiguousarray(o, dtype=np.float32) for o in outs)
```

```python
import numpy as np
from contextlib import ExitStack
import concourse.bass as bass
import concourse.mybir as mybir
from concourse.bass_utils import run_bass_kernel_spmd

F32 = mybir.dt.float32
BF16 = mybir.dt.bfloat16
AF = mybir.ActivationFunctionType
ALU = mybir.AluOpType

ENGS = ['pe', 'act', 'dve', 'pool', 'sp']
D = 1024
DFF = 2816
NP = 2048
NS = 64
TG = 1088
NCORES = 8
EPS = 1e-6


class Sched:
    EPOCH = 3000

    def __init__(self, nc):
        self.nc = nc
        self.q = {e: [] for e in ENGS}
        self.cnt = {e: 0 for e in ENGS}
        self.sems = {}
        self.waited = {e: {} for e in ENGS}
        self.lastw = {}
        self.readers = {}
        self.dcount = {}
        self.nsem = 0
        self.pool_out = []
        self.pool_sum = 0

    def sem(self, key):
        if key not in self.sems:
            self.sems[key] = self.nc.alloc_semaphore(name="s%d" % self.nsem)
            self.nsem += 1
        return self.sems[key]

    def _filter(self, eng, deps):
        out = {}
        wd = self.waited[eng]
        for key, val in deps:
            if eng == 'pe' and key[0] == 'e' and key[1] == 'pe':
                continue
            if wd.get(key, 0) >= val:
                continue
            if out.get(key, 0) < val:
                out[key] = val
        for k, v in out.items():
            wd[k] = v
        return list(out.items())

    def _deps(self, eng, reads, writes):
        deps = []
        for r in reads:
            if r in self.lastw:
                deps.append(self.lastw[r])
        for w in writes:
            if w in self.lastw:
                deps.append(self.lastw[w])
            deps.extend(self.readers.get(w, ()))
        return self._filter(eng, deps)

    def _book(self, me, reads, writes):
        for r in reads:
            self.readers.setdefault(r, []).append(me)
        for w in writes:
            self.lastw[w] = me
            self.readers[w] = []

    def op(self, eng, fn, reads=(), writes=()):
        waits = self._deps(eng, reads, writes)
        idx = self.cnt[eng]
        self.cnt[eng] += 1
        key = ('e', eng, idx // self.EPOCH)
        self.sem(key)
        val = idx % self.EPOCH + 1
        self.q[eng].append((waits, fn, key, 1))
        self._book((key, val), reads, writes)

    POOL_DESC_LIMIT = 1 << 40

    def dma(self, eng, fn, semname, reads=(), writes=(), ndesc=1024):
        waits = self._deps(eng, reads, writes)
        if eng == 'pool':
            extra = []
            while self.pool_out and self.pool_sum + ndesc > self.POOL_DESC_LIMIT:
                k0, v0, n0 = self.pool_out.pop(0)
                self.pool_sum -= n0
                extra.append((k0, v0))
            if extra:
                waits = waits + self._filter(eng, extra)
        key = ('d', semname)
        self.sem(key)
        self.dcount[key] = self.dcount.get(key, 0) + 16
        self.q[eng].append((waits, fn, key, 16))
        self._book((key, self.dcount[key]), reads, writes)
        if eng == 'pool':
            self.pool_out.append((key, self.dcount[key], ndesc))
            self.pool_sum += ndesc

    def raw(self, eng, fn):
        self.q[eng].append(([], fn, None, 0))

    def _now(self, engs):
        deps = []
        for e in engs:
            if self.cnt[e] > 0:
                idx = self.cnt[e] - 1
                deps.append((('e', e, idx // self.EPOCH), idx % self.EPOCH + 1))
        return deps

    def barrier(self, engs=('pe', 'act', 'dve'), dma_prefix=('o_', 'l_'), wait_engs=None):
        deps = self._now(list(engs) + ['pool'])
        for key, v in self.dcount.items():
            if key[1].startswith(dma_prefix):
                deps.append((key, v))
        if wait_engs is None:
            wait_engs = engs
        for e in list(wait_engs) + ['sp']:
            w = self._filter(e, deps)
            if w:
                self.q[e].append((w, None, None, 0))

    def finish(self, eng='sp'):
        deps = self._now(ENGS)
        for key, v in self.dcount.items():
            deps.append((key, v))
        self.q[eng].append((deps, None, None, 0))

    def emit(self, block):
        sems = self.sems
        qs = self.q

        def run(engobj, lst):
            for waits, fn, key, inc in lst:
                for k, v in waits:
                    engobj.wait_ge(sems[k], v)
                if fn is not None:
                    ins = fn(engobj)
                    if key is not None:
                        ins.then_inc(sems[key], inc)

        @block.tensor
        def _(e):
            run(e, qs['pe'])

        @block.scalar
        def _(e):
            run(e, qs['act'])

        @block.vector
        def _(e):
            run(e, qs['dve'])

        @block.gpsimd
        def _(e):
            run(e, qs['pool'])

        @block.sync
        def _(e):
            run(e, qs['sp'])


class Rot:
    def __init__(self, n):
        self.n = n
        self.i = 0
        self.held = set()

    def __call__(self):
        for _ in range(self.n):
            v = self.i
            self.i = (self.i + 1) % self.n
            if v not in self.held:
                return v
        raise RuntimeError("all slots held")

    def hold(self, *vs):
        self.held.update(vs)

    def release(self, *vs):
        self.held.difference_update(vs)


def build_nc(do_even=True, do_odd=True, do_ffn=True, groups=(0, 1), estop=9, ostop=9):
    nc = bass.Bass("TRN2", target_bir_lowering=False)

    def din(name, shape):
        return nc.dram_tensor(name, list(shape), F32, kind="ExternalInput").ap()

    def dout(name, shape):
        return nc.dram_tensor(name, list(shape), F32, kind="ExternalOutput").ap()

    xT = din("xT", [D, NP + NS])
    cT = din("cT", [D, 17])
    cachek = din("cachek", [16, 128, 128])
    cachev = din("cachev", [16, 128, 128])
    convS = din("convS", [128, 4, 16, 3])
    h0S = din("h0S", [128, 4, 16])
    s0S = din("s0S", [16, 8, 128, 128])
    normpre = din("normpre", [128, 6, 8])
    normpost = din("normpost", [128, 6, 8])
    adab = din("adab", [128, 6, 24])
    ada_w = din("ada_w", [2, 3, D, 3 * D])
    ffn_w_in = [din("ffn1_w_in", [2, D, 2 * DFF]), din("ffn2_w_in", [2, D, 2 * DFF])]
    ffn_w_out = [din("ffn1_w_out", [2, DFF, D]), din("ffn2_w_out", [2, DFF, D])]
    even_w_in = din("even_w_in", [D, 1792])
    even_w_out = din("even_w_out", [D, D])
    sinkT = din("sinkT", [128, 4])
    convw = din("convw", [128, 4, 4])
    convb = din("convb", [128, 4])
    rg_wa = din("rg_wa", [8, 64, 64])
    rg_wx = din("rg_wx", [8, 64, 64])
    rgba = din("rgba", [128, 4])
    rgbx = din("rgbx", [128, 4])
    rglam = din("rglam", [128, 4])
    odd_w_in = din("odd_w_in", [D, 4096])
    odd_w_out = din("odd_w_out", [D, D])
    lbl = din("lbl", [128, 2, 8])
    gnorm = din("gnorm", [128, 1])
    ident_d = din("ident", [128, 128])
    amask_d = din("amask", [128, 256])
    hmask_d = din("hmask", [128, 128])
    smask_d = din("smask", [64, 64])
    mc_d = din("mcmask", [128, 4])
    rmask_d = din("rmask", [128, 512 + 64])
    onehot_d = din("onehot", [64, 16])
    ropec_d = din("ropec", [128, NP + NS])
    ropes_d = din("ropes", [128, NP + NS])

    yT = dout("yT", [D, NP + NS])
    kwp = dout("kwp", [128, 128])
    vwp = dout("vwp", [128, 128])
    convp = dout("convp", [128, 4, 3])
    hp = dout("hp", [128, 4])
    sp_o = dout("sp_o", [8, 128, 128])
    kws = dout("kws", [16, 128, 128])
    vws = dout("vws", [16, 128, 128])
    convs = dout("convs", [128, 4, 16, 3])
    hs_o = dout("hs_o", [128, 4, 16])
    ss_o = dout("ss_o", [16, 8, 128, 128])

    es = ExitStack()

    def sb(name, shape, dt):
        return es.enter_context(nc.sbuf_tensor("s_" + name, list(shape), dt))

    with es:
        s = Sched(nc)
        x = sb("x", [128, 8, TG], F32)
        h = sb("h", [128, 8, TG], BF16)
        ARW = 21120
        arena = sb("arena", [128, ARW], F32)
        NWS = 4
        wst = sb("wst", [128, NWS, 4096], BF16)
        NT32 = 6
        t32 = sb("t32", [128, NT32, 512], F32)
        NT16 = 4
        t16 = sb("t16", [128, NT16, 512], BF16)
        NINV = 2
        inv = sb("inv", [128, NINV, 512], F32)
        Amod = sb("Amod", [128, 6, 8, 17], F32)
        Smod = sb("Smod", [128, 6, 8, 17], F32)
        Gmod = sb("Gmod", [128, 6, 8, 17], F32)
        ones_bf = sb("ones_bf", [128, 128], BF16)
        ident_f = sb("ident_f", [128, 128], F32)
        ident_b = sb("ident_b", [128, 128], BF16)
        epsc = sb("epsc", [128, 1], F32)
        onec = sb("onec", [128, 1], F32)
        amask = sb("amask", [128, 256], BF16)
        hmask = sb("hmask", [128, 128], BF16)
        smask = sb("smask", [64, 64], BF16)
        mcm = sb("mcm", [128, 4], BF16)
        rmask = sb("rmask", [128, 576], F32)
        onehot = sb("onehot", [64, 16], F32)
        npre = sb("npre", [128, 6, 8], F32)
        npost = sb("npost", [128, 6, 8], F32)
        adabs = sb("adabs", [128, 6, 24], F32)
        sinkexp = sb("sinkexp", [128, 4], F32)
        cw = sb("cw", [128, 4, 4], F32)
        cb = sb("cb", [128, 4], F32)
        hba = sb("hba", [128, 4], F32)
        hbx = sb("hbx", [128, 4], F32)
        hcoef = sb("hcoef", [128, 4], F32)
        BDa = sb("BDa", [128, 4, 128], BF16)
        BDx = sb("BDx", [128, 4, 128], BF16)
        hc0 = sb("hc0", [128, 8], F32)
        hc1 = sb("hc1", [128, 8], F32)
        lbv = sb("lbv", [128, 8], F32)
        omlb = sb("omlb", [128, 8], F32)
        gn = sb("gn", [128, 1], F32)
        convc = sb("convc", [128, 4, 3], F32)
        hcar = sb("hcar", [128, 4], F32)
        sm1 = sb("sm1", [128, 2, 8], F32)
        ps = es.enter_context(nc.psum_tensor("ps", [128, 8, 512], F32))

        nb = Rot(8)
        n32 = Rot(NT32)
        n16 = Rot(NT16)
        ninv = Rot(NINV)
        nws = Rot(NWS)

        def mm(out, lhsT, rhs, start, stop, reads, writes, **kw):
            s.op('pe', lambda e: e.matmul(out, lhsT=lhsT, rhs=rhs, start=start, stop=stop, **kw), reads, writes)

        def tr(out, in_, ident, reads, writes):
            s.op('pe', lambda e: e.transpose(out=out, in_=in_, identity=ident), reads, writes)

        SET6 = (AF.Ln, AF.Exp, AF.Square, AF.Copy, AF.Identity)
        actstate = {'cur6': False}

        def act(out, in_, func, reads, writes, bias=None, scale=None, force6=False):
            if func not in SET6:
                actstate['cur6'] = False
            kw = {}
            if bias is not None:
                kw['bias'] = bias
            if scale is not None:
                kw['scale'] = scale
            s.op('act', lambda e: e.activation(out=out, in_=in_, func=func, **kw), reads, writes)

        def tt(out, in0, in1, op, reads, writes, eng='dve'):
            s.op(eng, lambda e: e.tensor_tensor(out=out, in0=in0, in1=in1, op=op), reads, writes)

        def ts(out, in0, s1, s2, op0, op1, reads, writes, eng='dve'):
            s.op(eng, lambda e: e.tensor_scalar(out=out, in0=in0, scalar1=s1, scalar2=s2, op0=op0, op1=op1), reads, writes)

        def stt(out, in0, scalar, in1, op0, op1, reads, writes):
            s.op('dve', lambda e: e.scalar_tensor_tensor(out=out, in0=in0, scalar=scalar, in1=in1, op0=op0, op1=op1), reads, writes)

        def cp(out, in_, reads, writes, eng='dve'):
            if eng == 'act':
                s.op('act', lambda e: e.activation(out=out, in_=in_, func=AF.Copy), reads, writes)
            else:
                s.op(eng, lambda e: e.tensor_copy(out, in_), reads, writes)

        def recip(out, in_, reads, writes, scratch=None):
            if scratch is None:
                s.op('dve', lambda e: e.reciprocal(out, in_), reads, writes)
            else:
                s.op('dve', lambda e: e.reciprocal_approx_accurate(out, in_, scratch), reads, writes)

        def memset(ap, val, writes, eng='dve'):
            s.op(eng, lambda e: e.memset(ap, val), (), writes)

        def scan(out, d0, d1, init, reads, writes):
            s.op('dve', lambda e: e.tensor_tensor_scan(out=out, data0=d0, data1=d1, initial=init, op0=ALU.mult, op1=ALU.add), reads, writes)

        def dma(eng, out, in_, sem, reads=(), writes=()):
            nd = 1
            for d_ in list(out.shape)[:-1]:
                nd *= d_
            s.dma(eng, lambda e: e.dma_start(out=out, in_=in_), sem, reads, writes, ndesc=nd)

        def av(off_words, shape, dt):
            n = 1
            for d_ in shape[1:]:
                n *= d_
            if dt == BF16:
                assert n % 2 == 0
                w = n // 2
                v = arena[:, off_words:off_words + w].bitcast(BF16)
            else:
                w = n
                v = arena[:, off_words:off_words + w]
            assert off_words + w <= ARW, (off_words, w, ARW)
            if len(shape) == 3:
                v = v.rearrange("p (a b) -> p a b", a=shape[1])
            elif len(shape) == 4:
                v = v.rearrange("p (a b c) -> p a b c", a=shape[1], b=shape[2])
            return v[0:shape[0]], off_words + w

        def wslot(i, shape):
            v = wst[:, i, :]
            n = 1
            for d_ in shape[1:]:
                n *= d_
            v = v[:, 0:n]
            if len(shape) == 3:
                v = v.rearrange("p (a b) -> p a b", a=shape[1])
            elif len(shape) == 4:
                v = v.rearrange("p (a b c) -> p a b c", a=shape[1], b=shape[2])
            return v

        K = 'const'
        for (dst, src) in [(ident_f, ident_d), (rmask, rmask_d), (onehot, onehot_d), (npre, normpre), (npost, normpost),
                           (adabs, adab), (cw, convw), (cb, convb), (gn, gnorm)]:
            dma('sp', dst[:], src, 'l_c_%s' % src.name, writes=[K])
        for (dst, src) in [(ident_b, ident_d), (amask, amask_d), (hmask, hmask_d), (smask, smask_d), (mcm, mc_d)]:
            dma('pool', dst[:], src, 'l_cb_%s' % src.name, writes=[K])
        memset(ones_bf[:], 1.0, [K])
        memset(epsc[:], EPS, [K])
        memset(onec[:], 1.0, [K])
        memset(BDa[:], 0.0, ['BD'])
        memset(BDx[:], 0.0, ['BD'])
        for k in range(8):
            c, hf = k // 2, k % 2
            dma('pool', BDa[hf * 64:(hf + 1) * 64, c, hf * 64:(hf + 1) * 64], rg_wa[k], 'l_bd', reads=(), writes=['BD'])
            dma('pool', BDx[hf * 64:(hf + 1) * 64, c, hf * 64:(hf + 1) * 64], rg_wx[k], 'l_bd', reads=(), writes=['BD'])
        dma('sp', sinkexp[:], sinkT, 'l_p1', writes=['p_sink'])
        act(sinkexp[:], sinkexp[:], AF.Exp, ['p_sink'], ['p_sink'])
        dma('sp', hba[:], rgba, 'l_p2', writes=['p_hba'])
        ts(hba[:], hba[:], 0.5, None, ALU.mult, ALU.bypass, ['p_hba'], ['p_hba'])
        dma('sp', hbx[:], rgbx, 'l_p3', writes=['p_hbx'])
        ts(hbx[:], hbx[:], 0.5, None, ALU.mult, ALU.bypass, ['p_hbx'], ['p_hbx'])
        dma('sp', hcoef[:], rglam, 'l_p4', writes=['p_hc'])
        act(hcoef[:], hcoef[:], AF.Exp, ['p_hc'], ['p_hc'], scale=-1.0)
        act(hcoef[:], hcoef[:], AF.Ln, ['p_hc', K], ['p_hc'], bias=onec[:], scale=1.0)
        ts(hcoef[:], hcoef[:], -4.0, None, ALU.mult, ALU.bypass, ['p_hc'], ['p_hc'])
        dma('sp', sm1[:], lbl, 'l_p5', writes=['p_lb'])
        tt(hc0[:], sm1[:, 1, :], sm1[:, 0, :], ALU.subtract, ['p_lb'], ['p_hc0'])
        act(hc0[:], hc0[:], AF.Tanh, ['p_hc0'], ['p_hc0'], scale=0.5)
        ts(hc1[:], hc0[:], -0.25, 0.25, ALU.mult, ALU.add, ['p_hc0'], ['p_hc1'])
        ts(hc0[:], hc0[:], 0.25, 0.75, ALU.mult, ALU.add, ['p_hc0', 'p_hc1'], ['p_hc0'])
        tt(lbv[:], hc0[:], hc1[:], ALU.subtract, ['p_hc0', 'p_hc1'], ['p_lbv'])
        ts(omlb[:], hc1[:], 2.0, None, ALU.mult, ALU.bypass, ['p_hc1'], ['p_lbv'])
        s.barrier()
        PK = [K, 'BD', 'p_sink', 'p_hba', 'p_hbx', 'p_hc', 'p_hc0', 'p_hc1']

        sc, o = av(0, [128, 8, 17], BF16)
        cts, o = av(o, [128, 8, 17], F32)
        modr, o = av(o, [128, 6, 24, 17], F32)
        dma('sp', cts, cT.rearrange("(c p) s -> p c s", p=128), 'l_p6', writes=['cts'])
        act(sc, cts, AF.Silu, ['cts'], ['sc'])
        for sub in range(6):
            l, k3 = sub // 3, sub % 3
            bank = nb()
            for blk in range(6):
                sl = nws()
                wv = wslot(sl, [128, 8, 512])
                dma('pool', wv, ada_w[l, k3, :, blk * 512:(blk + 1) * 512].rearrange("(c p) f -> p c f", p=128),
                    'w%d' % sl, writes=[('ws', sl)])
                for fl in range(4):
                    fc = blk * 4 + fl
                    for kc in range(8):
                        mm(ps[:, bank, fc * 17:(fc + 1) * 17], wv[:, kc, fl * 128:(fl + 1) * 128], sc[:, kc, :],
                           kc == 0, kc == 7, [('ws', sl), 'sc'], [('ps', bank)])
            tt(modr[:, sub, :, :], ps[:, bank, 0:408].rearrange("p (a b) -> p a b", a=24),
               adabs[:, sub, :].unsqueeze(2).to_broadcast([128, 24, 17]), ALU.add, [('ps', bank), K], [('modr', sub)])
            stt(Amod[:, sub], modr[:, sub, 8:16, :], 1.0, npre[:, sub, :].unsqueeze(2).to_broadcast([128, 8, 17]),
                ALU.add, ALU.mult, [('modr', sub), K], ['mods'])
            cp(Smod[:, sub], modr[:, sub, 0:8, :], [('modr', sub)], ['mods'])
            stt(Gmod[:, sub], modr[:, sub, 16:24, :], 1.0, npost[:, sub, :].unsqueeze(2).to_broadcast([128, 8, 17]),
                ALU.add, ALU.mult, [('modr', sub), K], ['mods'])
            if k3 != 1:
                ts(Gmod[:, sub], Gmod[:, sub], 0.5, None, ALU.mult, ALU.bypass, ['mods'], ['mods'])
        s.barrier()

        def rms_inv(src_fn, rkeys, n, scale):
            bank = nb()
            for c in range(8):
                q = n16()
                if True:
                    act(t16[:, q, :n], src_fn(c), AF.Square, rkeys(c), [('t16', q)])
                else:
                    tt(t16[:, q, :n], src_fn(c), src_fn(c), ALU.mult, rkeys(c), [('t16', q)], eng='pool')
                mm(ps[:, bank, :n], ones_bf[:], t16[:, q, :n], c == 0, c == 7, [K, ('t16', q)], [('ps', bank)])
            r = n32()
            iv = ninv()
            act(t32[:, r, :n], ps[:, bank, :n], AF.Ln, [('ps', bank), K], [('t32', r)], bias=epsc[:], scale=scale, force6=True)
            act(inv[:, iv, :n], t32[:, r, :n], AF.Exp, [('t32', r)], [('inv', iv)], scale=-0.5, force6=True)
            return iv

        def expand_mod(src, sub):
            r = n32()
            cp(t32[:, r, :].rearrange("p (c s i) -> p c s i", c=8, s=16),
               src[:, sub, :, 1:17].unsqueeze(3).to_broadcast([128, 8, 16, 4]), ['mods'], [('t32', r)])
            return r

        def prenorm_tile(sub, ti, tile):
            t0, n, kind = tile
            iv = rms_inv(lambda c: x[:, c, t0:t0 + n], lambda c: [('x', ti, c)], n, 1.0 / D)
            if kind == 'p':
                for c in range(8):
                    r = n32()
                    stt(t32[:, r, :n], x[:, c, t0:t0 + n], Amod[:, sub, c, 0:1], inv[:, iv, :n], ALU.mult, ALU.mult,
                        [('x', ti, c), 'mods', ('inv', iv)], [('t32', r)])
                    act(h[:, c, t0:t0 + n], t32[:, r, :n], AF.Identity, [('t32', r), 'mods'], [('h', ti, c)],
                        bias=Smod[:, sub, c, 0:1], scale=1.0)
            else:
                r = n32()
                v = t32[:, r, :].rearrange("p (c t) -> p c t", c=8)
                allx = [('x', ti, c) for c in range(8)]
                tt(v, x[:, :, t0:t0 + n], inv[:, iv, :n].unsqueeze(1).to_broadcast([128, 8, n]), ALU.mult,
                   allx + [('inv', iv)], [('t32', r)])
                ra = expand_mod(Amod, sub)
                tt(v, v, t32[:, ra, :].rearrange("p (c t) -> p c t", c=8), ALU.mult, [('t32', r), ('t32', ra)], [('t32', r)])
                rs = expand_mod(Smod, sub)
                tt(h[:, :, t0:t0 + n], v, t32[:, rs, :].rearrange("p (c t) -> p c t", c=8), ALU.add, [('t32', r), ('t32', rs)],
                   [('h', ti, c) for c in range(8)])

        def postnorm(sub, ti, tile, yb, ykey):
            t0, n, kind = tile
            iv = rms_inv(yb, lambda c: [ykey(c)], n, 1.0 / D)
            if kind == 'p':
                for c in range(8):
                    r = n32()
                    stt(t32[:, r, :n], yb(c), Gmod[:, sub, c, 0:1], inv[:, iv, :n], ALU.mult, ALU.mult,
                        [ykey(c), 'mods', ('inv', iv)], [('t32', r)])
                    tt(x[:, c, t0:t0 + n], x[:, c, t0:t0 + n], t32[:, r, :n], ALU.add, [('x', ti, c), ('t32', r)], [('x', ti, c)], eng='pool')
            else:
                rg = expand_mod(Gmod, sub)
                gv = t32[:, rg, :].rearrange("p (c t) -> p c t", c=8)
                n32.hold(rg)
                for c in range(8):
                    r = n32()
                    tt(t32[:, r, :n], yb(c), inv[:, iv, :n], ALU.mult, [ykey(c), ('inv', iv)], [('t32', r)])
                    tt(t32[:, r, :n], t32[:, r, :n], gv[:, c, :], ALU.mult, [('t32', r), ('t32', rg)], [('t32', r)])
                    tt(x[:, c, t0:t0 + n], x[:, c, t0:t0 + n], t32[:, r, :n], ALU.add, [('x', ti, c), ('t32', r)], [('x', ti, c)], eng='pool')
                n32.release(rg)

        def ffn(l, which, sub, tiles, hook):
            a_, o = av(0, [128, 22, TG], BF16)
            yb_, o = av(o, [128, 8, TG], F32)
            w_in = ffn_w_in[which][l]
            w_out = ffn_w_out[which][l]
            for blk in range(11):
                sl = nws()
                wv = wslot(sl, [128, 2, 8, 256])
                for gu in range(2):
                    c0 = gu * DFF + blk * 256
                    dma('pool', wv[:, gu], w_in[:, c0:c0 + 256].rearrange("(c p) f -> p c f", p=128), 'w%d' % sl,
                        writes=[('ws', sl)])
                for ti, (t0, n, kind) in enumerate(tiles):
                    for jj in range(2):
                        j = blk * 2 + jj
                        bg, bu = nb(), nb()
                        for gu, bk in ((0, bg), (1, bu)):
                            for kc in range(8):
                                mm(ps[:, bk, :n], wv[:, gu, kc, jj * 128:(jj + 1) * 128], h[:, kc, t0:t0 + n], kc == 0, kc == 7,
                                   [('ws', sl), ('h', ti, kc)], [('ps', bk)])
                        r = n32()
                        act(t32[:, r, :n], ps[:, bg, :n], AF.Silu, [('ps', bg)], [('t32', r)])
                        tt(a_[:, j, t0:t0 + n], t32[:, r, :n], ps[:, bu, :n], ALU.mult, [('t32', r), ('ps', bu)], [('a', j, ti)])
            jblocks = [(0, 4), (4, 4), (8, 4), (12, 4), (16, 4), (20, 2)]
            for bi, (j0, nj) in enumerate(jblocks):
                sl = nws()
                wv = wslot(sl, [128, 4, 1024])
                dma('pool', wv[:, 0:nj, :], w_out[j0 * 128:(j0 + nj) * 128, :].rearrange("(j p) f -> p j f", p=128), 'w%d' % sl,
                    writes=[('ws', sl)])
                for ti, (t0, n, kind) in enumerate(tiles):
                    for dc in range(8):
                        bank = nb()
                        for jl in range(nj):
                            mm(ps[:, bank, :n], wv[:, jl, dc * 128:(dc + 1) * 128], a_[:, j0 + jl, t0:t0 + n], jl == 0, jl == nj - 1,
                               [('ws', sl), ('a', j0 + jl, ti)], [('ps', bank)])
                        if bi == 0:
                            cp(yb_[:, dc, t0:t0 + n], ps[:, bank, :n], [('ps', bank)], [('yb', ti, dc)], eng='act')
                        else:
                            tt(yb_[:, dc, t0:t0 + n], yb_[:, dc, t0:t0 + n], ps[:, bank, :n], ALU.add, [('yb', ti, dc), ('ps', bank)], [('yb', ti, dc)])
            for ti, tile in enumerate(tiles):
                t0, n, kind = tile
                postnorm(sub, ti, tile, lambda c: yb_[:, c, t0:t0 + n], lambda c: ('yb', ti, c))
                hook(ti, tile)

        def outproj(sub, tiles, w_dram_chunk, src_fn, src_keys, ybt, hook):
            sls = []
            for half in range(2):
                sl = nws()
                wv = wslot(sl, [128, 8, 512])
                for (dst, ap) in w_dram_chunk(wv, half):
                    dma('pool', dst, ap, 'w%d' % sl, writes=[('ws', sl)])
                sls.append((sl, wv))
            for ti, tile in enumerate(tiles):
                t0, n, kind = tile
                for half in range(2):
                    sl, wv = sls[half]
                    for dcl in range(4):
                        dc = half * 4 + dcl
                        bank = nb()
                        for kc in range(8):
                            mm(ps[:, bank, :n], wv[:, kc, dcl * 128:(dcl + 1) * 128], src_fn(kc, t0, n), kc == 0, kc == 7,
                               [('ws', sl)] + src_keys(kc, ti), [('ps', bank)])
                        cp(ybt[:, dc, :n], ps[:, bank, :n], [('ps', bank)], [('ybt', dc)], eng='act')
                postnorm(sub, ti, tile, lambda c: ybt[:, c, :n], lambda c: ('ybt', c))
                hook(ti, tile)

        def even_mixer(sub, g, tiles, hook):
            has_s = (g == 0)
            o = 0
            oa, o = av(o, [128, 4, TG], BF16)
            ob, o = av(o, [128, 4, TG], BF16)
            oP = o
            qr, o = av(oP, [128, 4, TG], BF16)
            kr, o = av(o, [128, 1216], BF16)
            krf, o = av(o, [128, 192], F32)
            vtok, o = av(o, [128, 10, 128], BF16)
            vwf, o = av(o, [128, 2, 128], F32)
            cosT, o = av(o, [128, TG], F32)
            sinT, o = av(o, [128, TG], F32)
            kcache, o = av(o, [128, 2, 128], F32)
            vcache, o = av(o, [128, 2, 128], F32)
            KT, o = av(o, [128, 2, 128], BF16)
            Vb, o = av(o, [128, 16, 128], BF16)
            ktr, o = av(o, [64, 128], F32)
            gl, o = av(oP, [128, 4, TG], BF16)
            xr, o = av(o, [128, 4, 3 + 1024], F32)
            xrs, o = av(o, [128, 4, 16, 7], F32)
            xc, o = av(o, [128, TG], F32)
            xcb, o = av(o, [128, TG], BF16)
            ac, o = av(o, [128, TG], F32)
            bc, o = av(o, [128, TG], F32)
            hsb, o = av(o, [128, TG], F32)
            h0s, o = av(o, [128, 4, 16], F32)
            hsS, o = av(o, [128, 4, 16], F32)
            ybt, _ = av(oP, [128, 8, 512], F32)

            dma('sp', cosT[:, 0:1024], ropec_d[:, g * 1024:(g + 1) * 1024], 'l_rope', writes=['rope'])
            dma('sp', sinT[:, 0:1024], ropes_d[:, g * 1024:(g + 1) * 1024], 'l_rope', writes=['rope'])
            if has_s:
                dma('sp', cosT[:, 1024:1088], ropec_d[:, NP:NP + NS], 'l_rope', writes=['rope'])
                dma('sp', sinT[:, 1024:1088], ropes_d[:, NP:NP + NS], 'l_rope', writes=['rope'])
            W = even_w_in
            Wv_ = W.rearrange("(c p) f -> p c f", p=128)
            slq, slqs, slk, slst = nws(), nws(), nws(), nws()
            wq = wslot(slq, [128, 8, 512])
            wqs = wslot(slqs, [128, 8, 512])
            wk = wslot(slk, [128, 8, 384])
            wstg = wslot(slst, [128, 8, 512])
            dma('pool', wstg, Wv_[:, :, 0:512], 'w%d' % slst, writes=[('ws', slst)])
            dma('pool', wk[:, :, 0:128], Wv_[:, :, 512:640], 'w%d' % slk, writes=[('ws', slk)])
            dma('pool', wk[:, :, 256:384], Wv_[:, :, 640:768], 'w%d' % slk, writes=[('ws', slk)])
            src5 = wstg.rearrange("p c (hf j d) -> p c hf j d", hf=2, j=4)
            dq5 = wq.rearrange("p c (j hf d) -> p c j hf d", j=4, hf=2)
            dqs5 = wqs.rearrange("p c (j hf d) -> p c j hf d", j=4, hf=2)
            for hf in range(2):
                cp(dq5[:, :, :, hf, :], src5[:, :, hf, :, :], [('ws', slst)], [('ws', slq)], eng='pool')
                cp(dqs5[:, :, :, hf, 0:8], src5[:, :, hf, :, 8:16], [('ws', slst)], [('ws', slqs)], eng='pool')
                cp(dqs5[:, :, :, hf, 8:16], src5[:, :, hf, :, 0:8], [('ws', slst)], [('ws', slqs)], eng='pool')
                cp(dqs5[:, :, :, hf, 16:64], src5[:, :, hf, :, 16:64], [('ws', slst)], [('ws', slqs)], eng='pool')
            ks4 = wk[:, :, 0:128].rearrange("p c (kv d) -> p c kv d", kv=2)
            kd4 = wk[:, :, 128:256].rearrange("p c (kv d) -> p c kv d", kv=2)
            cp(kd4[:, :, :, 0:8], ks4[:, :, :, 8:16], [('ws', slk)], [('ws', slk)], eng='pool')
            cp(kd4[:, :, :, 8:16], ks4[:, :, :, 0:8], [('ws', slk)], [('ws', slk)], eng='pool')
            cp(kd4[:, :, :, 16:64], ks4[:, :, :, 16:64], [('ws', slk)], [('ws', slk)], eng='pool')

            def kcol(t0):
                return 128 + t0
            for ti, (t0, n, kind) in enumerate(tiles):
                hk = [('h', ti, kc) for kc in range(8)]
                for j in range(4):
                    b1, b2 = nb(), nb()
                    for kc in range(8):
                        mm(ps[:, b1, :n], wq[:, kc, j * 128:(j + 1) * 128], h[:, kc, t0:t0 + n], kc == 0, kc == 7,
                           [('ws', slq), ('h', ti, kc)], [('ps', b1)])
                    for kc in range(8):
                        mm(ps[:, b2, :n], wqs[:, kc, j * 128:(j + 1) * 128], h[:, kc, t0:t0 + n], kc == 0, kc == 7,
                           [('ws', slqs), ('h', ti, kc)], [('ps', b2)])
                    r1, r2 = n32(), n32()
                    tt(t32[:, r1, :n], ps[:, b1, :n], cosT[:, t0:t0 + n], ALU.mult, [('ps', b1), 'rope'], [('t32', r1)])
                    tt(t32[:, r2, :n], ps[:, b2, :n], sinT[:, t0:t0 + n], ALU.mult, [('ps', b2), 'rope'], [('t32', r2)])
                    tt(qr[:, j, t0:t0 + n], t32[:, r1, :n], t32[:, r2, :n], ALU.add, [('t32', r1), ('t32', r2)], [('qr', ti, j)])
                b1, b2 = nb(), nb()
                for kc in range(8):
                    mm(ps[:, b1, :n], wk[:, kc, 0:128], h[:, kc, t0:t0 + n], kc == 0, kc == 7, [('ws', slk), ('h', ti, kc)], [('ps', b1)])
                for kc in range(8):
                    mm(ps[:, b2, :n], wk[:, kc, 128:256], h[:, kc, t0:t0 + n], kc == 0, kc == 7, [('ws', slk), ('h', ti, kc)], [('ps', b2)])
                r1, r2 = n32(), n32()
                tt(t32[:, r1, :n], ps[:, b1, :n], cosT[:, t0:t0 + n], ALU.mult, [('ps', b1), 'rope'], [('t32', r1)])
                tt(t32[:, r2, :n], ps[:, b2, :n], sinT[:, t0:t0 + n], ALU.mult, [('ps', b2), 'rope'], [('t32', r2)])
                tt(kr[:, kcol(t0):kcol(t0) + n], t32[:, r1, :n], t32[:, r2, :n], ALU.add, [('t32', r1), ('t32', r2)], [('kr', ti)])
                if kind == 's':
                    tt(krf[:, 128:192], t32[:, r1, :n], t32[:, r2, :n], ALU.add, [('t32', r1), ('t32', r2)], ['krf_s'])
                elif g == 1 and ti == 1:
                    tt(krf[:, 0:128], t32[:, r1, 384:512], t32[:, r2, 384:512], ALU.add, [('t32', r1), ('t32', r2)], ['krf_p'])
                nblk = (n + 127) // 128
                for bl in range(nblk):
                    nt = min(128, n - bl * 128)
                    blk = (t0 // 128 + bl) if kind == 'p' else 8
                    bank = nb()
                    for kc in range(8):
                        mm(ps[:nt, bank, 0:128], h[:, kc, t0 + bl * 128:t0 + bl * 128 + nt], wk[:, kc, 256:384], kc == 0, kc == 7,
                           [('ws', slk), ('h', ti, kc)], [('ps', bank)])
                    cp(vtok[:nt, 1 + blk, :], ps[:nt, bank, 0:128], [('ps', bank)], [('vtok', 1 + blk)], eng='act')
                    if kind == 's':
                        cp(vwf[:nt, 1, :], ps[:nt, bank, 0:128], [('ps', bank)], ['vwf_s'], eng='act')
                    elif g == 1 and blk == 7:
                        cp(vwf[:, 0, :], ps[:, bank, 0:128], [('ps', bank)], ['vwf_p'], eng='act')
            if estop <= 1:
                return
            anorm = (int(estop * 100 + 0.5) % 10) != 5 and estop >= 2
            alvl = (int(estop * 100 + 0.5) % 10) if estop < 2 else 9
            if g == 1:
                cp(kr[:, 0:128], kcar[:], ['kcar'], [('krc',)])
                cp(vtok[:, 0, :], vcar[:], ['vcar'], [('vtok', 0)])
            def tile_of(col):
                return col // 512
            for b in range(8 if estop >= 2 else int((estop - 1) * 10 + 0.5)):
                has_prev = not (g == 0 and b == 0)
                tq = tile_of(b * 128)
                kkeys = [('kr', tile_of(b * 128))]
                if has_prev:
                    kkeys.append(('kr', tile_of((b - 1) * 128)) if b > 0 else ('krc',))
                bo, bd = nb(), nb()
                nb.hold(bo, bd)
                for jp in range(2):
                    banks = [nb(), nb()]
                    for hf in range(2):
                        p0 = hf * 64
                        for jl in range(2):
                            j = jp * 2 + jl
                            qa = qr[p0:p0 + 64, j, b * 128:(b + 1) * 128]
                            mm(ps[:, banks[hf], (jl * 2) * 128:(jl * 2 + 1) * 128], kr[p0:p0 + 64, 128 * (1 + b):128 * (2 + b)], qa, True, True,
                               kkeys + [('qr', tq, j)], [('ps', banks[hf])])
                            if has_prev:
                                mm(ps[:, banks[hf], (jl * 2 + 1) * 128:(jl * 2 + 2) * 128], kr[p0:p0 + 64, 128 * b:128 * (1 + b)], qa, True, True,
                                   kkeys + [('qr', tq, j)], [('ps', banks[hf])])
                    for hf in range(2):
                        p0 = hf * 64
                        bank = banks[hf]
                        q = n16()
                        if alvl < 1:
                            continue
                        if has_prev:
                            act(t16[:, q, :], ps[:, bank, :], AF.Exp, [('ps', bank)], [('t16', q)], scale=0.125)
                            ev = t16[:, q, :].rearrange("p (a b) -> p a b", a=2)
                            tt(ev, ev, amask[:].unsqueeze(1).to_broadcast([128, 2, 256]), ALU.mult, [('t16', q), K], [('t16', q)])
                        else:
                            ev4 = t16[:, q, :].rearrange("p (a b c) -> p a b c", a=2, b=2)
                            pv4 = ps[:, bank, :].rearrange("p (a b c) -> p a b c", a=2, b=2)
                            act(ev4[:, :, 0, :], pv4[:, :, 0, :], AF.Exp, [('ps', bank)], [('t16', q)], scale=0.125)
                            tt(ev4[:, :, 0, :], ev4[:, :, 0, :], amask[:, 0:128].unsqueeze(1).to_broadcast([128, 2, 128]), ALU.mult,
                               [('t16', q), K], [('t16', q)])
                        for jl in range(2 if alvl >= 2 else 0):
                            j = jp * 2 + jl
                            ed = t16[:, q, (jl * 2) * 128:(jl * 2 + 1) * 128]
                            ep = t16[:, q, (jl * 2 + 1) * 128:(jl * 2 + 2) * 128]
                            mm(ps[p0:p0 + 64, bo, j * 128:(j + 1) * 128], vtok[:, 1 + b, p0:p0 + 64], ed, True, not has_prev,
                               [('vtok', 1 + b), ('t16', q)], [('ps', bo)])
                            if has_prev:
                                mm(ps[p0:p0 + 64, bo, j * 128:(j + 1) * 128], vtok[:, b, p0:p0 + 64], ep, False, True,
                                   [('vtok', b), ('t16', q)], [('ps', bo)])
                            if alvl < 3:
                                continue
                            mm(ps[p0:p0 + 64, bd, j * 128:(j + 1) * 128], ones_bf[:, 0:64], ed, True, not has_prev, [K, ('t16', q)], [('ps', bd)])
                            if has_prev:
                                mm(ps[p0:p0 + 64, bd, j * 128:(j + 1) * 128], ones_bf[:, 0:64], ep, False, True, [K, ('t16', q)], [('ps', bd)])
                nb.release(bo, bd)
                if not anorm:
                    continue
                r = n32()
                rv = t32[:, r, :].rearrange("p (a b) -> p a b", a=4)
                tt(rv, ps[:, bd, :].rearrange("p (a b) -> p a b", a=4), sinkexp[:].unsqueeze(2).to_broadcast([128, 4, 128]), ALU.add,
                   [('ps', bd), 'p_sink'], [('t32', r)])
                r2 = n32()
                recip(t32[:, r2, :], t32[:, r, :], [('t32', r)], [('t32', r2)])
                tt(oa[:, :, b * 128:(b + 1) * 128], ps[:, bo, :].rearrange("p (a b) -> p a b", a=4),
                   t32[:, r2, :].rearrange("p (a b) -> p a b", a=4), ALU.mult, [('ps', bo), ('t32', r2)], [('oa', b // 4)])
            if g == 0:
                cp(kcar[:], kr[:, 128 * 8:128 * 9], [('kr', 1)], ['kcar'])
                cp(vcar[:], vtok[:, 8, :], [('vtok', 8)], ['vcar'])
            else:
                dma('sp', kwp, krf[:, 0:128], 'o_kwp', reads=['krf_p'])
                dma('sp', vwp, vwf[:, 0, :], 'o_vwp', reads=['vwf_p'])
            if estop <= 2:
                return
            if has_s:
                TS = 1024
                dma('sp', kws[:, 0:124, :], cachek[:, 4:128, :], 'o_kws')
                dma('sp', vws[:, 0:124, :], cachev[:, 4:128, :], 'o_vws')
                bsc = [nb(), nb()]
                nb.hold(*bsc)
                for sq in range(16):
                    rr = sq % 2
                    dma('sp', kcache[:, rr, :], cachek[sq], 'l_kc%d' % rr, writes=[('kcache', rr)])
                    dma('sp', vcache[:, rr, :], cachev[sq], 'l_vc%d' % rr, writes=[('vcache', rr)])
                    bt = nb()
                    tr(ps[:, bt, 0:128], kcache[:, rr, :], ident_f[:], [('kcache', rr), K], [('ps', bt)])
                    cp(KT[:, rr, :], ps[:, bt, 0:128], [('ps', bt)], [('KT', rr)], eng='act')
                    cp(Vb[:, sq, :], vcache[:, rr, :], [('vcache', rr)], [('Vb', sq)], eng='dve')
                    for hf in range(2):
                        p0 = hf * 64
                        for j in range(4):
                            c0 = (sq * 4 + j) * 4
                            mm(ps[:, bsc[hf], c0:c0 + 4], KT[p0:p0 + 64, rr, :], qr[p0:p0 + 64, j, TS + sq * 4:TS + sq * 4 + 4], True, True,
                               [('KT', rr), ('qr', 2, j)], [('ps', bsc[hf])])
                qc = [n16(), n16()]
                for hf in range(2):
                    act(t16[:, qc[hf], 0:256], ps[:, bsc[hf], 0:256], AF.Exp, [('ps', bsc[hf])], [('t16', qc[hf])], scale=0.125)
                    ecv = t16[:, qc[hf], 0:256].rearrange("p (a b) -> p a b", b=4)
                    tt(ecv, ecv, mcm[:].unsqueeze(1).to_broadcast([128, 64, 4]), ALU.mult, [('t16', qc[hf]), K], [('t16', qc[hf])])
                nb.release(*bsc)
                bn = [nb(), nb()]
                for hf in range(2):
                    p0 = hf * 64
                    for j in range(4):
                        mm(ps[0:64, bn[hf], j * 64:(j + 1) * 64], kr[p0:p0 + 64, 128 + TS:128 + TS + 64], qr[p0:p0 + 64, j, TS:TS + 64], True, True,
                           [('kr', 2), ('qr', 2, j)], [('ps', bn[hf])])
                qn = [n16(), n16()]
                for hf in range(2):
                    act(t16[0:64, qn[hf], 0:256], ps[0:64, bn[hf], 0:256], AF.Exp, [('ps', bn[hf])], [('t16', qn[hf])], scale=0.125)
                    env = t16[0:64, qn[hf], 0:256].rearrange("p (a b) -> p a b", a=4)
                    tt(env, env, smask[:].unsqueeze(1).to_broadcast([64, 4, 64]), ALU.mult, [('t16', qn[hf]), K], [('t16', qn[hf])])
                bo, bd = nb(), nb()
                for j in range(4):
                    for hf in range(2):
                        p0 = hf * 64
                        en_ = t16[0:64, qn[hf], j * 64:(j + 1) * 64]
                        mm(ps[p0:p0 + 64, bo, j * 64:(j + 1) * 64], vtok[0:64, 9, p0:p0 + 64], en_, True, False,
                           [('vtok', 9), ('t16', qn[hf])], [('ps', bo)], skip_group_check=True)
                        mm(ps[p0:p0 + 64, bd, j * 64:(j + 1) * 64], ones_bf[0:64, 0:64], en_, True, False, [K, ('t16', qn[hf])], [('ps', bd)],
                           skip_group_check=True)
                        for sq in range(16):
                            c1 = (sq * 4 + j) * 4
                            ec_ = t16[:, qc[hf], c1:c1 + 4]
                            mm(ps[p0:p0 + 64, bo, j * 64 + sq * 4:j * 64 + sq * 4 + 4], Vb[:, sq, p0:p0 + 64], ec_, False, True,
                               [('Vb', sq), ('t16', qc[hf])], [('ps', bo)], skip_group_check=True)
                            mm(ps[p0:p0 + 64, bd, j * 64 + sq * 4:j * 64 + sq * 4 + 4], ones_bf[:, 0:64], ec_, False, True,
                               [K, ('t16', qc[hf])], [('ps', bd)], skip_group_check=True)
                r = n32()
                rv = t32[:, r, 0:256].rearrange("p (a b) -> p a b", a=4)
                tt(rv, ps[:, bd, 0:256].rearrange("p (a b) -> p a b", a=4), sinkexp[:].unsqueeze(2).to_broadcast([128, 4, 64]), ALU.add,
                   [('ps', bd), 'p_sink'], [('t32', r)])
                r2 = n32()
                recip(t32[:, r2, 0:256], t32[:, r, 0:256], [('t32', r)], [('t32', r2)])
                tt(oa[:, :, TS:TS + 64], ps[:, bo, 0:256].rearrange("p (a b) -> p a b", a=4),
                   t32[:, r2, 0:256].rearrange("p (a b) -> p a b", a=4), ALU.mult, [('ps', bo), ('t32', r2)], [('oa', 2)])
                bt = nb()
                tr(ps[0:64, bt, 0:128], krf[:, 128:192], ident_f[:], ['krf_s', K], [('ps', bt)])
                cp(ktr[:], ps[0:64, bt, 0:128], [('ps', bt)], ['ktr'], eng='act')
                for sq in range(16):
                    dma('sp', kws[sq, 124:128, :], ktr[sq * 4:sq * 4 + 4, :], 'o_kws2', reads=['ktr'])
                    dma('sp', vws[sq, 124:128, :], vwf[sq * 4:sq * 4 + 4, 1, :], 'o_vws2', reads=['vwf_s'])

            if estop <= 3:
                return
            s.barrier()
            slg, slr = nws(), nws()
            wg = wslot(slg, [128, 8, 512])
            wr = wslot(slr, [128, 8, 512])
            dma('pool', wg, Wv_[:, :, 768:1280], 'w%d' % slg, writes=[('ws', slg)])
            dma('pool', wr, Wv_[:, :, 1280:1792], 'w%d' % slr, writes=[('ws', slr)])
            if g == 0:
                memset(xr[:, :, 0:3], 0.0, ['xr_c'])
                dma('sp', xrs[:, :, :, 0:3], convS, 'l_cs1', writes=['xrs_c'])
                dma('sp', h0s, h0S, 'l_cs2', writes=['h0s'])
            else:
                cp(xr[:, :, 0:3], convc[:], ['convc'], ['xr_c'])
            for ti, (t0, n, kind) in enumerate(tiles):
                for j in range(4):
                    bank = nb()
                    for kc in range(8):
                        mm(ps[:, bank, :n], wg[:, kc, j * 128:(j + 1) * 128], h[:, kc, t0:t0 + n], kc == 0, kc == 7,
                           [('ws', slg), ('h', ti, kc)], [('ps', bank)])
                    r1, r2 = n32(), n32()
                    act(t32[:, r1, :n], ps[:, bank, :n], AF.Square, [('ps', bank)], [('t32', r1)])
                    ts(t32[:, r1, :n], t32[:, r1, :n], 0.044715, 1.0, ALU.mult, ALU.add, [('t32', r1)], [('t32', r1)])
                    tt(t32[:, r2, :n], t32[:, r1, :n], ps[:, bank, :n], ALU.mult, [('t32', r1), ('ps', bank)], [('t32', r2)])
                    act(t32[:, r1, :n], t32[:, r2, :n], AF.Tanh, [('t32', r2)], [('t32', r1)], scale=0.7978845608028654)
                    stt(gl[:, j, t0:t0 + n], t32[:, r1, :n], 1.0, ps[:, bank, :n], ALU.add, ALU.mult, [('t32', r1), ('ps', bank)], [('gl', ti, j)])
                    bank = nb()
                    for kc in range(8):
                        mm(ps[:, bank, :n], wr[:, kc, j * 128:(j + 1) * 128], h[:, kc, t0:t0 + n], kc == 0, kc == 7,
                           [('ws', slr), ('h', ti, kc)], [('ps', bank)])
                    if kind == 'p':
                        cp(xr[:, j, 3 + t0:3 + t0 + n], ps[:, bank, :n], [('ps', bank)], [('xr', j, ti)], eng='act')
                    else:
                        cp(xrs[:, j, :, 3:7], ps[:, bank, 0:64].rearrange("p (a b) -> p a b", b=4), [('ps', bank)], [('xr', j, ti)], eng='act')
            nt_ = len(tiles)
            for c in range(4):
                xk = [('xr', c, ti) for ti in range(nt_)] + ['xr_c', 'xrs_c']
                ts(xc[:, 0:1024], xr[:, c, 3:1027], cw[:, c, 3:4], cb[:, c:c + 1], ALU.mult, ALU.add, xk + [K], ['xc'])
                for jj in range(3):
                    stt(xc[:, 0:1024], xr[:, c, jj:jj + 1024], cw[:, c, jj:jj + 1], xc[:, 0:1024], ALU.mult, ALU.add, xk + [K, 'xc'], ['xc'])
                if has_s:
                    xcs = xc[:, 1024:1088].rearrange("p (a b) -> p a b", b=4)
                    ts(xcs, xrs[:, c, :, 3:7], cw[:, c, 3:4], cb[:, c:c + 1], ALU.mult, ALU.add, xk + [K], ['xcs'])
                    for jj in range(3):
                        stt(xcs, xrs[:, c, :, jj:jj + 4], cw[:, c, jj:jj + 1], xcs, ALU.mult, ALU.add, xk + [K, 'xcs'], ['xcs'])
                ntot = 1088 if has_s else 1024
                cp(xcb[:, 0:ntot], xc[:, 0:ntot], ['xc', 'xcs'], ['xcb'], eng='act')
                for ti, (t0, n, kind) in enumerate(tiles):
                    b1, b2 = nb(), nb()
                    mm(ps[:, b1, :n], BDa[:, c, :], xcb[:, t0:t0 + n], True, True, ['BD', 'xcb'], [('ps', b1)])
                    mm(ps[:, b2, :n], BDx[:, c, :], xcb[:, t0:t0 + n], True, True, ['BD', 'xcb'], [('ps', b2)])
                    r1, r2, r3 = n32(), n32(), n32()
                    act(t32[:, r1, :n], ps[:, b1, :n], AF.Tanh, [('ps', b1), 'p_hba'], [('t32', r1)], bias=hba[:, c:c + 1], scale=0.5)
                    act(ac[:, t0:t0 + n], t32[:, r1, :n], AF.Exp, [('t32', r1), 'p_hc'], [('ac', ti)], bias=hcoef[:, c:c + 1], scale=hcoef[:, c:c + 1])
                    act(t32[:, r2, :n], ps[:, b2, :n], AF.Tanh, [('ps', b2), 'p_hbx'], [('t32', r2)], bias=hbx[:, c:c + 1], scale=0.5)
                    tt(t32[:, r1, :n], ac[:, t0:t0 + n], ac[:, t0:t0 + n], ALU.mult, [('ac', ti), ('t32', r1)], [('t32', r1)])
                    ts(t32[:, r1, :n], t32[:, r1, :n], -1.0, 1.0, ALU.mult, ALU.add, [('t32', r1)], [('t32', r1)])
                    ts(t32[:, r1, :n], t32[:, r1, :n], 0.0, None, ALU.max, ALU.bypass, [('t32', r1)], [('t32', r1)])
                    act(t32[:, r3, :n], t32[:, r1, :n], AF.Sqrt, [('t32', r1)], [('t32', r3)])
                    stt(t32[:, r2, :n], t32[:, r2, :n], 1.0, xc[:, t0:t0 + n], ALU.add, ALU.mult, [('t32', r2), 'xc', 'xcs'], [('t32', r2)])
                    stt(bc[:, t0:t0 + n], t32[:, r2, :n], 0.5, t32[:, r3, :n], ALU.mult, ALU.mult, [('t32', r2), ('t32', r3)], [('bc', ti)])
                allab = [('ac', ti) for ti in range(nt_)] + [('bc', ti) for ti in range(nt_)]
                if has_s:
                    a0 = ac[:, 1024:1088].rearrange("p (a b) -> p a b", b=4)[:, :, 0]
                    b0 = bc[:, 1024:1088].rearrange("p (a b) -> p a b", b=4)[:, :, 0]
                    r = n32()
                    tt(t32[:, r, 0:16], a0, h0s[:, c, :], ALU.mult, allab + ['h0s'], [('t32', r)])
                    tt(b0, b0, t32[:, r, 0:16], ALU.add, allab + [('t32', r)], [('bc', 2)])
                    memset(a0, 0.0, [('ac', 2)])
                init = 0.0 if g == 0 else hcar[:, c:c + 1]
                scan(hsb[:, 0:1024], ac[:, 0:1024], bc[:, 0:1024], init, allab + ['hcar'], ['hsb'])
                if has_s:
                    scan(hsb[:, 1024:1088], ac[:, 1024:1088], bc[:, 1024:1088], 0.0, allab, ['hsbs'])
                stt(ob[:, c, 0:ntot], gl[:, c, 0:ntot], 0.5, hsb[:, 0:ntot], ALU.mult, ALU.mult,
                    [('gl', ti, c) for ti in range(nt_)] + ['hsb', 'hsbs'], [('ob', c)])
                cp(hcar[:, c:c + 1], hsb[:, 1023:1024], ['hsb'], ['hcar'])
                if has_s:
                    cp(hsS[:, c, :], hsb[:, 1024:1088].rearrange("p (a b) -> p a b", b=4)[:, :, 3], ['hsbs'], ['hsS'])
            cp(convc[:], xr[:, :, 1024:1027], [('xr', c, 1) for c in range(4)], ['convc'])
            if g == 1:
                dma('sp', hp, hcar[:], 'o_hp', reads=['hcar'])
                dma('sp', convp, convc[:], 'o_cp', reads=['convc'])
            if has_s:
                dma('sp', hs_o, hsS, 'o_hs', reads=['hsS'])
                dma('sp', convs, xrs[:, :, :, 4:7], 'o_cs', reads=[('xr', c, 2) for c in range(4)])
            if estop <= 4:
                return
            s.barrier()
            Wo = even_w_out

            def wchunk(wv, half):
                cs = slice(half * 512, (half + 1) * 512)
                out = []
                for hf in range(2):
                    out.append((wv[hf * 64:(hf + 1) * 64, 0:4, :],
                                Wo[hf * 256:(hf + 1) * 256, cs].rearrange("(j d) f -> d j f", j=4)))
                out.append((wv[:, 4:8, :], Wo[512:1024, cs].rearrange("(c p) f -> p c f", p=128)))
                return out

            def src(kc, t0, n):
                return oa[:, kc, t0:t0 + n] if kc < 4 else ob[:, kc - 4, t0:t0 + n]

            def srck(kc, ti):
                return [('oa', ti)] if kc < 4 else [('ob', kc - 4)]
            outproj(sub, tiles, wchunk, src, srck, ybt, hook)

        def odd_mixer(sub, g, tiles, hook):
            has_s = (g == 0)
            o = 0
            qe, o = av(o, [128, 8, TG], BF16)
            ybt, _ = av(0, [128, 8, 512], F32)
            ke, o = av(o, [128, 8, TG], BF16)
            sg, o = av(o, [128, 8, TG], BF16)
            vtk, o = av(o, [128, 9, 1024], BF16)
            keT, o = av(o, [128, 2, 1024], BF16)
            Sbf, o = av(o, [128, 8, 128], BF16)
            tmpS, o = av(o, [128, 8, 128], F32)
            ebl, o = av(o, [128, 8, 32], F32)
            wsc, o = av(o, [128, 8, 32], F32)
            ebr, o = av(o, [128, 8, 32], F32)
            ebls, o = av(o, [128, 8, 16], F32)
            zT = h
            W = odd_w_in
            Wv_ = W.rearrange("(c p) f -> p c f", p=128)
            for hd in range(8):
                sl = nws()
                wv = wslot(sl, [128, 2, 8, 128])
                dma('pool', wv[:, 0], Wv_[:, :, hd * 128:(hd + 1) * 128], 'w%d' % sl, writes=[('ws', sl)])
                dma('pool', wv[:, 1], Wv_[:, :, 1024 + hd * 128:1024 + (hd + 1) * 128], 'w%d' % sl, writes=[('ws', sl)])
                for ti, (t0, n, kind) in enumerate(tiles):
                    bq, bf = nb(), nb()
                    for kc in range(8):
                        mm(ps[:, bq, :n], wv[:, 0, kc, :], h[:, kc, t0:t0 + n], kc == 0, kc == 7, [('ws', sl), ('h', ti, kc)], [('ps', bq)])
                    for kc in range(8):
                        mm(ps[:, bf, :n], wv[:, 1, kc, :], h[:, kc, t0:t0 + n], kc == 0, kc == 7, [('ws', sl), ('h', ti, kc)], [('ps', bf)])
                    rA, rB, rC, rD = n32(), n32(), n32(), n32()
                    A_, B_, C_, D_ = t32[:, rA, :n], t32[:, rB, :n], t32[:, rC, :n], t32[:, rD, :n]
                    act(A_, ps[:, bf, :n], AF.Exp, [('ps', bf)], [('t32', rA)], scale=-1.0)
                    act(C_, A_, AF.Ln, [('t32', rA), K], [('t32', rC)], bias=onec[:], scale=1.0)
                    act(D_, A_, AF.Ln, [('t32', rA), K, 'p_lbv'], [('t32', rD)], bias=onec[:], scale=lbv[:, hd:hd + 1])
                    act(B_, C_, AF.Exp, [('t32', rC)], [('t32', rB)], scale=-1.0)
                    stt(B_, A_, omlb[:, hd:hd + 1], B_, ALU.mult, ALU.mult, [('t32', rA), ('t32', rB), 'p_lbv'], [('t32', rB)])
                    tt(C_, D_, C_, ALU.subtract, [('t32', rD), ('t32', rC)], [('t32', rC)])
                    if kind == 'p':
                        scan(A_, rmask[:, 0:n], C_, 0.0, [K, ('t32', rC), ('t32', rA)], [('t32', rA)])
                        nch = n // 32
                        ch0 = t0 // 32
                        bv = A_.rearrange("p (c l) -> p c l", l=32)
                        tt(D_.rearrange("p (c l) -> p c l", l=32), bv, bv[:, :, 15].unsqueeze(2).to_broadcast([128, nch, 32]), ALU.subtract,
                           [('t32', rA)], [('t32', rD)])
                        act(ebl[:, hd, ch0:ch0 + nch], bv[:, :, 31], AF.Exp, [('t32', rA)], [('ebl', hd, ti)])
                        act(ebr[:, hd, ch0:ch0 + nch], bv[:, :, 15], AF.Exp, [('t32', rA)], [('ebr', hd, ti)])
                        act(wsc[:, hd, ch0:ch0 + nch], D_.rearrange("p (c l) -> p c l", l=32)[:, :, 31], AF.Exp, [('t32', rD)], [('wsc', hd, ti)])
                        dsrc, dk = D_, ('t32', rD)
                    else:
                        scan(A_, rmask[:, 512:512 + n], C_, 0.0, [K, ('t32', rC), ('t32', rA)], [('t32', rA)])
                        act(ebls[:, hd, :], A_.rearrange("p (c l) -> p c l", l=4)[:, :, 3], AF.Exp, [('t32', rA)], [('ebls', hd)])
                        dsrc, dk = A_, ('t32', rA)
                    act(C_, dsrc, AF.Exp, [dk, ('t32', rC)], [('t32', rC)])
                    tt(qe[:, hd, t0:t0 + n], ps[:, bq, :n], C_, ALU.mult, [('ps', bq), ('t32', rC)], [('qe', hd, ti)])
                    act(C_, dsrc, AF.Exp, [dk, ('t32', rC)], [('t32', rC)], scale=-1.0)
                    tt(ke[:, hd, t0:t0 + n], B_, C_, ALU.mult, [('t32', rB), ('t32', rC)], [('ke', hd, ti)])
            for half in range(2):
                sl = nws()
                wv = wslot(sl, [128, 8, 512])
                dma('pool', wv, Wv_[:, :, 2048 + half * 512:2048 + (half + 1) * 512], 'w%d' % sl, writes=[('ws', sl)])
                for ti, (t0, n, kind) in enumerate(tiles):
                    nblk = (n + 127) // 128
                    for bl in range(nblk):
                        nt = min(128, n - bl * 128)
                        blk = (t0 // 128 + bl) if kind == 'p' else 8
                        bank = nb()
                        for kc in range(8):
                            mm(ps[:nt, bank, :], h[:, kc, t0 + bl * 128:t0 + bl * 128 + nt], wv[:, kc, :], kc == 0, kc == 7,
                               [('ws', sl), ('h', ti, kc)], [('ps', bank)])
                        cp(vtk[:nt, blk, half * 512:(half + 1) * 512], ps[:nt, bank, :], [('ps', bank)], [('vtk', blk, half)], eng='act')
            for half in range(2):
                sl = nws()
                wv = wslot(sl, [128, 8, 512])
                dma('pool', wv, Wv_[:, :, 3072 + half * 512:3072 + (half + 1) * 512], 'w%d' % sl, writes=[('ws', sl)])
                for ti, (t0, n, kind) in enumerate(tiles):
                    for hl in range(4):
                        hd = half * 4 + hl
                        bank = nb()
                        for kc in range(8):
                            mm(ps[:, bank, :n], wv[:, kc, hl * 128:(hl + 1) * 128], h[:, kc, t0:t0 + n], kc == 0, kc == 7,
                               [('ws', sl), ('h', ti, kc)], [('ps', bank)])
                        act(sg[:, hd, t0:t0 + n], ps[:, bank, :n], AF.Silu, [('ps', bank)], [('sg', hd, ti)])
            s.barrier()
            allh = [('h', ti, c) for ti in range(len(tiles)) for c in range(8)]

            def finish_o(banks, ncols, zdst_fn, sg_fn, sgkeys, zkeys):
                for (bk, h0_, nh) in banks:
                    w = nh * ncols
                    q = n16()
                    act(t16[:, q, :w], ps[:, bk, :w], AF.Square, [('ps', bk)], [('t16', q)])
                    b2 = nb()
                    mm(ps[:, b2, :w], ones_bf[:], t16[:, q, :w], True, True, [K, ('t16', q)], [('ps', b2)])
                    r = n32()
                    act(t32[:, r, :w], ps[:, b2, :w], AF.Sqrt, [('ps', b2), K], [('t32', r)], bias=epsc[:], scale=1.0 / 128)
                    r2 = n32()
                    recip(t32[:, r2, :w], t32[:, r, :w], [('t32', r)], [('t32', r2)])
                    tt(t32[:, r, :w], ps[:, bk, :w], t32[:, r2, :w], ALU.mult, [('ps', bk), ('t32', r2), ('t32', r)], [('t32', r)])
                    stt(zdst_fn(h0_, nh), t32[:, r, :w].rearrange("p (a b) -> p a b", a=nh), gn[:, 0:1], sg_fn(h0_, nh), ALU.mult, ALU.mult,
                        [('t32', r), K] + sgkeys + allh, zkeys)

            if has_s:
                TS = 1024
                bs = nb()
                for hd in range(8):
                    mm(ps[0:64, bs, hd * 64:(hd + 1) * 64], ke[:, hd, TS:TS + 64], qe[:, hd, TS:TS + 64], True, True,
                       [('ke', hd, 2), ('qe', hd, 2)], [('ps', bs)])
                qp = n16()
                tt(t16[0:64, qp, :].rearrange("p (a b) -> p a b", a=8), ps[0:64, bs, :].rearrange("p (a b) -> p a b", a=8),
                   smask[:].unsqueeze(1).to_broadcast([64, 8, 64]), ALU.mult, [('ps', bs), K], [('t16', qp)])
                btr = nb()
                ptb = ps[:, btr, :].bitcast(BF16)
                for hd in range(8):
                    tr(ptb[0:64, hd * 128:(hd + 1) * 128], ke[:, hd, TS:TS + 64], ident_b[:], [('ke', hd, 2), K], [('ps', btr)])
                cp(keT[0:64, 0, :], ptb[0:64, :], [('ps', btr)], [('keT', 0)], eng='act')
                bo = nb()
                nb.hold(bo)
                for hd in range(8):
                    mm(ps[:, bo, hd * 64:(hd + 1) * 64], vtk[0:64, 8, hd * 128:(hd + 1) * 128], t16[0:64, qp, hd * 64:(hd + 1) * 64], hd == 0, False,
                       [('vtk', 8, hd // 4), ('t16', qp)], [('ps', bo)], skip_group_check=True)
                for sq in range(16):
                    rr = sq % 2
                    S0 = Sst if rr == 0 else tmpS
                    skey = 'Sst' if rr == 0 else 'tmpS'
                    dma('sp', S0, s0S[sq].rearrange("h k v -> k h v"), 'l_s0%d' % rr, writes=[skey])
                    cp(Sbf, S0, [skey], ['Sbf'], eng='act')
                    for hd in range(8):
                        c0 = hd * 64 + sq * 4
                        mm(ps[:, bo, c0:c0 + 4], Sbf[:, hd, :], qe[:, hd, TS + sq * 4:TS + sq * 4 + 4], False, True,
                           ['Sbf', ('qe', hd, 2)], [('ps', bo)], skip_group_check=True)
                    ts(keT[0:64, 1, :], keT[0:64, 0, :], onehot[:, sq:sq + 1], None, ALU.mult, ALU.bypass, [('keT', 0), K], [('keT', 1)])
                    u0, u1 = nb(), nb()
                    for hd in range(8):
                        ub = u0 if hd < 4 else u1
                        mm(ps[:, ub, (hd % 4) * 128:(hd % 4 + 1) * 128], keT[0:64, 1, hd * 128:(hd + 1) * 128], vtk[0:64, 8, hd * 128:(hd + 1) * 128],
                           True, True, [('keT', 1), ('vtk', 8, hd // 4)], [('ps', ub)])
                    for half, ub in ((0, u0), (1, u1)):
                        sv = S0[:, half * 4:(half + 1) * 4, :]
                        tt(sv, sv, ps[:, ub, :].rearrange("p (a b) -> p a b", a=4), ALU.add, [skey, ('ps', ub)], [skey])
                        tt(sv, sv, ebls[:, half * 4:(half + 1) * 4, sq:sq + 1].to_broadcast([128, 4, 128]), ALU.mult,
                           [skey] + [('ebls', hh) for hh in range(8)], [skey])
                    dma('sp', ss_o[sq].rearrange("h k v -> k h v"), S0, 'o_ss%d' % rr, reads=[skey])
                finish_o([(bo, 0, 8)], 64,
                         lambda h0_, nh: zT[:, h0_:h0_ + nh, TS:TS + 64],
                         lambda h0_, nh: sg[:, h0_:h0_ + nh, TS:TS + 64],
                         [('sg', hh, 2) for hh in range(8)], [('z', 2)])
                nb.release(bo)
                s.barrier(dma_prefix=('o_ss', 'l_s0'))
                memset(Sst, 0.0, ['Sst', ('Sst', 0), ('Sst', 1)])
            HS = [slice(0, 4), slice(4, 8)]
            for b in range(8):
                ti = b // 4
                cols = slice(b * 128, (b + 1) * 128)
                ek = [('ebr', hh, ti) for hh in range(8)] + [('ebl', hh, ti) for hh in range(8)] + [('wsc', hh, ti) for hh in range(8)]
                btr = nb()
                ptb = ps[:, btr, :].bitcast(BF16)
                for half in range(2):
                    qk = n16()
                    kdv = t16[:, qk, :].rearrange("p (h c l) -> p h c l", h=4, c=4)
                    tt(kdv, ke[:, HS[half], cols].rearrange("p h (c l) -> p h c l", c=4),
                       wsc[:, HS[half], b * 4:b * 4 + 4].unsqueeze(3).to_broadcast([128, 4, 4, 32]), ALU.mult,
                       [('ke', hh, ti) for hh in range(half * 4, half * 4 + 4)] + ek, [('t16', qk)], eng='pool')
                    for hl in range(4):
                        hd = half * 4 + hl
                        tr(ptb[:, hd * 128:(hd + 1) * 128], t16[:, qk, hl * 128:(hl + 1) * 128], ident_b[:], [('t16', qk), K], [('ps', btr)])
                kslot = b % 2
                cp(keT[:, kslot, :], ptb[:, :], [('ps', btr)], [('keT', kslot)], eng='act')
                pbanks = []
                for half in range(2):
                    bk = nb()
                    for hl in range(4):
                        hd = half * 4 + hl
                        mm(ps[:, bk, hl * 128:(hl + 1) * 128], ke[:, hd, cols], qe[:, hd, cols], True, True, [('ke', hd, ti), ('qe', hd, ti)], [('ps', bk)])
                    qp = n16()
                    tt(t16[:, qp, :].rearrange("p (a b) -> p a b", a=4), ps[:, bk, :].rearrange("p (a b) -> p a b", a=4),
                       hmask[:].unsqueeze(1).to_broadcast([128, 4, 128]), ALU.mult, [('ps', bk), K], [('t16', qp)])
                    pbanks.append(qp)
                obanks = [nb(), nb()]
                nb.hold(*obanks)
                for hd in range(8):
                    ob_ = obanks[hd // 4]
                    qp = pbanks[hd // 4]
                    hl = hd % 4
                    mm(ps[:, ob_, hl * 128:(hl + 1) * 128], vtk[:, b, hd * 128:(hd + 1) * 128], t16[:, qp, hl * 128:(hl + 1) * 128], hl == 0, False,
                       [('vtk', b, hd // 4), ('t16', qp)], [('ps', ob_)], skip_group_check=True)
                for c in range(4):
                    ch = b * 4 + c
                    for half in range(2):
                        tt(Sbf[:, HS[half], :], Sst[:, HS[half], :], ebr[:, HS[half], ch:ch + 1].to_broadcast([128, 4, 128]), ALU.mult,
                           [('Sst', half), 'Sst'] + ek, [('Sbf', half)])
                    us = [nb(), nb()]
                    for half in range(2):
                        for hl in range(4):
                            hd = half * 4 + hl
                            c0 = hl * 128 + c * 32
                            mm(ps[:, obanks[half], c0:c0 + 32], Sbf[:, hd, :], qe[:, hd, b * 128 + c * 32:b * 128 + (c + 1) * 32], False, True,
                               [('Sbf', half), ('qe', hd, ti)], [('ps', obanks[half])], skip_group_check=True)
                        for hl in range(4):
                            hd = half * 4 + hl
                            mm(ps[:, us[half], hl * 128:(hl + 1) * 128], keT[32 * c:32 * (c + 1), kslot, hd * 128:(hd + 1) * 128],
                               vtk[32 * c:32 * (c + 1), b, hd * 128:(hd + 1) * 128], True, True, [('keT', kslot), ('vtk', b, hd // 4)], [('ps', us[half])],
                               tile_position=(32 * c, 0))
                    for half in range(2):
                        tt(Sst[:, HS[half], :], Sst[:, HS[half], :], ebl[:, HS[half], ch:ch + 1].to_broadcast([128, 4, 128]), ALU.mult,
                           [('Sst', half), 'Sst'] + ek, [('Sst', half)], eng='pool')
                    for half in range(2):
                        tt(Sst[:, HS[half], :], Sst[:, HS[half], :], ps[:, us[half], :].rearrange("p (a b) -> p a b", a=4), ALU.add,
                           [('Sst', half), ('ps', us[half])], [('Sst', half)])
                finish_o([(obanks[0], 0, 4), (obanks[1], 4, 4)], 128,
                         lambda h0_, nh: zT[:, h0_:h0_ + nh, cols],
                         lambda h0_, nh: sg[:, h0_:h0_ + nh, cols],
                         [('sg', hh, ti) for hh in range(8)], [('z', ti)])
                nb.release(*obanks)
            if g == 1:
                dma('sp', sp_o.rearrange("h k v -> k h v"), Sst, 'o_sp', reads=['Sst', ('Sst', 0), ('Sst', 1)])
            s.barrier()
            Wo = odd_w_out

            def wchunk(wv, half):
                return [(wv, Wo[:, half * 512:(half + 1) * 512].rearrange("(c p) f -> p c f", p=128))]
            outproj(sub, tiles, wchunk, lambda kc, t0, n: zT[:, kc, t0:t0 + n], lambda kc, ti: [('z', ti), ('h', ti, kc)], ybt, hook)

        kcar = sb("kcar", [128, 128], BF16)
        vcar = sb("vcar", [128, 128], BF16)
        Sst = sb("Sst", [128, 8, 128], F32)[:]

        for g in groups:
            if g == 0:
                tiles = [(0, 512, 'p'), (512, 512, 'p'), (1024, 64, 's')]
            else:
                tiles = [(0, 512, 'p'), (512, 512, 'p')]
            xv = xT.rearrange("(c p) t -> p c t", p=128)
            for ti, (t0, n, kind) in enumerate(tiles):
                src_c0 = g * 1024 + t0 if kind == 'p' else NP
                dma('sp', x[:, :, t0:t0 + n], xv[:, :, src_c0:src_c0 + n], 'l_x%d' % ti, writes=[('x', ti, c) for c in range(8)])
            yv = yT.rearrange("(c p) t -> p c t", p=128)
            for l in range(2):
                for k3 in range(3):
                    sub = l * 3 + k3

                    def hook(ti, tile, sub=sub):
                        t0, n, kind = tile
                        if sub == 5:
                            dst_c0 = g * 1024 + t0 if kind == 'p' else NP
                            dma('sp', yv[:, :, dst_c0:dst_c0 + n], x[:, :, t0:t0 + n], 'o_y%d' % ti,
                                reads=[('x', ti, c) for c in range(8)])
                    enabled = do_ffn if k3 != 1 else (do_even if l == 0 else do_odd)
                    for ti, tile in enumerate(tiles):
                        prenorm_tile(sub, ti, tile)
                    if not enabled:
                        for ti, tile in enumerate(tiles):
                            hook(ti, tile)
                    elif k3 == 0 or k3 == 2:
                        ffn(l, 0 if k3 == 0 else 1, sub, tiles, hook)
                    elif l == 0:
                        even_mixer(sub, g, tiles, hook)
                    else:
                        odd_mixer(sub, g, tiles, hook)
                    s.barrier(wait_engs=('act', 'dve'))
            s.barrier()
        s.finish()
        global _LAST_COUNTS
        _LAST_COUNTS = (dict(s.cnt), {e: len(v) for e, v in s.q.items()}, s.nsem)
        with nc.Block() as block:
            s.emit(block)
    return nc


def _consts():
    c = {}
    c['ident'] = np.eye(128, dtype=np.float32)
    k = np.arange(128)[:, None]
    q = np.arange(128)[None, :]
    am = np.zeros((128, 2, 128), np.float32)
    am[:, 0, :] = (k <= q)
    am[:, 1, :] = (k > q)
    c['amask'] = am.reshape(128, 256)
    c['hmask'] = ((k // 32 == q // 32) & (k <= q)).astype(np.float32)
    k6 = np.arange(64)[:, None]
    q6 = np.arange(64)[None, :]
    c['smask'] = ((k6 // 4 == q6 // 4) & (k6 <= q6)).astype(np.float32)
    c['mcmask'] = (np.arange(128)[:, None] > np.arange(4)[None, :]).astype(np.float32)
    rm = np.ones((128, 576), np.float32)
    rm[:, 0:512:32] = 0.0
    rm[:, 512:576:4] = 0.0
    c['rmask'] = rm
    c['onehot'] = (np.arange(64)[:, None] // 4 == np.arange(16)[None, :]).astype(np.float32)
    pos = np.concatenate([np.arange(NP, dtype=np.int32), np.tile(16384 + np.arange(4, dtype=np.int32), 16)]).astype(np.float32)
    inv_freq = (np.float32(500000.0) ** (-(np.arange(0, 16, 2, dtype=np.float32)) / np.float32(16))).astype(np.float32)
    ang = (pos[None, :] * inv_freq[:, None]).astype(np.float32)
    cs = np.cos(ang).astype(np.float32)
    sn = np.sin(ang).astype(np.float32)
    rc = np.ones((64, NP + NS), np.float32)
    rs = np.zeros((64, NP + NS), np.float32)
    rc[0:8] = cs
    rc[8:16] = cs
    rs[0:8] = -sn
    rs[8:16] = sn
    c['ropec'] = np.concatenate([rc, rc], 0)
    c['ropes'] = np.concatenate([rs, rs], 0)
    return c


_NC_CACHE = {}


def kernel(x_prompt, x_sample, c_prompt, c_sample, cache_k_win, cache_v_win, state_conv_rglru,
           state_h_rglru, state_s_hgrn, norm_pre, norm_post, ada_w, ada_b, ffn1_w_in, ffn1_w_out,
           ffn2_w_in, ffn2_w_out, even_w_in, even_w_out, attn_sinks, rg_conv_w, rg_conv_b, rg_wa,
           rg_ba, rg_wx, rg_bx, rg_lambda, odd_w_in, odd_w_out, hgrn_lb_logits, hgrn_gnorm, _flags=None):
    f = lambda a: np.ascontiguousarray(np.asarray(a, dtype=np.float32))
    flags = dict(_flags or {})
    ncores_run = flags.pop('ncores', NCORES)
    key = tuple(sorted(flags.items()))
    if key not in _NC_CACHE:
        _NC_CACHE[key] = build_nc(**flags)
    nc = _NC_CACHE[key]
    consts = _consts()

    def fm(v, nchunk):
        v = np.asarray(v, np.float32)
        lead = v.shape[:-1]
        v = v.reshape(lead + (nchunk, 128))
        v = np.moveaxis(v, -1, 0)
        return np.ascontiguousarray(v)

    shared = {
        'normpre': fm(np.asarray(norm_pre).reshape(6, D), 8),
        'normpost': fm(np.asarray(norm_post).reshape(6, D), 8),
        'adab': fm(np.asarray(ada_b).reshape(6, 3 * D), 24),
        'ada_w': f(ada_w),
        'ffn1_w_in': f(ffn1_w_in), 'ffn2_w_in': f(ffn2_w_in), 'ffn1_w_out': f(ffn1_w_out), 'ffn2_w_out': f(ffn2_w_out),
        'even_w_in': f(even_w_in)[0], 'even_w_out': f(even_w_out)[0],
        'convw': np.ascontiguousarray(np.moveaxis(np.asarray(rg_conv_w, np.float32)[0].reshape(4, 4, 128), 2, 0).transpose(0, 2, 1)),
        'convb': fm(np.asarray(rg_conv_b)[0], 4),
        'rg_wa': f(rg_wa)[0], 'rg_wx': f(rg_wx)[0],
        'rgba': fm(np.asarray(rg_ba)[0], 4), 'rgbx': fm(np.asarray(rg_bx)[0], 4), 'rglam': fm(np.asarray(rg_lambda)[0], 4),
        'odd_w_in': f(odd_w_in)[0], 'odd_w_out': f(odd_w_out)[0],
        'lbl': fm(np.asarray(hgrn_lb_logits), 8),
        'gnorm': f(np.asarray(hgrn_gnorm)[0].reshape(128, 1)),
    }
    sk = np.asarray(attn_sinks, np.float32)[0]
    sT = np.zeros((128, 4), np.float32)
    sT[0:64, :] = sk[0:4][None, :]
    sT[64:128, :] = sk[4:8][None, :]
    shared['sinkT'] = sT
    shared.update(consts)
    cwv = np.asarray(rg_conv_w, np.float32)[0].reshape(4, 4, 128)
    shared['convw'] = np.ascontiguousarray(cwv.transpose(2, 1, 0))

    xp = np.asarray(x_prompt, np.float32)
    xs = np.asarray(x_sample, np.float32)
    in_maps = []
    for i in range(NCORES):
        m = dict(shared)
        sl = slice(16 * i, 16 * i + 16)
        xt = np.concatenate([xp[i], xs[sl].reshape(64, D)], 0)
        m['xT'] = np.ascontiguousarray(xt.T)
        ct = np.concatenate([np.asarray(c_prompt, np.float32)[i:i + 1], np.asarray(c_sample, np.float32)[sl]], 0)
        m['cT'] = np.ascontiguousarray(ct.T)
        m['cachek'] = f(np.asarray(cache_k_win)[0, sl].reshape(16, 128, 128))
        m['cachev'] = f(np.asarray(cache_v_win)[0, sl].reshape(16, 128, 128))
        cs_ = np.asarray(state_conv_rglru, np.float32)[0, sl]
        m['convS'] = np.ascontiguousarray(cs_.reshape(16, 3, 4, 128).transpose(3, 2, 0, 1))
        hs_ = np.asarray(state_h_rglru, np.float32)[0, sl]
        m['h0S'] = np.ascontiguousarray(hs_.reshape(16, 4, 128).transpose(2, 1, 0))
        m['s0S'] = f(np.asarray(state_s_hgrn)[0, sl])
        in_maps.append(m)
    res = run_bass_kernel_spmd(nc, in_maps[:ncores_run], core_ids=list(range(ncores_run)))
    R = list(res.results)
    while len(R) < NCORES:
        R.append(R[0])
    y_prompt = np.stack([R[i]['yT'][:, :NP].T for i in range(NCORES)], 0)
    y_sample = np.concatenate([R[i]['yT'][:, NP:].T.reshape(16, 4, D) for i in range(NCORES)], 0)
    k_win_p = np.stack([R[i]['kwp'].T.reshape(128, 2, 64) for i in range(NCORES)], 0)[None]
    v_win_p = np.stack([R[i]['vwp'].reshape(128, 2, 64) for i in range(NCORES)], 0)[None]
    conv_p = np.stack([R[i]['convp'].transpose(2, 1, 0).reshape(3, 512) for i in range(NCORES)], 0)[None]
    h_p = np.stack([R[i]['hp'].T.reshape(512) for i in range(NCORES)], 0)[None]
    s_p = np.stack([R[i]['sp_o'] for i in range(NCORES)], 0)[None]
    k_win_s = np.concatenate([R[i]['kws'].reshape(16, 128, 2, 64) for i in range(NCORES)], 0)[None]
    v_win_s = np.concatenate([R[i]['vws'].reshape(16, 128, 2, 64) for i in range(NCORES)], 0)[None]
    conv_s = np.concatenate([R[i]['convs'].transpose(2, 3, 1, 0).reshape(16, 3, 512) for i in range(NCORES)], 0)[None]
    h_s = np.concatenate([R[i]['hs_o'].transpose(2, 1, 0).reshape(16, 512) for i in range(NCORES)], 0)[None]
    s_s = np.concatenate([R[i]['ss_o'] for i in range(NCORES)], 0)[None]
    outs = (y_prompt, y_sample, k_win_p, v_win_p, conv_p, h_p, s_p, k_win_s, v_win_s, conv_s, h_s, s_s)
    return tuple(np.ascontiguousarray(o, dtype=np.float32) for o in outs)
```

```python
import numpy as np
from contextlib import ExitStack
import concourse.bass as bass
import concourse.mybir as mybir
from concourse.bass_utils import run_bass_kernel_spmd

F32 = mybir.dt.float32
BF16 = mybir.dt.bfloat16
AF = mybir.ActivationFunctionType
ALU = mybir.AluOpType

ENGS = ['pe', 'act', 'dve', 'pool', 'sp']
D = 1024
DFF = 2816
NP = 2048
NS = 64
TG = 1088
NCORES = 8
EPS = 1e-6


class Sched:
    EPOCH = 3000

    def __init__(self, nc):
        self.nc = nc
        self.q = {e: [] for e in ENGS}
        self.cnt = {e: 0 for e in ENGS}
        self.sems = {}
        self.waited = {e: {} for e in ENGS}
        self.lastw = {}
        self.readers = {}
        self.dcount = {}
        self.nsem = 0
        self.pool_out = []
        self.pool_sum = 0

    def sem(self, key):
        if key not in self.sems:
            self.sems[key] = self.nc.alloc_semaphore(name="s%d" % self.nsem)
            self.nsem += 1
        return self.sems[key]

    def _filter(self, eng, deps):
        out = {}
        wd = self.waited[eng]
        for key, val in deps:
            if eng == 'pe' and key[0] == 'e' and key[1] == 'pe':
                continue
            if wd.get(key, 0) >= val:
                continue
            if out.get(key, 0) < val:
                out[key] = val
        for k, v in out.items():
            wd[k] = v
        return list(out.items())

    def _deps(self, eng, reads, writes):
        deps = []
        for r in reads:
            if r in self.lastw:
                deps.append(self.lastw[r])
        for w in writes:
            if w in self.lastw:
                deps.append(self.lastw[w])
            deps.extend(self.readers.get(w, ()))
        return self._filter(eng, deps)

    def _book(self, me, reads, writes):
        for r in reads:
            self.readers.setdefault(r, []).append(me)
        for w in writes:
            self.lastw[w] = me
            self.readers[w] = []

    def op(self, eng, fn, reads=(), writes=()):
        waits = self._deps(eng, reads, writes)
        idx = self.cnt[eng]
        self.cnt[eng] += 1
        key = ('e', eng, idx // self.EPOCH)
        self.sem(key)
        val = idx % self.EPOCH + 1
        self.q[eng].append((waits, fn, key, 1))
        self._book((key, val), reads, writes)

    POOL_DESC_LIMIT = 1 << 40

    def dma(self, eng, fn, semname, reads=(), writes=(), ndesc=1024):
        waits = self._deps(eng, reads, writes)
        if eng == 'pool':
            extra = []
            while self.pool_out and self.pool_sum + ndesc > self.POOL_DESC_LIMIT:
                k0, v0, n0 = self.pool_out.pop(0)
                self.pool_sum -= n0
                extra.append((k0, v0))
            if extra:
                waits = waits + self._filter(eng, extra)
        key = ('d', semname)
        self.sem(key)
        self.dcount[key] = self.dcount.get(key, 0) + 16
        self.q[eng].append((waits, fn, key, 16))
        self._book((key, self.dcount[key]), reads, writes)
        if eng == 'pool':
            self.pool_out.append((key, self.dcount[key], ndesc))
            self.pool_sum += ndesc

    def raw(self, eng, fn):
        self.q[eng].append(([], fn, None, 0))

    def _now(self, engs):
        deps = []
        for e in engs:
            if self.cnt[e] > 0:
                idx = self.cnt[e] - 1
                deps.append((('e', e, idx // self.EPOCH), idx % self.EPOCH + 1))
        return deps

    def barrier(self, engs=('pe', 'act', 'dve'), dma_prefix=('o_', 'l_'), wait_engs=None):
        deps = self._now(list(engs) + ['pool'])
        for key, v in self.dcount.items():
            if key[1].startswith(dma_prefix):
                deps.append((key, v))
        if wait_engs is None:
            wait_engs = engs
        for e in list(wait_engs) + ['sp']:
            w = self._filter(e, deps)
            if w:
                self.q[e].append((w, None, None, 0))

    def finish(self, eng='sp'):
        deps = self._now(ENGS)
        for key, v in self.dcount.items():
            deps.append((key, v))
        self.q[eng].append((deps, None, None, 0))

    def emit(self, block):
        sems = self.sems
        qs = self.q

        def run(engobj, lst):
            for waits, fn, key, inc in lst:
                for k, v in waits:
                    engobj.wait_ge(sems[k], v)
                if fn is not None:
                    ins = fn(engobj)
                    if key is not None:
                        ins.then_inc(sems[key], inc)

        @block.tensor
        def _(e):
            run(e, qs['pe'])

        @block.scalar
        def _(e):
            run(e, qs['act'])

        @block.vector
        def _(e):
            run(e, qs['dve'])

        @block.gpsimd
        def _(e):
            run(e, qs['pool'])

        @block.sync
        def _(e):
            run(e, qs['sp'])


class Rot:
    def __init__(self, n):
        self.n = n
        self.i = 0
        self.held = set()

    def __call__(self):
        for _ in range(self.n):
            v = self.i
            self.i = (self.i + 1) % self.n
            if v not in self.held:
                return v
        raise RuntimeError("all slots held")

    def hold(self, *vs):
        self.held.update(vs)

    def release(self, *vs):
        self.held.difference_update(vs)


def build_nc(do_even=True, do_odd=True, do_ffn=True, groups=(0, 1), estop=9, ostop=9):
    nc = bass.Bass("TRN2", target_bir_lowering=False)

    def din(name, shape):
        return nc.dram_tensor(name, list(shape), F32, kind="ExternalInput").ap()

    def dout(name, shape):
        return nc.dram_tensor(name, list(shape), F32, kind="ExternalOutput").ap()

    xT = din("xT", [D, NP + NS])
    cT = din("cT", [D, 17])
    cachek = din("cachek", [16, 128, 128])
    cachev = din("cachev", [16, 128, 128])
    convS = din("convS", [128, 4, 16, 3])
    h0S = din("h0S", [128, 4, 16])
    s0S = din("s0S", [16, 8, 128, 128])
    normpre = din("normpre", [128, 6, 8])
    normpost = din("normpost", [128, 6, 8])
    adab = din("adab", [128, 6, 24])
    ada_w = din("ada_w", [2, 3, D, 3 * D])
    ffn_w_in = [din("ffn1_w_in", [2, D, 2 * DFF]), din("ffn2_w_in", [2, D, 2 * DFF])]
    ffn_w_out = [din("ffn1_w_out", [2, DFF, D]), din("ffn2_w_out", [2, DFF, D])]
    even_w_in = din("even_w_in", [D, 1792])
    even_w_out = din("even_w_out", [D, D])
    sinkT = din("sinkT", [128, 4])
    convw = din("convw", [128, 4, 4])
    convb = din("convb", [128, 4])
    rg_wa = din("rg_wa", [8, 64, 64])
    rg_wx = din("rg_wx", [8, 64, 64])
    rgba = din("rgba", [128, 4])
    rgbx = din("rgbx", [128, 4])
    rglam = din("rglam", [128, 4])
    odd_w_in = din("odd_w_in", [D, 4096])
    odd_w_out = din("odd_w_out", [D, D])
    lbl = din("lbl", [128, 2, 8])
    gnorm = din("gnorm", [128, 1])
    ident_d = din("ident", [128, 128])
    amask_d = din("amask", [128, 256])
    hmask_d = din("hmask", [128, 128])
    smask_d = din("smask", [64, 64])
    mc_d = din("mcmask", [128, 4])
    rmask_d = din("rmask", [128, 512 + 64])
    onehot_d = din("onehot", [64, 16])
    ropec_d = din("ropec", [128, NP + NS])
    ropes_d = din("ropes", [128, NP + NS])

    yT = dout("yT", [D, NP + NS])
    kwp = dout("kwp", [128, 128])
    vwp = dout("vwp", [128, 128])
    convp = dout("convp", [128, 4, 3])
    hp = dout("hp", [128, 4])
    sp_o = dout("sp_o", [8, 128, 128])
    kws = dout("kws", [16, 128, 128])
    vws = dout("vws", [16, 128, 128])
    convs = dout("convs", [128, 4, 16, 3])
    hs_o = dout("hs_o", [128, 4, 16])
    ss_o = dout("ss_o", [16, 8, 128, 128])

    es = ExitStack()

    def sb(name, shape, dt):
        return es.enter_context(nc.sbuf_tensor("s_" + name, list(shape), dt))

    with es:
        s = Sched(nc)
        x = sb("x", [128, 8, TG], F32)
        h = sb("h", [128, 8, TG], BF16)
        ARW = 21120
        arena = sb("arena", [128, ARW], F32)
        NWS = 4
        wst = sb("wst", [128, NWS, 4096], BF16)
        NT32 = 6
        t32 = sb("t32", [128, NT32, 512], F32)
        NT16 = 4
        t16 = sb("t16", [128, NT16, 512], BF16)
        NINV = 2
        inv = sb("inv", [128, NINV, 512], F32)
        Amod = sb("Amod", [128, 6, 8, 17], F32)
        Smod = sb("Smod", [128, 6, 8, 17], F32)
        Gmod = sb("Gmod", [128, 6, 8, 17], F32)
        ones_bf = sb("ones_bf", [128, 128], BF16)
        ident_f = sb("ident_f", [128, 128], F32)
        ident_b = sb("ident_b", [128, 128], BF16)
        epsc = sb("epsc", [128, 1], F32)
        onec = sb("onec", [128, 1], F32)
        amask = sb("amask", [128, 256], BF16)
        hmask = sb("hmask", [128, 128], BF16)
        smask = sb("smask", [64, 64], BF16)
        mcm = sb("mcm", [128, 4], BF16)
        rmask = sb("rmask", [128, 576], F32)
        onehot = sb("onehot", [64, 16], F32)
        npre = sb("npre", [128, 6, 8], F32)
        npost = sb("npost", [128, 6, 8], F32)
        adabs = sb("adabs", [128, 6, 24], F32)
        sinkexp = sb("sinkexp", [128, 4], F32)
        cw = sb("cw", [128, 4, 4], F32)
        cb = sb("cb", [128, 4], F32)
        hba = sb("hba", [128, 4], F32)
        hbx = sb("hbx", [128, 4], F32)
        hcoef = sb("hcoef", [128, 4], F32)
        BDa = sb("BDa", [128, 4, 128], BF16)
        BDx = sb("BDx", [128, 4, 128], BF16)
        hc0 = sb("hc0", [128, 8], F32)
        hc1 = sb("hc1", [128, 8], F32)
        lbv = sb("lbv", [128, 8], F32)
        omlb = sb("omlb", [128, 8], F32)
        gn = sb("gn", [128, 1], F32)
        convc = sb("convc", [128, 4, 3], F32)
        hcar = sb("hcar", [128, 4], F32)
        sm1 = sb("sm1", [128, 2, 8], F32)
        ps = es.enter_context(nc.psum_tensor("ps", [128, 8, 512], F32))

        nb = Rot(8)
        n32 = Rot(NT32)
        n16 = Rot(NT16)
        ninv = Rot(NINV)
        nws = Rot(NWS)

        def mm(out, lhsT, rhs, start, stop, reads, writes, **kw):
            s.op('pe', lambda e: e.matmul(out, lhsT=lhsT, rhs=rhs, start=start, stop=stop, **kw), reads, writes)

        def tr(out, in_, ident, reads, writes):
            s.op('pe', lambda e: e.transpose(out=out, in_=in_, identity=ident), reads, writes)

        SET6 = (AF.Ln, AF.Exp, AF.Square, AF.Copy, AF.Identity)
        actstate = {'cur6': False}

        def act(out, in_, func, reads, writes, bias=None, scale=None, force6=False):
            if func not in SET6:
                actstate['cur6'] = False
            kw = {}
            if bias is not None:
                kw['bias'] = bias
            if scale is not None:
                kw['scale'] = scale
            s.op('act', lambda e: e.activation(out=out, in_=in_, func=func, **kw), reads, writes)

        def tt(out, in0, in1, op, reads, writes, eng='dve'):
            s.op(eng, lambda e: e.tensor_tensor(out=out, in0=in0, in1=in1, op=op), reads, writes)

        def ts(out, in0, s1, s2, op0, op1, reads, writes, eng='dve'):
            s.op(eng, lambda e: e.tensor_scalar(out=out, in0=in0, scalar1=s1, scalar2=s2, op0=op0, op1=op1), reads, writes)

        def stt(out, in0, scalar, in1, op0, op1, reads, writes):
            s.op('dve', lambda e: e.scalar_tensor_tensor(out=out, in0=in0, scalar=scalar, in1=in1, op0=op0, op1=op1), reads, writes)

        def cp(out, in_, reads, writes, eng='dve'):
            if eng == 'act':
                s.op('act', lambda e: e.activation(out=out, in_=in_, func=AF.Copy), reads, writes)
            else:
                s.op(eng, lambda e: e.tensor_copy(out, in_), reads, writes)

        def recip(out, in_, reads, writes, scratch=None):
            if scratch is None:
                s.op('dve', lambda e: e.reciprocal(out, in_), reads, writes)
            else:
                s.op('dve', lambda e: e.reciprocal_approx_accurate(out, in_, scratch), reads, writes)

        def memset(ap, val, writes, eng='dve'):
            s.op(eng, lambda e: e.memset(ap, val), (), writes)

        def scan(out, d0, d1, init, reads, writes):
            s.op('dve', lambda e: e.tensor_tensor_scan(out=out, data0=d0, data1=d1, initial=init, op0=ALU.mult, op1=ALU.add), reads, writes)

        def dma(eng, out, in_, sem, reads=(), writes=()):
            nd = 1
            for d_ in list(out.shape)[:-1]:
                nd *= d_
            s.dma(eng, lambda e: e.dma_start(out=out, in_=in_), sem, reads, writes, ndesc=nd)

        def av(off_words, shape, dt):
            n = 1
            for d_ in shape[1:]:
                n *= d_
            if dt == BF16:
                assert n % 2 == 0
                w = n // 2
                v = arena[:, off_words:off_words + w].bitcast(BF16)
            else:
                w = n
                v = arena[:, off_words:off_words + w]
            assert off_words + w <= ARW, (off_words, w, ARW)
            if len(shape) == 3:
                v = v.rearrange("p (a b) -> p a b", a=shape[1])
            elif len(shape) == 4:
                v = v.rearrange("p (a b c) -> p a b c", a=shape[1], b=shape[2])
            return v[0:shape[0]], off_words + w

        def wslot(i, shape):
            v = wst[:, i, :]
            n = 1
            for d_ in shape[1:]:
                n *= d_
            v = v[:, 0:n]
            if len(shape) == 3:
                v = v.rearrange("p (a b) -> p a b", a=shape[1])
            elif len(shape) == 4:
                v = v.rearrange("p (a b c) -> p a b c", a=shape[1], b=shape[2])
            return v

        K = 'const'
        for (dst, src) in [(ident_f, ident_d), (rmask, rmask_d), (onehot, onehot_d), (npre, normpre), (npost, normpost),
                           (adabs, adab), (cw, convw), (cb, convb), (gn, gnorm)]:
            dma('sp', dst[:], src, 'l_c_%s' % src.name, writes=[K])
        for (dst, src) in [(ident_b, ident_d), (amask, amask_d), (hmask, hmask_d), (smask, smask_d), (mcm, mc_d)]:
            dma('pool', dst[:], src, 'l_cb_%s' % src.name, writes=[K])
        memset(ones_bf[:], 1.0, [K])
        memset(epsc[:], EPS, [K])
        memset(onec[:], 1.0, [K])
        memset(BDa[:], 0.0, ['BD'])
        memset(BDx[:], 0.0, ['BD'])
        for k in range(8):
            c, hf = k // 2, k % 2
            dma('pool', BDa[hf * 64:(hf + 1) * 64, c, hf * 64:(hf + 1) * 64], rg_wa[k], 'l_bd', reads=(), writes=['BD'])
            dma('pool', BDx[hf * 64:(hf + 1) * 64, c, hf * 64:(hf + 1) * 64], rg_wx[k], 'l_bd', reads=(), writes=['BD'])
        dma('sp', sinkexp[:], sinkT, 'l_p1', writes=['p_sink'])
        act(sinkexp[:], sinkexp[:], AF.Exp, ['p_sink'], ['p_sink'])
        dma('sp', hba[:], rgba, 'l_p2', writes=['p_hba'])
        ts(hba[:], hba[:], 0.5, None, ALU.mult, ALU.bypass, ['p_hba'], ['p_hba'])
        dma('sp', hbx[:], rgbx, 'l_p3', writes=['p_hbx'])
        ts(hbx[:], hbx[:], 0.5, None, ALU.mult, ALU.bypass, ['p_hbx'], ['p_hbx'])
        dma('sp', hcoef[:], rglam, 'l_p4', writes=['p_hc'])
        act(hcoef[:], hcoef[:], AF.Exp, ['p_hc'], ['p_hc'], scale=-1.0)
        act(hcoef[:], hcoef[:], AF.Ln, ['p_hc', K], ['p_hc'], bias=onec[:], scale=1.0)
        ts(hcoef[:], hcoef[:], -4.0, None, ALU.mult, ALU.bypass, ['p_hc'], ['p_hc'])
        dma('sp', sm1[:], lbl, 'l_p5', writes=['p_lb'])
        tt(hc0[:], sm1[:, 1, :], sm1[:, 0, :], ALU.subtract, ['p_lb'], ['p_hc0'])
        act(hc0[:], hc0[:], AF.Tanh, ['p_hc0'], ['p_hc0'], scale=0.5)
        ts(hc1[:], hc0[:], -0.25, 0.25, ALU.mult, ALU.add, ['p_hc0'], ['p_hc1'])
        ts(hc0[:], hc0[:], 0.25, 0.75, ALU.mult, ALU.add, ['p_hc0', 'p_hc1'], ['p_hc0'])
        tt(lbv[:], hc0[:], hc1[:], ALU.subtract, ['p_hc0', 'p_hc1'], ['p_lbv'])
        ts(omlb[:], hc1[:], 2.0, None, ALU.mult, ALU.bypass, ['p_hc1'], ['p_lbv'])
        s.barrier()
        PK = [K, 'BD', 'p_sink', 'p_hba', 'p_hbx', 'p_hc', 'p_hc0', 'p_hc1']

        sc, o = av(0, [128, 8, 17], BF16)
        cts, o = av(o, [128, 8, 17], F32)
        modr, o = av(o, [128, 6, 24, 17], F32)
        dma('sp', cts, cT.rearrange("(c p) s -> p c s", p=128), 'l_p6', writes=['cts'])
        act(sc, cts, AF.Silu, ['cts'], ['sc'])
        for sub in range(6):
            l, k3 = sub // 3, sub % 3
            bank = nb()
            for blk in range(6):
                sl = nws()
                wv = wslot(sl, [128, 8, 512])
                dma('pool', wv, ada_w[l, k3, :, blk * 512:(blk + 1) * 512].rearrange("(c p) f -> p c f", p=128),
                    'w%d' % sl, writes=[('ws', sl)])
                for fl in range(4):
                    fc = blk * 4 + fl
                    for kc in range(8):
                        mm(ps[:, bank, fc * 17:(fc + 1) * 17], wv[:, kc, fl * 128:(fl + 1) * 128], sc[:, kc, :],
                           kc == 0, kc == 7, [('ws', sl), 'sc'], [('ps', bank)])
            tt(modr[:, sub, :, :], ps[:, bank, 0:408].rearrange("p (a b) -> p a b", a=24),
               adabs[:, sub, :].unsqueeze(2).to_broadcast([128, 24, 17]), ALU.add, [('ps', bank), K], [('modr', sub)])
            stt(Amod[:, sub], modr[:, sub, 8:16, :], 1.0, npre[:, sub, :].unsqueeze(2).to_broadcast([128, 8, 17]),
                ALU.add, ALU.mult, [('modr', sub), K], ['mods'])
            cp(Smod[:, sub], modr[:, sub, 0:8, :], [('modr', sub)], ['mods'])
            stt(Gmod[:, sub], modr[:, sub, 16:24, :], 1.0, npost[:, sub, :].unsqueeze(2).to_broadcast([128, 8, 17]),
                ALU.add, ALU.mult, [('modr', sub), K], ['mods'])
            if k3 != 1:
                ts(Gmod[:, sub], Gmod[:, sub], 0.5, None, ALU.mult, ALU.bypass, ['mods'], ['mods'])
        s.barrier()

        def rms_inv(src_fn, rkeys, n, scale):
            bank = nb()
            for c in range(8):
                q = n16()
                if True:
                    act(t16[:, q, :n], src_fn(c), AF.Square, rkeys(c), [('t16', q)])
                else:
                    tt(t16[:, q, :n], src_fn(c), src_fn(c), ALU.mult, rkeys(c), [('t16', q)], eng='pool')
                mm(ps[:, bank, :n], ones_bf[:], t16[:, q, :n], c == 0, c == 7, [K, ('t16', q)], [('ps', bank)])
            r = n32()
            iv = ninv()
            act(t32[:, r, :n], ps[:, bank, :n], AF.Ln, [('ps', bank), K], [('t32', r)], bias=epsc[:], scale=scale, force6=True)
            act(inv[:, iv, :n], t32[:, r, :n], AF.Exp, [('t32', r)], [('inv', iv)], scale=-0.5, force6=True)
            return iv

        def expand_mod(src, sub):
            r = n32()
            cp(t32[:, r, :].rearrange("p (c s i) -> p c s i", c=8, s=16),
               src[:, sub, :, 1:17].unsqueeze(3).to_broadcast([128, 8, 16, 4]), ['mods'], [('t32', r)])
            return r

        def prenorm_tile(sub, ti, tile):
            t0, n, kind = tile
            iv = rms_inv(lambda c: x[:, c, t0:t0 + n], lambda c: [('x', ti, c)], n, 1.0 / D)
            if kind == 'p':
                for c in range(8):
                    r = n32()
                    stt(t32[:, r, :n], x[:, c, t0:t0 + n], Amod[:, sub, c, 0:1], inv[:, iv, :n], ALU.mult, ALU.mult,
                        [('x', ti, c), 'mods', ('inv', iv)], [('t32', r)])
                    act(h[:, c, t0:t0 + n], t32[:, r, :n], AF.Identity, [('t32', r), 'mods'], [('h', ti, c)],
                        bias=Smod[:, sub, c, 0:1], scale=1.0)
            else:
                r = n32()
                v = t32[:, r, :].rearrange("p (c t) -> p c t", c=8)
                allx = [('x', ti, c) for c in range(8)]
                tt(v, x[:, :, t0:t0 + n], inv[:, iv, :n].unsqueeze(1).to_broadcast([128, 8, n]), ALU.mult,
                   allx + [('inv', iv)], [('t32', r)])
                ra = expand_mod(Amod, sub)
                tt(v, v, t32[:, ra, :].rearrange("p (c t) -> p c t", c=8), ALU.mult, [('t32', r), ('t32', ra)], [('t32', r)])
                rs = expand_mod(Smod, sub)
                tt(h[:, :, t0:t0 + n], v, t32[:, rs, :].rearrange("p (c t) -> p c t", c=8), ALU.add, [('t32', r), ('t32', rs)],
                   [('h', ti, c) for c in range(8)])

        def postnorm(sub, ti, tile, yb, ykey):
            t0, n, kind = tile
            iv = rms_inv(yb, lambda c: [ykey(c)], n, 1.0 / D)
            if kind == 'p':
                for c in range(8):
                    r = n32()
                    stt(t32[:, r, :n], yb(c), Gmod[:, sub, c, 0:1], inv[:, iv, :n], ALU.mult, ALU.mult,
                        [ykey(c), 'mods', ('inv', iv)], [('t32', r)])
                    tt(x[:, c, t0:t0 + n], x[:, c, t0:t0 + n], t32[:, r, :n], ALU.add, [('x', ti, c), ('t32', r)], [('x', ti, c)], eng='pool')
            else:
                rg = expand_mod(Gmod, sub)
                gv = t32[:, rg, :].rearrange("p (c t) -> p c t", c=8)
                n32.hold(rg)
                for c in range(8):
                    r = n32()
                    tt(t32[:, r, :n], yb(c), inv[:, iv, :n], ALU.mult, [ykey(c), ('inv', iv)], [('t32', r)])
                    tt(t32[:, r, :n], t32[:, r, :n], gv[:, c, :], ALU.mult, [('t32', r), ('t32', rg)], [('t32', r)])
                    tt(x[:, c, t0:t0 + n], x[:, c, t0:t0 + n], t32[:, r, :n], ALU.add, [('x', ti, c), ('t32', r)], [('x', ti, c)], eng='pool')
                n32.release(rg)

        def ffn(l, which, sub, tiles, hook):
            a_, o = av(0, [128, 22, TG], BF16)
            yb_, o = av(o, [128, 8, TG], F32)
            w_in = ffn_w_in[which][l]
            w_out = ffn_w_out[which][l]
            for blk in range(11):
                sl = nws()
                wv = wslot(sl, [128, 2, 8, 256])
                for gu in range(2):
                    c0 = gu * DFF + blk * 256
                    dma('pool', wv[:, gu], w_in[:, c0:c0 + 256].rearrange("(c p) f -> p c f", p=128), 'w%d' % sl,
                        writes=[('ws', sl)])
                for ti, (t0, n, kind) in enumerate(tiles):
                    for jj in range(2):
                        j = blk * 2 + jj
                        bg, bu = nb(), nb()
                        for gu, bk in ((0, bg), (1, bu)):
                            for kc in range(8):
                                mm(ps[:, bk, :n], wv[:, gu, kc, jj * 128:(jj + 1) * 128], h[:, kc, t0:t0 + n], kc == 0, kc == 7,
                                   [('ws', sl), ('h', ti, kc)], [('ps', bk)])
                        r = n32()
                        act(t32[:, r, :n], ps[:, bg, :n], AF.Silu, [('ps', bg)], [('t32', r)])
                        tt(a_[:, j, t0:t0 + n], t32[:, r, :n], ps[:, bu, :n], ALU.mult, [('t32', r), ('ps', bu)], [('a', j, ti)])
            jblocks = [(0, 4), (4, 4), (8, 4), (12, 4), (16, 4), (20, 2)]
            for bi, (j0, nj) in enumerate(jblocks):
                sl = nws()
                wv = wslot(sl, [128, 4, 1024])
                dma('pool', wv[:, 0:nj, :], w_out[j0 * 128:(j0 + nj) * 128, :].rearrange("(j p) f -> p j f", p=128), 'w%d' % sl,
                    writes=[('ws', sl)])
                for ti, (t0, n, kind) in enumerate(tiles):
                    for dc in range(8):
                        bank = nb()
                        for jl in range(nj):
                            mm(ps[:, bank, :n], wv[:, jl, dc * 128:(dc + 1) * 128], a_[:, j0 + jl, t0:t0 + n], jl == 0, jl == nj - 1,
                               [('ws', sl), ('a', j0 + jl, ti)], [('ps', bank)])
                        if bi == 0:
                            cp(yb_[:, dc, t0:t0 + n], ps[:, bank, :n], [('ps', bank)], [('yb', ti, dc)], eng='act')
                        else:
                            tt(yb_[:, dc, t0:t0 + n], yb_[:, dc, t0:t0 + n], ps[:, bank, :n], ALU.add, [('yb', ti, dc), ('ps', bank)], [('yb', ti, dc)])
            for ti, tile in enumerate(tiles):
                t0, n, kind = tile
                postnorm(sub, ti, tile, lambda c: yb_[:, c, t0:t0 + n], lambda c: ('yb', ti, c))
                hook(ti, tile)

        def outproj(sub, tiles, w_dram_chunk, src_fn, src_keys, ybt, hook):
            sls = []
            for half in range(2):
                sl = nws()
                wv = wslot(sl, [128, 8, 512])
                for (dst, ap) in w_dram_chunk(wv, half):
                    dma('pool', dst, ap, 'w%d' % sl, writes=[('ws', sl)])
                sls.append((sl, wv))
            for ti, tile in enumerate(tiles):
                t0, n, kind = tile
                for half in range(2):
                    sl, wv = sls[half]
                    for dcl in range(4):
                        dc = half * 4 + dcl
                        bank = nb()
                        for kc in range(8):
                            mm(ps[:, bank, :n], wv[:, kc, dcl * 128:(dcl + 1) * 128], src_fn(kc, t0, n), kc == 0, kc == 7,
                               [('ws', sl)] + src_keys(kc, ti), [('ps', bank)])
                        cp(ybt[:, dc, :n], ps[:, bank, :n], [('ps', bank)], [('ybt', dc)], eng='act')
                postnorm(sub, ti, tile, lambda c: ybt[:, c, :n], lambda c: ('ybt', c))
                hook(ti, tile)

        def even_mixer(sub, g, tiles, hook):
            has_s = (g == 0)
            o = 0
            oa, o = av(o, [128, 4, TG], BF16)
            ob, o = av(o, [128, 4, TG], BF16)
            oP = o
            qr, o = av(oP, [128, 4, TG], BF16)
            kr, o = av(o, [128, 1216], BF16)
            krf, o = av(o, [128, 192], F32)
            vtok, o = av(o, [128, 10, 128], BF16)
            vwf, o = av(o, [128, 2, 128], F32)
            cosT, o = av(o, [128, TG], F32)
            sinT, o = av(o, [128, TG], F32)
            kcache, o = av(o, [128, 2, 128], F32)
            vcache, o = av(o, [128, 2, 128], F32)
            KT, o = av(o, [128, 2, 128], BF16)
            Vb, o = av(o, [128, 16, 128], BF16)
            ktr, o = av(o, [64, 128], F32)
            gl, o = av(oP, [128, 4, TG], BF16)
            xr, o = av(o, [128, 4, 3 + 1024], F32)
            xrs, o = av(o, [128, 4, 16, 7], F32)
            xc, o = av(o, [128, TG], F32)
            xcb, o = av(o, [128, TG], BF16)
            ac, o = av(o, [128, TG], F32)
            bc, o = av(o, [128, TG], F32)
            hsb, o = av(o, [128, TG], F32)
            h0s, o = av(o, [128, 4, 16], F32)
            hsS, o = av(o, [128, 4, 16], F32)
            ybt, _ = av(oP, [128, 8, 512], F32)

            dma('sp', cosT[:, 0:1024], ropec_d[:, g * 1024:(g + 1) * 1024], 'l_rope', writes=['rope'])
            dma('sp', sinT[:, 0:1024], ropes_d[:, g * 1024:(g + 1) * 1024], 'l_rope', writes=['rope'])
            if has_s:
                dma('sp', cosT[:, 1024:1088], ropec_d[:, NP:NP + NS], 'l_rope', writes=['rope'])
                dma('sp', sinT[:, 1024:1088], ropes_d[:, NP:NP + NS], 'l_rope', writes=['rope'])
            W = even_w_in
            Wv_ = W.rearrange("(c p) f -> p c f", p=128)
            slq, slqs, slk, slst = nws(), nws(), nws(), nws()
            wq = wslot(slq, [128, 8, 512])
            wqs = wslot(slqs, [128, 8, 512])
            wk = wslot(slk, [128, 8, 384])
            wstg = wslot(slst, [128, 8, 512])
            dma('pool', wstg, Wv_[:, :, 0:512], 'w%d' % slst, writes=[('ws', slst)])
            dma('pool', wk[:, :, 0:128], Wv_[:, :, 512:640], 'w%d' % slk, writes=[('ws', slk)])
            dma('pool', wk[:, :, 256:384], Wv_[:, :, 640:768], 'w%d' % slk, writes=[('ws', slk)])
            src5 = wstg.rearrange("p c (hf j d) -> p c hf j d", hf=2, j=4)
            dq5 = wq.rearrange("p c (j hf d) -> p c j hf d", j=4, hf=2)
            dqs5 = wqs.rearrange("p c (j hf d) -> p c j hf d", j=4, hf=2)
            for hf in range(2):
                cp(dq5[:, :, :, hf, :], src5[:, :, hf, :, :], [('ws', slst)], [('ws', slq)], eng='pool')
                cp(dqs5[:, :, :, hf, 0:8], src5[:, :, hf, :, 8:16], [('ws', slst)], [('ws', slqs)], eng='pool')
                cp(dqs5[:, :, :, hf, 8:16], src5[:, :, hf, :, 0:8], [('ws', slst)], [('ws', slqs)], eng='pool')
                cp(dqs5[:, :, :, hf, 16:64], src5[:, :, hf, :, 16:64], [('ws', slst)], [('ws', slqs)], eng='pool')
            ks4 = wk[:, :, 0:128].rearrange("p c (kv d) -> p c kv d", kv=2)
            kd4 = wk[:, :, 128:256].rearrange("p c (kv d) -> p c kv d", kv=2)
            cp(kd4[:, :, :, 0:8], ks4[:, :, :, 8:16], [('ws', slk)], [('ws', slk)], eng='pool')
            cp(kd4[:, :, :, 8:16], ks4[:, :, :, 0:8], [('ws', slk)], [('ws', slk)], eng='pool')
            cp(kd4[:, :, :, 16:64], ks4[:, :, :, 16:64], [('ws', slk)], [('ws', slk)], eng='pool')

            def kcol(t0):
                return 128 + t0
            for ti, (t0, n, kind) in enumerate(tiles):
                hk = [('h', ti, kc) for kc in range(8)]
                for j in range(4):
                    b1, b2 = nb(), nb()
                    for kc in range(8):
                        mm(ps[:, b1, :n], wq[:, kc, j * 128:(j + 1) * 128], h[:, kc, t0:t0 + n], kc == 0, kc == 7,
                           [('ws', slq), ('h', ti, kc)], [('ps', b1)])
                    for kc in range(8):
                        mm(ps[:, b2, :n], wqs[:, kc, j * 128:(j + 1) * 128], h[:, kc, t0:t0 + n], kc == 0, kc == 7,
                           [('ws', slqs), ('h', ti, kc)], [('ps', b2)])
                    r1, r2 = n32(), n32()
                    tt(t32[:, r1, :n], ps[:, b1, :n], cosT[:, t0:t0 + n], ALU.mult, [('ps', b1), 'rope'], [('t32', r1)])
                    tt(t32[:, r2, :n], ps[:, b2, :n], sinT[:, t0:t0 + n], ALU.mult, [('ps', b2), 'rope'], [('t32', r2)])
                    tt(qr[:, j, t0:t0 + n], t32[:, r1, :n], t32[:, r2, :n], ALU.add, [('t32', r1), ('t32', r2)], [('qr', ti, j)])
                b1, b2 = nb(), nb()
                for kc in range(8):
                    mm(ps[:, b1, :n], wk[:, kc, 0:128], h[:, kc, t0:t0 + n], kc == 0, kc == 7, [('ws', slk), ('h', ti, kc)], [('ps', b1)])
                for kc in range(8):
                    mm(ps[:, b2, :n], wk[:, kc, 128:256], h[:, kc, t0:t0 + n], kc == 0, kc == 7, [('ws', slk), ('h', ti, kc)], [('ps', b2)])
                r1, r2 = n32(), n32()
                tt(t32[:, r1, :n], ps[:, b1, :n], cosT[:, t0:t0 + n], ALU.mult, [('ps', b1), 'rope'], [('t32', r1)])
                tt(t32[:, r2, :n], ps[:, b2, :n], sinT[:, t0:t0 + n], ALU.mult, [('ps', b2), 'rope'], [('t32', r2)])
                tt(kr[:, kcol(t0):kcol(t0) + n], t32[:, r1, :n], t32[:, r2, :n], ALU.add, [('t32', r1), ('t32', r2)], [('kr', ti)])
                if kind == 's':
                    tt(krf[:, 128:192], t32[:, r1, :n], t32[:, r2, :n], ALU.add, [('t32', r1), ('t32', r2)], ['krf_s'])
                elif g == 1 and ti == 1:
                    tt(krf[:, 0:128], t32[:, r1, 384:512], t32[:, r2, 384:512], ALU.add, [('t32', r1), ('t32', r2)], ['krf_p'])
                nblk = (n + 127) // 128
                for bl in range(nblk):
                    nt = min(128, n - bl * 128)
                    blk = (t0 // 128 + bl) if kind == 'p' else 8
                    bank = nb()
                    for kc in range(8):
                        mm(ps[:nt, bank, 0:128], h[:, kc, t0 + bl * 128:t0 + bl * 128 + nt], wk[:, kc, 256:384], kc == 0, kc == 7,
                           [('ws', slk), ('h', ti, kc)], [('ps', bank)])
                    cp(vtok[:nt, 1 + blk, :], ps[:nt, bank, 0:128], [('ps', bank)], [('vtok', 1 + blk)], eng='act')
                    if kind == 's':
                        cp(vwf[:nt, 1, :], ps[:nt, bank, 0:128], [('ps', bank)], ['vwf_s'], eng='act')
                    elif g == 1 and blk == 7:
                        cp(vwf[:, 0, :], ps[:, bank, 0:128], [('ps', bank)], ['vwf_p'], eng='act')
            if estop <= 1:
                return
            anorm = (int(estop * 100 + 0.5) % 10) != 5 and estop >= 2
            alvl = (int(estop * 100 + 0.5) % 10) if estop < 2 else 9
            if g == 1:
                cp(kr[:, 0:128], kcar[:], ['kcar'], [('krc',)])
                cp(vtok[:, 0, :], vcar[:], ['vcar'], [('vtok', 0)])
            def tile_of(col):
                return col // 512
            for b in range(8 if estop >= 2 else int((estop - 1) * 10 + 0.5)):
                has_prev = not (g == 0 and b == 0)
                tq = tile_of(b * 128)
                kkeys = [('kr', tile_of(b * 128))]
                if has_prev:
                    kkeys.append(('kr', tile_of((b - 1) * 128)) if b > 0 else ('krc',))
                bo, bd = nb(), nb()
                nb.hold(bo, bd)
                for jp in range(2):
                    banks = [nb(), nb()]
                    for hf in range(2):
                        p0 = hf * 64
                        for jl in range(2):
                            j = jp * 2 + jl
                            qa = qr[p0:p0 + 64, j, b * 128:(b + 1) * 128]
                            mm(ps[:, banks[hf], (jl * 2) * 128:(jl * 2 + 1) * 128], kr[p0:p0 + 64, 128 * (1 + b):128 * (2 + b)], qa, True, True,
                               kkeys + [('qr', tq, j)], [('ps', banks[hf])])
                            if has_prev:
                                mm(ps[:, banks[hf], (jl * 2 + 1) * 128:(jl * 2 + 2) * 128], kr[p0:p0 + 64, 128 * b:128 * (1 + b)], qa, True, True,
                                   kkeys + [('qr', tq, j)], [('ps', banks[hf])])
                    for hf in range(2):
                        p0 = hf * 64
                        bank = banks[hf]
                        q = n16()
                        if alvl < 1:
                            continue
                        if has_prev:
                            act(t16[:, q, :], ps[:, bank, :], AF.Exp, [('ps', bank)], [('t16', q)], scale=0.125)
                            ev = t16[:, q, :].rearrange("p (a b) -> p a b", a=2)
                            tt(ev, ev, amask[:].unsqueeze(1).to_broadcast([128, 2, 256]), ALU.mult, [('t16', q), K], [('t16', q)])
                        else:
                            ev4 = t16[:, q, :].rearrange("p (a b c) -> p a b c", a=2, b=2)
                            pv4 = ps[:, bank, :].rearrange("p (a b c) -> p a b c", a=2, b=2)
                            act(ev4[:, :, 0, :], pv4[:, :, 0, :], AF.Exp, [('ps', bank)], [('t16', q)], scale=0.125)
                            tt(ev4[:, :, 0, :], ev4[:, :, 0, :], amask[:, 0:128].unsqueeze(1).to_broadcast([128, 2, 128]), ALU.mult,
                               [('t16', q), K], [('t16', q)])
                        for jl in range(2 if alvl >= 2 else 0):
                            j = jp * 2 + jl
                            ed = t16[:, q, (jl * 2) * 128:(jl * 2 + 1) * 128]
                            ep = t16[:, q, (jl * 2 + 1) * 128:(jl * 2 + 2) * 128]
                            mm(ps[p0:p0 + 64, bo, j * 128:(j + 1) * 128], vtok[:, 1 + b, p0:p0 + 64], ed, True, not has_prev,
                               [('vtok', 1 + b), ('t16', q)], [('ps', bo)])
                            if has_prev:
                                mm(ps[p0:p0 + 64, bo, j * 128:(j + 1) * 128], vtok[:, b, p0:p0 + 64], ep, False, True,
                                   [('vtok', b), ('t16', q)], [('ps', bo)])
                            if alvl < 3:
                                continue
                            mm(ps[p0:p0 + 64, bd, j * 128:(j + 1) * 128], ones_bf[:, 0:64], ed, True, not has_prev, [K, ('t16', q)], [('ps', bd)])
                            if has_prev:
                                mm(ps[p0:p0 + 64, bd, j * 128:(j + 1) * 128], ones_bf[:, 0:64], ep, False, True, [K, ('t16', q)], [('ps', bd)])
                nb.release(bo, bd)
                if not anorm:
                    continue
                r = n32()
                rv = t32[:, r, :].rearrange("p (a b) -> p a b", a=4)
                tt(rv, ps[:, bd, :].rearrange("p (a b) -> p a b", a=4), sinkexp[:].unsqueeze(2).to_broadcast([128, 4, 128]), ALU.add,
                   [('ps', bd), 'p_sink'], [('t32', r)])
                r2 = n32()
                recip(t32[:, r2, :], t32[:, r, :], [('t32', r)], [('t32', r2)])
                tt(oa[:, :, b * 128:(b + 1) * 128], ps[:, bo, :].rearrange("p (a b) -> p a b", a=4),
                   t32[:, r2, :].rearrange("p (a b) -> p a b", a=4), ALU.mult, [('ps', bo), ('t32', r2)], [('oa', b // 4)])
            if g == 0:
                cp(kcar[:], kr[:, 128 * 8:128 * 9], [('kr', 1)], ['kcar'])
                cp(vcar[:], vtok[:, 8, :], [('vtok', 8)], ['vcar'])
            else:
                dma('sp', kwp, krf[:, 0:128], 'o_kwp', reads=['krf_p'])
                dma('sp', vwp, vwf[:, 0, :], 'o_vwp', reads=['vwf_p'])
            if estop <= 2:
                return
            if has_s:
                TS = 1024
                dma('sp', kws[:, 0:124, :], cachek[:, 4:128, :], 'o_kws')
                dma('sp', vws[:, 0:124, :], cachev[:, 4:128, :], 'o_vws')
                bsc = [nb(), nb()]
                nb.hold(*bsc)
                for sq in range(16):
                    rr = sq % 2
                    dma('sp', kcache[:, rr, :], cachek[sq], 'l_kc%d' % rr, writes=[('kcache', rr)])
                    dma('sp', vcache[:, rr, :], cachev[sq], 'l_vc%d' % rr, writes=[('vcache', rr)])
                    bt = nb()
                    tr(ps[:, bt, 0:128], kcache[:, rr, :], ident_f[:], [('kcache', rr), K], [('ps', bt)])
                    cp(KT[:, rr, :], ps[:, bt, 0:128], [('ps', bt)], [('KT', rr)], eng='act')
                    cp(Vb[:, sq, :], vcache[:, rr, :], [('vcache', rr)], [('Vb', sq)], eng='dve')
                    for hf in range(2):
                        p0 = hf * 64
                        for j in range(4):
                            c0 = (sq * 4 + j) * 4
                            mm(ps[:, bsc[hf], c0:c0 + 4], KT[p0:p0 + 64, rr, :], qr[p0:p0 + 64, j, TS + sq * 4:TS + sq * 4 + 4], True, True,
                               [('KT', rr), ('qr', 2, j)], [('ps', bsc[hf])])
                qc = [n16(), n16()]
                for hf in range(2):
                    act(t16[:, qc[hf], 0:256], ps[:, bsc[hf], 0:256], AF.Exp, [('ps', bsc[hf])], [('t16', qc[hf])], scale=0.125)
                    ecv = t16[:, qc[hf], 0:256].rearrange("p (a b) -> p a b", b=4)
                    tt(ecv, ecv, mcm[:].unsqueeze(1).to_broadcast([128, 64, 4]), ALU.mult, [('t16', qc[hf]), K], [('t16', qc[hf])])
                nb.release(*bsc)
                bn = [nb(), nb()]
                for hf in range(2):
                    p0 = hf * 64
                    for j in range(4):
                        mm(ps[0:64, bn[hf], j * 64:(j + 1) * 64], kr[p0:p0 + 64, 128 + TS:128 + TS + 64], qr[p0:p0 + 64, j, TS:TS + 64], True, True,
                           [('kr', 2), ('qr', 2, j)], [('ps', bn[hf])])
                qn = [n16(), n16()]
                for hf in range(2):
                    act(t16[0:64, qn[hf], 0:256], ps[0:64, bn[hf], 0:256], AF.Exp, [('ps', bn[hf])], [('t16', qn[hf])], scale=0.125)
                    env = t16[0:64, qn[hf], 0:256].rearrange("p (a b) -> p a b", a=4)
                    tt(env, env, smask[:].unsqueeze(1).to_broadcast([64, 4, 64]), ALU.mult, [('t16', qn[hf]), K], [('t16', qn[hf])])
                bo, bd = nb(), nb()
                for j in range(4):
                    for hf in range(2):
                        p0 = hf * 64
                        en_ = t16[0:64, qn[hf], j * 64:(j + 1) * 64]
                        mm(ps[p0:p0 + 64, bo, j * 64:(j + 1) * 64], vtok[0:64, 9, p0:p0 + 64], en_, True, False,
                           [('vtok', 9), ('t16', qn[hf])], [('ps', bo)], skip_group_check=True)
                        mm(ps[p0:p0 + 64, bd, j * 64:(j + 1) * 64], ones_bf[0:64, 0:64], en_, True, False, [K, ('t16', qn[hf])], [('ps', bd)],
                           skip_group_check=True)
                        for sq in range(16):
                            c1 = (sq * 4 + j) * 4
                            ec_ = t16[:, qc[hf], c1:c1 + 4]
                            mm(ps[p0:p0 + 64, bo, j * 64 + sq * 4:j * 64 + sq * 4 + 4], Vb[:, sq, p0:p0 + 64], ec_, False, True,
                               [('Vb', sq), ('t16', qc[hf])], [('ps', bo)], skip_group_check=True)
                            mm(ps[p0:p0 + 64, bd, j * 64 + sq * 4:j * 64 + sq * 4 + 4], ones_bf[:, 0:64], ec_, False, True,
                               [K, ('t16', qc[hf])], [('ps', bd)], skip_group_check=True)
                r = n32()
                rv = t32[:, r, 0:256].rearrange("p (a b) -> p a b", a=4)
                tt(rv, ps[:, bd, 0:256].rearrange("p (a b) -> p a b", a=4), sinkexp[:].unsqueeze(2).to_broadcast([128, 4, 64]), ALU.add,
                   [('ps', bd), 'p_sink'], [('t32', r)])
                r2 = n32()
                recip(t32[:, r2, 0:256], t32[:, r, 0:256], [('t32', r)], [('t32', r2)])
                tt(oa[:, :, TS:TS + 64], ps[:, bo, 0:256].rearrange("p (a b) -> p a b", a=4),
                   t32[:, r2, 0:256].rearrange("p (a b) -> p a b", a=4), ALU.mult, [('ps', bo), ('t32', r2)], [('oa', 2)])
                bt = nb()
                tr(ps[0:64, bt, 0:128], krf[:, 128:192], ident_f[:], ['krf_s', K], [('ps', bt)])
                cp(ktr[:], ps[0:64, bt, 0:128], [('ps', bt)], ['ktr'], eng='act')
                for sq in range(16):
                    dma('sp', kws[sq, 124:128, :], ktr[sq * 4:sq * 4 + 4, :], 'o_kws2', reads=['ktr'])
                    dma('sp', vws[sq, 124:128, :], vwf[sq * 4:sq * 4 + 4, 1, :], 'o_vws2', reads=['vwf_s'])

            if estop <= 3:
                return
            s.barrier()
            slg, slr = nws(), nws()
            wg = wslot(slg, [128, 8, 512])
            wr = wslot(slr, [128, 8, 512])
            dma('pool', wg, Wv_[:, :, 768:1280], 'w%d' % slg, writes=[('ws', slg)])
            dma('pool', wr, Wv_[:, :, 1280:1792], 'w%d' % slr, writes=[('ws', slr)])
            if g == 0:
                memset(xr[:, :, 0:3], 0.0, ['xr_c'])
                dma('sp', xrs[:, :, :, 0:3], convS, 'l_cs1', writes=['xrs_c'])
                dma('sp', h0s, h0S, 'l_cs2', writes=['h0s'])
            else:
                cp(xr[:, :, 0:3], convc[:], ['convc'], ['xr_c'])
            for ti, (t0, n, kind) in enumerate(tiles):
                for j in range(4):
                    bank = nb()
                    for kc in range(8):
                        mm(ps[:, bank, :n], wg[:, kc, j * 128:(j + 1) * 128], h[:, kc, t0:t0 + n], kc == 0, kc == 7,
                           [('ws', slg), ('h', ti, kc)], [('ps', bank)])
                    r1, r2 = n32(), n32()
                    act(t32[:, r1, :n], ps[:, bank, :n], AF.Square, [('ps', bank)], [('t32', r1)])
                    ts(t32[:, r1, :n], t32[:, r1, :n], 0.044715, 1.0, ALU.mult, ALU.add, [('t32', r1)], [('t32', r1)])
                    tt(t32[:, r2, :n], t32[:, r1, :n], ps[:, bank, :n], ALU.mult, [('t32', r1), ('ps', bank)], [('t32', r2)])
                    act(t32[:, r1, :n], t32[:, r2, :n], AF.Tanh, [('t32', r2)], [('t32', r1)], scale=0.7978845608028654)
                    stt(gl[:, j, t0:t0 + n], t32[:, r1, :n], 1.0, ps[:, bank, :n], ALU.add, ALU.mult, [('t32', r1), ('ps', bank)], [('gl', ti, j)])
                    bank = nb()
                    for kc in range(8):
                        mm(ps[:, bank, :n], wr[:, kc, j * 128:(j + 1) * 128], h[:, kc, t0:t0 + n], kc == 0, kc == 7,
                           [('ws', slr), ('h', ti, kc)], [('ps', bank)])
                    if kind == 'p':
                        cp(xr[:, j, 3 + t0:3 + t0 + n], ps[:, bank, :n], [('ps', bank)], [('xr', j, ti)], eng='act')
                    else:
                        cp(xrs[:, j, :, 3:7], ps[:, bank, 0:64].rearrange("p (a b) -> p a b", b=4), [('ps', bank)], [('xr', j, ti)], eng='act')
            nt_ = len(tiles)
            for c in range(4):
                xk = [('xr', c, ti) for ti in range(nt_)] + ['xr_c', 'xrs_c']
                ts(xc[:, 0:1024], xr[:, c, 3:1027], cw[:, c, 3:4], cb[:, c:c + 1], ALU.mult, ALU.add, xk + [K], ['xc'])
                for jj in range(3):
                    stt(xc[:, 0:1024], xr[:, c, jj:jj + 1024], cw[:, c, jj:jj + 1], xc[:, 0:1024], ALU.mult, ALU.add, xk + [K, 'xc'], ['xc'])
                if has_s:
                    xcs = xc[:, 1024:1088].rearrange("p (a b) -> p a b", b=4)
                    ts(xcs, xrs[:, c, :, 3:7], cw[:, c, 3:4], cb[:, c:c + 1], ALU.mult, ALU.add, xk + [K], ['xcs'])
                    for jj in range(3):
                        stt(xcs, xrs[:, c, :, jj:jj + 4], cw[:, c, jj:jj + 1], xcs, ALU.mult, ALU.add, xk + [K, 'xcs'], ['xcs'])
                ntot = 1088 if has_s else 1024
                cp(xcb[:, 0:ntot], xc[:, 0:ntot], ['xc', 'xcs'], ['xcb'], eng='act')
                for ti, (t0, n, kind) in enumerate(tiles):
                    b1, b2 = nb(), nb()
                    mm(ps[:, b1, :n], BDa[:, c, :], xcb[:, t0:t0 + n], True, True, ['BD', 'xcb'], [('ps', b1)])
                    mm(ps[:, b2, :n], BDx[:, c, :], xcb[:, t0:t0 + n], True, True, ['BD', 'xcb'], [('ps', b2)])
                    r1, r2, r3 = n32(), n32(), n32()
                    act(t32[:, r1, :n], ps[:, b1, :n], AF.Tanh, [('ps', b1), 'p_hba'], [('t32', r1)], bias=hba[:, c:c + 1], scale=0.5)
                    act(ac[:, t0:t0 + n], t32[:, r1, :n], AF.Exp, [('t32', r1), 'p_hc'], [('ac', ti)], bias=hcoef[:, c:c + 1], scale=hcoef[:, c:c + 1])
                    act(t32[:, r2, :n], ps[:, b2, :n], AF.Tanh, [('ps', b2), 'p_hbx'], [('t32', r2)], bias=hbx[:, c:c + 1], scale=0.5)
                    tt(t32[:, r1, :n], ac[:, t0:t0 + n], ac[:, t0:t0 + n], ALU.mult, [('ac', ti), ('t32', r1)], [('t32', r1)])
                    ts(t32[:, r1, :n], t32[:, r1, :n], -1.0, 1.0, ALU.mult, ALU.add, [('t32', r1)], [('t32', r1)])
                    ts(t32[:, r1, :n], t32[:, r1, :n], 0.0, None, ALU.max, ALU.bypass, [('t32', r1)], [('t32', r1)])
                    act(t32[:, r3, :n], t32[:, r1, :n], AF.Sqrt, [('t32', r1)], [('t32', r3)])
                    stt(t32[:, r2, :n], t32[:, r2, :n], 1.0, xc[:, t0:t0 + n], ALU.add, ALU.mult, [('t32', r2), 'xc', 'xcs'], [('t32', r2)])
                    stt(bc[:, t0:t0 + n], t32[:, r2, :n], 0.5, t32[:, r3, :n], ALU.mult, ALU.mult, [('t32', r2), ('t32', r3)], [('bc', ti)])
                allab = [('ac', ti) for ti in range(nt_)] + [('bc', ti) for ti in range(nt_)]
                if has_s:
                    a0 = ac[:, 1024:1088].rearrange("p (a b) -> p a b", b=4)[:, :, 0]
                    b0 = bc[:, 1024:1088].rearrange("p (a b) -> p a b", b=4)[:, :, 0]
                    r = n32()
                    tt(t32[:, r, 0:16], a0, h0s[:, c, :], ALU.mult, allab + ['h0s'], [('t32', r)])
                    tt(b0, b0, t32[:, r, 0:16], ALU.add, allab + [('t32', r)], [('bc', 2)])
                    memset(a0, 0.0, [('ac', 2)])
                init = 0.0 if g == 0 else hcar[:, c:c + 1]
                scan(hsb[:, 0:1024], ac[:, 0:1024], bc[:, 0:1024], init, allab + ['hcar'], ['hsb'])
                if has_s:
                    scan(hsb[:, 1024:1088], ac[:, 1024:1088], bc[:, 1024:1088], 0.0, allab, ['hsbs'])
                stt(ob[:, c, 0:ntot], gl[:, c, 0:ntot], 0.5, hsb[:, 0:ntot], ALU.mult, ALU.mult,
                    [('gl', ti, c) for ti in range(nt_)] + ['hsb', 'hsbs'], [('ob', c)])
                cp(hcar[:, c:c + 1], hsb[:, 1023:1024], ['hsb'], ['hcar'])
                if has_s:
                    cp(hsS[:, c, :], hsb[:, 1024:1088].rearrange("p (a b) -> p a b", b=4)[:, :, 3], ['hsbs'], ['hsS'])
            cp(convc[:], xr[:, :, 1024:1027], [('xr', c, 1) for c in range(4)], ['convc'])
            if g == 1:
                dma('sp', hp, hcar[:], 'o_hp', reads=['hcar'])
                dma('sp', convp, convc[:], 'o_cp', reads=['convc'])
            if has_s:
                dma('sp', hs_o, hsS, 'o_hs', reads=['hsS'])
                dma('sp', convs, xrs[:, :, :, 4:7], 'o_cs', reads=[('xr', c, 2) for c in range(4)])
            if estop <= 4:
                return
            s.barrier()
            Wo = even_w_out

            def wchunk(wv, half):
                cs = slice(half * 512, (half + 1) * 512)
                out = []
                for hf in range(2):
                    out.append((wv[hf * 64:(hf + 1) * 64, 0:4, :],
                                Wo[hf * 256:(hf + 1) * 256, cs].rearrange("(j d) f -> d j f", j=4)))
                out.append((wv[:, 4:8, :], Wo[512:1024, cs].rearrange("(c p) f -> p c f", p=128)))
                return out

            def src(kc, t0, n):
                return oa[:, kc, t0:t0 + n] if kc < 4 else ob[:, kc - 4, t0:t0 + n]

            def srck(kc, ti):
                return [('oa', ti)] if kc < 4 else [('ob', kc - 4)]
            outproj(sub, tiles, wchunk, src, srck, ybt, hook)

        def odd_mixer(sub, g, tiles, hook):
            has_s = (g == 0)
            o = 0
            qe, o = av(o, [128, 8, TG], BF16)
            ybt, _ = av(0, [128, 8, 512], F32)
            ke, o = av(o, [128, 8, TG], BF16)
            sg, o = av(o, [128, 8, TG], BF16)
            vtk, o = av(o, [128, 9, 1024], BF16)
            keT, o = av(o, [128, 2, 1024], BF16)
            Sbf, o = av(o, [128, 8, 128], BF16)
            tmpS, o = av(o, [128, 8, 128], F32)
            ebl, o = av(o, [128, 8, 32], F32)
            wsc, o = av(o, [128, 8, 32], F32)
            ebr, o = av(o, [128, 8, 32], F32)
            ebls, o = av(o, [128, 8, 16], F32)
            zT = h
            W = odd_w_in
            Wv_ = W.rearrange("(c p) f -> p c f", p=128)
            for hd in range(8):
                sl = nws()
                wv = wslot(sl, [128, 2, 8, 128])
                dma('pool', wv[:, 0], Wv_[:, :, hd * 128:(hd + 1) * 128], 'w%d' % sl, writes=[('ws', sl)])
                dma('pool', wv[:, 1], Wv_[:, :, 1024 + hd * 128:1024 + (hd + 1) * 128], 'w%d' % sl, writes=[('ws', sl)])
                for ti, (t0, n, kind) in enumerate(tiles):
                    bq, bf = nb(), nb()
                    for kc in range(8):
                        mm(ps[:, bq, :n], wv[:, 0, kc, :], h[:, kc, t0:t0 + n], kc == 0, kc == 7, [('ws', sl), ('h', ti, kc)], [('ps', bq)])
                    for kc in range(8):
                        mm(ps[:, bf, :n], wv[:, 1, kc, :], h[:, kc, t0:t0 + n], kc == 0, kc == 7, [('ws', sl), ('h', ti, kc)], [('ps', bf)])
                    rA, rB, rC, rD = n32(), n32(), n32(), n32()
                    A_, B_, C_, D_ = t32[:, rA, :n], t32[:, rB, :n], t32[:, rC, :n], t32[:, rD, :n]
                    act(A_, ps[:, bf, :n], AF.Exp, [('ps', bf)], [('t32', rA)], scale=-1.0)
                    act(C_, A_, AF.Ln, [('t32', rA), K], [('t32', rC)], bias=onec[:], scale=1.0)
                    act(D_, A_, AF.Ln, [('t32', rA), K, 'p_lbv'], [('t32', rD)], bias=onec[:], scale=lbv[:, hd:hd + 1])
                    act(B_, C_, AF.Exp, [('t32', rC)], [('t32', rB)], scale=-1.0)
                    stt(B_, A_, omlb[:, hd:hd + 1], B_, ALU.mult, ALU.mult, [('t32', rA), ('t32', rB), 'p_lbv'], [('t32', rB)])
                    tt(C_, D_, C_, ALU.subtract, [('t32', rD), ('t32', rC)], [('t32', rC)])
                    if kind == 'p':
                        scan(A_, rmask[:, 0:n], C_, 0.0, [K, ('t32', rC), ('t32', rA)], [('t32', rA)])
                        nch = n // 32
                        ch0 = t0 // 32
                        bv = A_.rearrange("p (c l) -> p c l", l=32)
                        tt(D_.rearrange("p (c l) -> p c l", l=32), bv, bv[:, :, 15].unsqueeze(2).to_broadcast([128, nch, 32]), ALU.subtract,
                           [('t32', rA)], [('t32', rD)])
                        act(ebl[:, hd, ch0:ch0 + nch], bv[:, :, 31], AF.Exp, [('t32', rA)], [('ebl', hd, ti)])
                        act(ebr[:, hd, ch0:ch0 + nch], bv[:, :, 15], AF.Exp, [('t32', rA)], [('ebr', hd, ti)])
                        act(wsc[:, hd, ch0:ch0 + nch], D_.rearrange("p (c l) -> p c l", l=32)[:, :, 31], AF.Exp, [('t32', rD)], [('wsc', hd, ti)])
                        dsrc, dk = D_, ('t32', rD)
                    else:
                        scan(A_, rmask[:, 512:512 + n], C_, 0.0, [K, ('t32', rC), ('t32', rA)], [('t32', rA)])
                        act(ebls[:, hd, :], A_.rearrange("p (c l) -> p c l", l=4)[:, :, 3], AF.Exp, [('t32', rA)], [('ebls', hd)])
                        dsrc, dk = A_, ('t32', rA)
                    act(C_, dsrc, AF.Exp, [dk, ('t32', rC)], [('t32', rC)])
                    tt(qe[:, hd, t0:t0 + n], ps[:, bq, :n], C_, ALU.mult, [('ps', bq), ('t32', rC)], [('qe', hd, ti)])
                    act(C_, dsrc, AF.Exp, [dk, ('t32', rC)], [('t32', rC)], scale=-1.0)
                    tt(ke[:, hd, t0:t0 + n], B_, C_, ALU.mult, [('t32', rB), ('t32', rC)], [('ke', hd, ti)])
            for half in range(2):
                sl = nws()
                wv = wslot(sl, [128, 8, 512])
                dma('pool', wv, Wv_[:, :, 2048 + half * 512:2048 + (half + 1) * 512], 'w%d' % sl, writes=[('ws', sl)])
                for ti, (t0, n, kind) in enumerate(tiles):
                    nblk = (n + 127) // 128
                    for bl in range(nblk):
                        nt = min(128, n - bl * 128)
                        blk = (t0 // 128 + bl) if kind == 'p' else 8
                        bank = nb()
                        for kc in range(8):
                            mm(ps[:nt, bank, :], h[:, kc, t0 + bl * 128:t0 + bl * 128 + nt], wv[:, kc, :], kc == 0, kc == 7,
                               [('ws', sl), ('h', ti, kc)], [('ps', bank)])
                        cp(vtk[:nt, blk, half * 512:(half + 1) * 512], ps[:nt, bank, :], [('ps', bank)], [('vtk', blk, half)], eng='act')
            for half in range(2):
                sl = nws()
                wv = wslot(sl, [128, 8, 512])
                dma('pool', wv, Wv_[:, :, 3072 + half * 512:3072 + (half + 1) * 512], 'w%d' % sl, writes=[('ws', sl)])
                for ti, (t0, n, kind) in enumerate(tiles):
                    for hl in range(4):
                        hd = half * 4 + hl
                        bank = nb()
                        for kc in range(8):
                            mm(ps[:, bank, :n], wv[:, kc, hl * 128:(hl + 1) * 128], h[:, kc, t0:t0 + n], kc == 0, kc == 7,
                               [('ws', sl), ('h', ti, kc)], [('ps', bank)])
                        act(sg[:, hd, t0:t0 + n], ps[:, bank, :n], AF.Silu, [('ps', bank)], [('sg', hd, ti)])
            s.barrier()
            allh = [('h', ti, c) for ti in range(len(tiles)) for c in range(8)]

            def finish_o(banks, ncols, zdst_fn, sg_fn, sgkeys, zkeys):
                for (bk, h0_, nh) in banks:
                    w = nh * ncols
                    q = n16()
                    act(t16[:, q, :w], ps[:, bk, :w], AF.Square, [('ps', bk)], [('t16', q)])
                    b2 = nb()
                    mm(ps[:, b2, :w], ones_bf[:], t16[:, q, :w], True, True, [K, ('t16', q)], [('ps', b2)])
                    r = n32()
                    act(t32[:, r, :w], ps[:, b2, :w], AF.Sqrt, [('ps', b2), K], [('t32', r)], bias=epsc[:], scale=1.0 / 128)
                    r2 = n32()
                    recip(t32[:, r2, :w], t32[:, r, :w], [('t32', r)], [('t32', r2)])
                    tt(t32[:, r, :w], ps[:, bk, :w], t32[:, r2, :w], ALU.mult, [('ps', bk), ('t32', r2), ('t32', r)], [('t32', r)])
                    stt(zdst_fn(h0_, nh), t32[:, r, :w].rearrange("p (a b) -> p a b", a=nh), gn[:, 0:1], sg_fn(h0_, nh), ALU.mult, ALU.mult,
                        [('t32', r), K] + sgkeys + allh, zkeys)

            if has_s:
                TS = 1024
                bs = nb()
                for hd in range(8):
                    mm(ps[0:64, bs, hd * 64:(hd + 1) * 64], ke[:, hd, TS:TS + 64], qe[:, hd, TS:TS + 64], True, True,
                       [('ke', hd, 2), ('qe', hd, 2)], [('ps', bs)])
                qp = n16()
                tt(t16[0:64, qp, :].rearrange("p (a b) -> p a b", a=8), ps[0:64, bs, :].rearrange("p (a b) -> p a b", a=8),
                   smask[:].unsqueeze(1).to_broadcast([64, 8, 64]), ALU.mult, [('ps', bs), K], [('t16', qp)])
                btr = nb()
                ptb = ps[:, btr, :].bitcast(BF16)
                for hd in range(8):
                    tr(ptb[0:64, hd * 128:(hd + 1) * 128], ke[:, hd, TS:TS + 64], ident_b[:], [('ke', hd, 2), K], [('ps', btr)])
                cp(keT[0:64, 0, :], ptb[0:64, :], [('ps', btr)], [('keT', 0)], eng='act')
                bo = nb()
                nb.hold(bo)
                for hd in range(8):
                    mm(ps[:, bo, hd * 64:(hd + 1) * 64], vtk[0:64, 8, hd * 128:(hd + 1) * 128], t16[0:64, qp, hd * 64:(hd + 1) * 64], hd == 0, False,
                       [('vtk', 8, hd // 4), ('t16', qp)], [('ps', bo)], skip_group_check=True)
                for sq in range(16):
                    rr = sq % 2
                    S0 = Sst if rr == 0 else tmpS
                    skey = 'Sst' if rr == 0 else 'tmpS'
                    dma('sp', S0, s0S[sq].rearrange("h k v -> k h v"), 'l_s0%d' % rr, writes=[skey])
                    cp(Sbf, S0, [skey], ['Sbf'], eng='act')
                    for hd in range(8):
                        c0 = hd * 64 + sq * 4
                        mm(ps[:, bo, c0:c0 + 4], Sbf[:, hd, :], qe[:, hd, TS + sq * 4:TS + sq * 4 + 4], False, True,
                           ['Sbf', ('qe', hd, 2)], [('ps', bo)], skip_group_check=True)
                    ts(keT[0:64, 1, :], keT[0:64, 0, :], onehot[:, sq:sq + 1], None, ALU.mult, ALU.bypass, [('keT', 0), K], [('keT', 1)])
                    u0, u1 = nb(), nb()
                    for hd in range(8):
                        ub = u0 if hd < 4 else u1
                        mm(ps[:, ub, (hd % 4) * 128:(hd % 4 + 1) * 128], keT[0:64, 1, hd * 128:(hd + 1) * 128], vtk[0:64, 8, hd * 128:(hd + 1) * 128],
                           True, True, [('keT', 1), ('vtk', 8, hd // 4)], [('ps', ub)])
                    for half, ub in ((0, u0), (1, u1)):
                        sv = S0[:, half * 4:(half + 1) * 4, :]
                        tt(sv, sv, ps[:, ub, :].rearrange("p (a b) -> p a b", a=4), ALU.add, [skey, ('ps', ub)], [skey])
                        tt(sv, sv, ebls[:, half * 4:(half + 1) * 4, sq:sq + 1].to_broadcast([128, 4, 128]), ALU.mult,
                           [skey] + [('ebls', hh) for hh in range(8)], [skey])
                    dma('sp', ss_o[sq].rearrange("h k v -> k h v"), S0, 'o_ss%d' % rr, reads=[skey])
                finish_o([(bo, 0, 8)], 64,
                         lambda h0_, nh: zT[:, h0_:h0_ + nh, TS:TS + 64],
                         lambda h0_, nh: sg[:, h0_:h0_ + nh, TS:TS + 64],
                         [('sg', hh, 2) for hh in range(8)], [('z', 2)])
                nb.release(bo)
                s.barrier(dma_prefix=('o_ss', 'l_s0'))
                memset(Sst, 0.0, ['Sst'] + [('Sst', hh) for hh in range(8)])
            HS = [slice(0, 4), slice(4, 8)]
            for b in range(8):
                ti = b // 4
                cols = slice(b * 128, (b + 1) * 128)
                ek = [('ebr', hh, ti) for hh in range(8)] + [('ebl', hh, ti) for hh in range(8)] + [('wsc', hh, ti) for hh in range(8)]
                btr = nb()
                ptb = ps[:, btr, :].bitcast(BF16)
                for half in range(2):
                    qk = n16()
                    kdv = t16[:, qk, :].rearrange("p (h c l) -> p h c l", h=4, c=4)
                    tt(kdv, ke[:, HS[half], cols].rearrange("p h (c l) -> p h c l", c=4),
                       wsc[:, HS[half], b * 4:b * 4 + 4].unsqueeze(3).to_broadcast([128, 4, 4, 32]), ALU.mult,
                       [('ke', hh, ti) for hh in range(half * 4, half * 4 + 4)] + ek, [('t16', qk)], eng='pool')
                    for hl in range(4):
                        hd = half * 4 + hl
                        tr(ptb[:, hd * 128:(hd + 1) * 128], t16[:, qk, hl * 128:(hl + 1) * 128], ident_b[:], [('t16', qk), K], [('ps', btr)])
                kslot = b % 2
                cp(keT[:, kslot, :], ptb[:, :], [('ps', btr)], [('keT', kslot)], eng='act')
                pbanks = []
                for half in range(2):
                    bk = nb()
                    for hl in range(4):
                        hd = half * 4 + hl
                        mm(ps[:, bk, hl * 128:(hl + 1) * 128], ke[:, hd, cols], qe[:, hd, cols], True, True, [('ke', hd, ti), ('qe', hd, ti)], [('ps', bk)])
                    qp = n16()
                    tt(t16[:, qp, :].rearrange("p (a b) -> p a b", a=4), ps[:, bk, :].rearrange("p (a b) -> p a b", a=4),
                       hmask[:].unsqueeze(1).to_broadcast([128, 4, 128]), ALU.mult, [('ps', bk), K], [('t16', qp)])
                    pbanks.append(qp)
                obanks = [nb(), nb()]
                nb.hold(*obanks)
                for hd in range(8):
                    ob_ = obanks[hd // 4]
                    qp = pbanks[hd // 4]
                    hl = hd % 4
                    mm(ps[:, ob_, hl * 128:(hl + 1) * 128], vtk[:, b, hd * 128:(hd + 1) * 128], t16[:, qp, hl * 128:(hl + 1) * 128], hl == 0, False,
                       [('vtk', b, hd // 4), ('t16', qp)], [('ps', ob_)], skip_group_check=True)
                for c in range(4):
                    ch = b * 4 + c
                    for hd in range(8):
                        act(Sbf[:, hd, :], Sst[:, hd, :], AF.Identity, [('Sst', hd), 'Sst'] + ek, [('Sbf', hd)], scale=ebr[:, hd, ch:ch + 1])
                    us = [nb(), nb()]
                    for half in range(2):
                        for hl in range(4):
                            hd = half * 4 + hl
                            c0 = hl * 128 + c * 32
                            mm(ps[:, obanks[half], c0:c0 + 32], Sbf[:, hd, :], qe[:, hd, b * 128 + c * 32:b * 128 + (c + 1) * 32], False, True,
                               [('Sbf', hd), ('qe', hd, ti)], [('ps', obanks[half])], skip_group_check=True)
                        for hl in range(4):
                            hd = half * 4 + hl
                            mm(ps[:, us[half], hl * 128:(hl + 1) * 128], keT[32 * c:32 * (c + 1), kslot, hd * 128:(hd + 1) * 128],
                               vtk[32 * c:32 * (c + 1), b, hd * 128:(hd + 1) * 128], True, True, [('keT', kslot), ('vtk', b, hd // 4)], [('ps', us[half])],
                               tile_position=(32 * c, 0))
                    for hd in range(8):
                        half, hl = hd // 4, hd % 4
                        stt(Sst[:, hd, :], Sst[:, hd, :], ebl[:, hd, ch:ch + 1], ps[:, us[half], hl * 128:(hl + 1) * 128], ALU.mult, ALU.add,
                            [('Sst', hd), ('ps', us[half])] + ek, [('Sst', hd)])
                finish_o([(obanks[0], 0, 4), (obanks[1], 4, 4)], 128,
                         lambda h0_, nh: zT[:, h0_:h0_ + nh, cols],
                         lambda h0_, nh: sg[:, h0_:h0_ + nh, cols],
                         [('sg', hh, ti) for hh in range(8)], [('z', ti)])
                nb.release(*obanks)
            if g == 1:
                dma('sp', sp_o.rearrange("h k v -> k h v"), Sst, 'o_sp', reads=['Sst'] + [('Sst', hh) for hh in range(8)])
            s.barrier()
            Wo = odd_w_out

            def wchunk(wv, half):
                return [(wv, Wo[:, half * 512:(half + 1) * 512].rearrange("(c p) f -> p c f", p=128))]
            outproj(sub, tiles, wchunk, lambda kc, t0, n: zT[:, kc, t0:t0 + n], lambda kc, ti: [('z', ti), ('h', ti, kc)], ybt, hook)

        kcar = sb("kcar", [128, 128], BF16)
        vcar = sb("vcar", [128, 128], BF16)
        Sst = sb("Sst", [128, 8, 128], F32)[:]

        for g in groups:
            if g == 0:
                tiles = [(0, 512, 'p'), (512, 512, 'p'), (1024, 64, 's')]
            else:
                tiles = [(0, 512, 'p'), (512, 512, 'p')]
            xv = xT.rearrange("(c p) t -> p c t", p=128)
            for ti, (t0, n, kind) in enumerate(tiles):
                src_c0 = g * 1024 + t0 if kind == 'p' else NP
                dma('sp', x[:, :, t0:t0 + n], xv[:, :, src_c0:src_c0 + n], 'l_x%d' % ti, writes=[('x', ti, c) for c in range(8)])
            yv = yT.rearrange("(c p) t -> p c t", p=128)
            for l in range(2):
                for k3 in range(3):
                    sub = l * 3 + k3

                    def hook(ti, tile, sub=sub):
                        t0, n, kind = tile
                        if sub == 5:
                            dst_c0 = g * 1024 + t0 if kind == 'p' else NP
                            dma('sp', yv[:, :, dst_c0:dst_c0 + n], x[:, :, t0:t0 + n], 'o_y%d' % ti,
                                reads=[('x', ti, c) for c in range(8)])
                    enabled = do_ffn if k3 != 1 else (do_even if l == 0 else do_odd)
                    for ti, tile in enumerate(tiles):
                        prenorm_tile(sub, ti, tile)
                    if not enabled:
                        for ti, tile in enumerate(tiles):
                            hook(ti, tile)
                    elif k3 == 0 or k3 == 2:
                        ffn(l, 0 if k3 == 0 else 1, sub, tiles, hook)
                    elif l == 0:
                        even_mixer(sub, g, tiles, hook)
                    else:
                        odd_mixer(sub, g, tiles, hook)
                    s.barrier(wait_engs=('act', 'dve'))
            s.barrier()
        s.finish()
        global _LAST_COUNTS
        _LAST_COUNTS = (dict(s.cnt), {e: len(v) for e, v in s.q.items()}, s.nsem)
        with nc.Block() as block:
            s.emit(block)
    return nc


def _consts():
    c = {}
    c['ident'] = np.eye(128, dtype=np.float32)
    k = np.arange(128)[:, None]
    q = np.arange(128)[None, :]
    am = np.zeros((128, 2, 128), np.float32)
    am[:, 0, :] = (k <= q)
    am[:, 1, :] = (k > q)
    c['amask'] = am.reshape(128, 256)
    c['hmask'] = ((k // 32 == q // 32) & (k <= q)).astype(np.float32)
    k6 = np.arange(64)[:, None]
    q6 = np.arange(64)[None, :]
    c['smask'] = ((k6 // 4 == q6 // 4) & (k6 <= q6)).astype(np.float32)
    c['mcmask'] = (np.arange(128)[:, None] > np.arange(4)[None, :]).astype(np.float32)
    rm = np.ones((128, 576), np.float32)
    rm[:, 0:512:32] = 0.0
    rm[:, 512:576:4] = 0.0
    c['rmask'] = rm
    c['onehot'] = (np.arange(64)[:, None] // 4 == np.arange(16)[None, :]).astype(np.float32)
    pos = np.concatenate([np.arange(NP, dtype=np.int32), np.tile(16384 + np.arange(4, dtype=np.int32), 16)]).astype(np.float32)
    inv_freq = (np.float32(500000.0) ** (-(np.arange(0, 16, 2, dtype=np.float32)) / np.float32(16))).astype(np.float32)
    ang = (pos[None, :] * inv_freq[:, None]).astype(np.float32)
    cs = np.cos(ang).astype(np.float32)
    sn = np.sin(ang).astype(np.float32)
    rc = np.ones((64, NP + NS), np.float32)
    rs = np.zeros((64, NP + NS), np.float32)
    rc[0:8] = cs
    rc[8:16] = cs
    rs[0:8] = -sn
    rs[8:16] = sn
    c['ropec'] = np.concatenate([rc, rc], 0)
    c['ropes'] = np.concatenate([rs, rs], 0)
    return c


_NC_CACHE = {}


def kernel(x_prompt, x_sample, c_prompt, c_sample, cache_k_win, cache_v_win, state_conv_rglru,
           state_h_rglru, state_s_hgrn, norm_pre, norm_post, ada_w, ada_b, ffn1_w_in, ffn1_w_out,
           ffn2_w_in, ffn2_w_out, even_w_in, even_w_out, attn_sinks, rg_conv_w, rg_conv_b, rg_wa,
           rg_ba, rg_wx, rg_bx, rg_lambda, odd_w_in, odd_w_out, hgrn_lb_logits, hgrn_gnorm, _flags=None):
    f = lambda a: np.ascontiguousarray(np.asarray(a, dtype=np.float32))
    flags = dict(_flags or {})
    ncores_run = flags.pop('ncores', NCORES)
    key = tuple(sorted(flags.items()))
    if key not in _NC_CACHE:
        _NC_CACHE[key] = build_nc(**flags)
    nc = _NC_CACHE[key]
    consts = _consts()

    def fm(v, nchunk):
        v = np.asarray(v, np.float32)
        lead = v.shape[:-1]
        v = v.reshape(lead + (nchunk, 128))
        v = np.moveaxis(v, -1, 0)
        return np.ascontiguousarray(v)

    shared = {
        'normpre': fm(np.asarray(norm_pre).reshape(6, D), 8),
        'normpost': fm(np.asarray(norm_post).reshape(6, D), 8),
        'adab': fm(np.asarray(ada_b).reshape(6, 3 * D), 24),
        'ada_w': f(ada_w),
        'ffn1_w_in': f(ffn1_w_in), 'ffn2_w_in': f(ffn2_w_in), 'ffn1_w_out': f(ffn1_w_out), 'ffn2_w_out': f(ffn2_w_out),
        'even_w_in': f(even_w_in)[0], 'even_w_out': f(even_w_out)[0],
        'convw': np.ascontiguousarray(np.moveaxis(np.asarray(rg_conv_w, np.float32)[0].reshape(4, 4, 128), 2, 0).transpose(0, 2, 1)),
        'convb': fm(np.asarray(rg_conv_b)[0], 4),
        'rg_wa': f(rg_wa)[0], 'rg_wx': f(rg_wx)[0],
        'rgba': fm(np.asarray(rg_ba)[0], 4), 'rgbx': fm(np.asarray(rg_bx)[0], 4), 'rglam': fm(np.asarray(rg_lambda)[0], 4),
        'odd_w_in': f(odd_w_in)[0], 'odd_w_out': f(odd_w_out)[0],
        'lbl': fm(np.asarray(hgrn_lb_logits), 8),
        'gnorm': f(np.asarray(hgrn_gnorm)[0].reshape(128, 1)),
    }
    sk = np.asarray(attn_sinks, np.float32)[0]
    sT = np.zeros((128, 4), np.float32)
    sT[0:64, :] = sk[0:4][None, :]
    sT[64:128, :] = sk[4:8][None, :]
    shared['sinkT'] = sT
    shared.update(consts)
    cwv = np.asarray(rg_conv_w, np.float32)[0].reshape(4, 4, 128)
    shared['convw'] = np.ascontiguousarray(cwv.transpose(2, 1, 0))

    xp = np.asarray(x_prompt, np.float32)
    xs = np.asarray(x_sample, np.float32)
    in_maps = []
    for i in range(NCORES):
        m = dict(shared)
        sl = slice(16 * i, 16 * i + 16)
        xt = np.concatenate([xp[i], xs[sl].reshape(64, D)], 0)
        m['xT'] = np.ascontiguousarray(xt.T)
        ct = np.concatenate([np.asarray(c_prompt, np.float32)[i:i + 1], np.asarray(c_sample, np.float32)[sl]], 0)
        m['cT'] = np.ascontiguousarray(ct.T)
        m['cachek'] = f(np.asarray(cache_k_win)[0, sl].reshape(16, 128, 128))
        m['cachev'] = f(np.asarray(cache_v_win)[0, sl].reshape(16, 128, 128))
        cs_ = np.asarray(state_conv_rglru, np.float32)[0, sl]
        m['convS'] = np.ascontiguousarray(cs_.reshape(16, 3, 4, 128).transpose(3, 2, 0, 1))
        hs_ = np.asarray(state_h_rglru, np.float32)[0, sl]
        m['h0S'] = np.ascontiguousarray(hs_.reshape(16, 4, 128).transpose(2, 1, 0))
        m['s0S'] = f(np.asarray(state_s_hgrn)[0, sl])
        in_maps.append(m)
    res = run_bass_kernel_spmd(nc, in_maps[:ncores_run], core_ids=list(range(ncores_run)))
    R = list(res.results)
    while len(R) < NCORES:
        R.append(R[0])
    y_prompt = np.stack([R[i]['yT'][:, :NP].T for i in range(NCORES)], 0)
    y_sample = np.concatenate([R[i]['yT'][:, NP:].T.reshape(16, 4, D) for i in range(NCORES)], 0)
    k_win_p = np.stack([R[i]['kwp'].T.reshape(128, 2, 64) for i in range(NCORES)], 0)[None]
    v_win_p = np.stack([R[i]['vwp'].reshape(128, 2, 64) for i in range(NCORES)], 0)[None]
    conv_p = np.stack([R[i]['convp'].transpose(2, 1, 0).reshape(3, 512) for i in range(NCORES)], 0)[None]
    h_p = np.stack([R[i]['hp'].T.reshape(512) for i in range(NCORES)], 0)[None]
    s_p = np.stack([R[i]['sp_o'] for i in range(NCORES)], 0)[None]
    k_win_s = np.concatenate([R[i]['kws'].reshape(16, 128, 2, 64) for i in range(NCORES)], 0)[None]
    v_win_s = np.concatenate([R[i]['vws'].reshape(16, 128, 2, 64) for i in range(NCORES)], 0)[None]
    conv_s = np.concatenate([R[i]['convs'].transpose(2, 3, 1, 0).reshape(16, 3, 512) for i in range(NCORES)], 0)[None]
    h_s = np.concatenate([R[i]['hs_o'].transpose(2, 1, 0).reshape(16, 512) for i in range(NCORES)], 0)[None]
    s_s = np.concatenate([R[i]['ss_o'] for i in range(NCORES)], 0)[None]
    outs = (y_prompt, y_sample, k_win_p, v_win_p, conv_p, h_p, s_p, k_win_s, v_win_s, conv_s, h_s, s_s)
    return tuple(np.ascontiguousarray(o, dtype=np.float32) for o in outs)
```

```python
import numpy as np
from contextlib import ExitStack
import concourse.bass as bass
import concourse.mybir as mybir
from concourse.bass_utils import run_bass_kernel_spmd

F32 = mybir.dt.float32
BF16 = mybir.dt.bfloat16
AF = mybir.ActivationFunctionType
ALU = mybir.AluOpType

ENGS = ['pe', 'act', 'dve', 'pool', 'sp']
D = 1024
DFF = 2816
NP = 2048
NS = 64
TG = 1088
NCORES = 8
EPS = 1e-6


class Sched:
    EPOCH = 3000

    def __init__(self, nc):
        self.nc = nc
        self.q = {e: [] for e in ENGS}
        self.cnt = {e: 0 for e in ENGS}
        self.sems = {}
        self.waited = {e: {} for e in ENGS}
        self.lastw = {}
        self.readers = {}
        self.dcount = {}
        self.nsem = 0
        self.pool_out = []
        self.pool_sum = 0

    def sem(self, key):
        if key not in self.sems:
            self.sems[key] = self.nc.alloc_semaphore(name="s%d" % self.nsem)
            self.nsem += 1
        return self.sems[key]

    def _filter(self, eng, deps):
        out = {}
        wd = self.waited[eng]
        for key, val in deps:
            if eng == 'pe' and key[0] == 'e' and key[1] == 'pe':
                continue
            if wd.get(key, 0) >= val:
                continue
            if out.get(key, 0) < val:
                out[key] = val
        for k, v in out.items():
            wd[k] = v
        return list(out.items())

    def _deps(self, eng, reads, writes):
        deps = []
        for r in reads:
            if r in self.lastw:
                deps.append(self.lastw[r])
        for w in writes:
            if w in self.lastw:
                deps.append(self.lastw[w])
            deps.extend(self.readers.get(w, ()))
        return self._filter(eng, deps)

    def _book(self, me, reads, writes):
        for r in reads:
            self.readers.setdefault(r, []).append(me)
        for w in writes:
            self.lastw[w] = me
            self.readers[w] = []

    def op(self, eng, fn, reads=(), writes=()):
        waits = self._deps(eng, reads, writes)
        idx = self.cnt[eng]
        self.cnt[eng] += 1
        key = ('e', eng, idx // self.EPOCH)
        self.sem(key)
        val = idx % self.EPOCH + 1
        self.q[eng].append((waits, fn, key, 1))
        self._book((key, val), reads, writes)

    POOL_DESC_LIMIT = 1 << 40

    def dma(self, eng, fn, semname, reads=(), writes=(), ndesc=1024):
        waits = self._deps(eng, reads, writes)
        if eng == 'pool':
            extra = []
            while self.pool_out and self.pool_sum + ndesc > self.POOL_DESC_LIMIT:
                k0, v0, n0 = self.pool_out.pop(0)
                self.pool_sum -= n0
                extra.append((k0, v0))
            if extra:
                waits = waits + self._filter(eng, extra)
        key = ('d', semname)
        self.sem(key)
        self.dcount[key] = self.dcount.get(key, 0) + 16
        self.q[eng].append((waits, fn, key, 16))
        self._book((key, self.dcount[key]), reads, writes)
        if eng == 'pool':
            self.pool_out.append((key, self.dcount[key], ndesc))
            self.pool_sum += ndesc

    def raw(self, eng, fn):
        self.q[eng].append(([], fn, None, 0))

    def _now(self, engs):
        deps = []
        for e in engs:
            if self.cnt[e] > 0:
                idx = self.cnt[e] - 1
                deps.append((('e', e, idx // self.EPOCH), idx % self.EPOCH + 1))
        return deps

    def barrier(self, engs=('pe', 'act', 'dve'), dma_prefix=('o_', 'l_'), wait_engs=None):
        deps = self._now(list(engs) + ['pool'])
        for key, v in self.dcount.items():
            if key[1].startswith(dma_prefix):
                deps.append((key, v))
        if wait_engs is None:
            wait_engs = engs
        for e in list(wait_engs) + ['sp']:
            w = self._filter(e, deps)
            if w:
                self.q[e].append((w, None, None, 0))

    def finish(self, eng='sp'):
        deps = self._now(ENGS)
        for key, v in self.dcount.items():
            deps.append((key, v))
        self.q[eng].append((deps, None, None, 0))

    def emit(self, block):
        sems = self.sems
        qs = self.q

        def run(engobj, lst):
            for waits, fn, key, inc in lst:
                for k, v in waits:
                    engobj.wait_ge(sems[k], v)
                if fn is not None:
                    ins = fn(engobj)
                    if key is not None:
                        ins.then_inc(sems[key], inc)

        @block.tensor
        def _(e):
            run(e, qs['pe'])

        @block.scalar
        def _(e):
            run(e, qs['act'])

        @block.vector
        def _(e):
            run(e, qs['dve'])

        @block.gpsimd
        def _(e):
            run(e, qs['pool'])

        @block.sync
        def _(e):
            run(e, qs['sp'])


class Rot:
    def __init__(self, n):
        self.n = n
        self.i = 0
        self.held = set()

    def __call__(self):
        for _ in range(self.n):
            v = self.i
            self.i = (self.i + 1) % self.n
            if v not in self.held:
                return v
        raise RuntimeError("all slots held")

    def hold(self, *vs):
        self.held.update(vs)

    def release(self, *vs):
        self.held.difference_update(vs)


def build_nc(do_even=True, do_odd=True, do_ffn=True, groups=(0, 1), estop=9, ostop=9):
    nc = bass.Bass("TRN2", target_bir_lowering=False)

    def din(name, shape):
        return nc.dram_tensor(name, list(shape), F32, kind="ExternalInput").ap()

    def dout(name, shape):
        return nc.dram_tensor(name, list(shape), F32, kind="ExternalOutput").ap()

    xT = din("xT", [D, NP + NS])
    cT = din("cT", [D, 17])
    cachek = din("cachek", [16, 128, 128])
    cachev = din("cachev", [16, 128, 128])
    convS = din("convS", [128, 4, 16, 3])
    h0S = din("h0S", [128, 4, 16])
    s0S = din("s0S", [16, 8, 128, 128])
    normpre = din("normpre", [128, 6, 8])
    normpost = din("normpost", [128, 6, 8])
    adab = din("adab", [128, 6, 24])
    ada_w = din("ada_w", [2, 3, D, 3 * D])
    ffn_w_in = [din("ffn1_w_in", [2, D, 2 * DFF]), din("ffn2_w_in", [2, D, 2 * DFF])]
    ffn_w_out = [din("ffn1_w_out", [2, DFF, D]), din("ffn2_w_out", [2, DFF, D])]
    even_w_in = din("even_w_in", [D, 1792])
    even_w_out = din("even_w_out", [D, D])
    sinkT = din("sinkT", [128, 4])
    convw = din("convw", [128, 4, 4])
    convb = din("convb", [128, 4])
    rg_wa = din("rg_wa", [8, 64, 64])
    rg_wx = din("rg_wx", [8, 64, 64])
    rgba = din("rgba", [128, 4])
    rgbx = din("rgbx", [128, 4])
    rglam = din("rglam", [128, 4])
    odd_w_in = din("odd_w_in", [D, 4096])
    odd_w_out = din("odd_w_out", [D, D])
    lbl = din("lbl", [128, 2, 8])
    gnorm = din("gnorm", [128, 1])
    ident_d = din("ident", [128, 128])
    amask_d = din("amask", [128, 256])
    hmask_d = din("hmask", [128, 128])
    smask_d = din("smask", [64, 64])
    mc_d = din("mcmask", [128, 4])
    rmask_d = din("rmask", [128, 512 + 64])
    onehot_d = din("onehot", [64, 16])
    ropec_d = din("ropec", [128, NP + NS])
    ropes_d = din("ropes", [128, NP + NS])

    yT = dout("yT", [D, NP + NS])
    kwp = dout("kwp", [128, 128])
    vwp = dout("vwp", [128, 128])
    convp = dout("convp", [128, 4, 3])
    hp = dout("hp", [128, 4])
    sp_o = dout("sp_o", [8, 128, 128])
    kws = dout("kws", [16, 128, 128])
    vws = dout("vws", [16, 128, 128])
    convs = dout("convs", [128, 4, 16, 3])
    hs_o = dout("hs_o", [128, 4, 16])
    ss_o = dout("ss_o", [16, 8, 128, 128])

    es = ExitStack()

    def sb(name, shape, dt):
        return es.enter_context(nc.sbuf_tensor("s_" + name, list(shape), dt))

    with es:
        s = Sched(nc)
        x = sb("x", [128, 8, TG], F32)
        h = sb("h", [128, 8, TG], BF16)
        ARW = 21120
        arena = sb("arena", [128, ARW], F32)
        NWS = 4
        wst = sb("wst", [128, NWS, 4096], BF16)
        NT32 = 6
        t32 = sb("t32", [128, NT32, 512], F32)
        NT16 = 4
        t16 = sb("t16", [128, NT16, 512], BF16)
        NINV = 2
        inv = sb("inv", [128, NINV, 512], F32)
        Amod = sb("Amod", [128, 6, 8, 17], F32)
        Smod = sb("Smod", [128, 6, 8, 17], F32)
        Gmod = sb("Gmod", [128, 6, 8, 17], F32)
        ones_bf = sb("ones_bf", [128, 128], BF16)
        ident_f = sb("ident_f", [128, 128], F32)
        ident_b = sb("ident_b", [128, 128], BF16)
        epsc = sb("epsc", [128, 1], F32)
        onec = sb("onec", [128, 1], F32)
        amask = sb("amask", [128, 256], BF16)
        hmask = sb("hmask", [128, 128], BF16)
        smask = sb("smask", [64, 64], BF16)
        mcm = sb("mcm", [128, 4], BF16)
        rmask = sb("rmask", [128, 576], F32)
        onehot = sb("onehot", [64, 16], F32)
        npre = sb("npre", [128, 6, 8], F32)
        npost = sb("npost", [128, 6, 8], F32)
        adabs = sb("adabs", [128, 6, 24], F32)
        sinkexp = sb("sinkexp", [128, 4], F32)
        cw = sb("cw", [128, 4, 4], F32)
        cb = sb("cb", [128, 4], F32)
        hba = sb("hba", [128, 4], F32)
        hbx = sb("hbx", [128, 4], F32)
        hcoef = sb("hcoef", [128, 4], F32)
        BDa = sb("BDa", [128, 4, 128], BF16)
        BDx = sb("BDx", [128, 4, 128], BF16)
        hc0 = sb("hc0", [128, 8], F32)
        hc1 = sb("hc1", [128, 8], F32)
        lbv = sb("lbv", [128, 8], F32)
        omlb = sb("omlb", [128, 8], F32)
        gn = sb("gn", [128, 1], F32)
        convc = sb("convc", [128, 4, 3], F32)
        hcar = sb("hcar", [128, 4], F32)
        sm1 = sb("sm1", [128, 2, 8], F32)
        ps = es.enter_context(nc.psum_tensor("ps", [128, 8, 512], F32))

        nb = Rot(8)
        n32 = Rot(NT32)
        n16 = Rot(NT16)
        ninv = Rot(NINV)
        nws = Rot(NWS)

        def mm(out, lhsT, rhs, start, stop, reads, writes, **kw):
            s.op('pe', lambda e: e.matmul(out, lhsT=lhsT, rhs=rhs, start=start, stop=stop, **kw), reads, writes)

        def tr(out, in_, ident, reads, writes):
            s.op('pe', lambda e: e.transpose(out=out, in_=in_, identity=ident), reads, writes)

        SET6 = (AF.Ln, AF.Exp, AF.Square, AF.Copy, AF.Identity)
        actstate = {'cur6': False}

        def act(out, in_, func, reads, writes, bias=None, scale=None, force6=False):
            if func not in SET6:
                actstate['cur6'] = False
            kw = {}
            if bias is not None:
                kw['bias'] = bias
            if scale is not None:
                kw['scale'] = scale
            s.op('act', lambda e: e.activation(out=out, in_=in_, func=func, **kw), reads, writes)

        def tt(out, in0, in1, op, reads, writes, eng='dve'):
            s.op(eng, lambda e: e.tensor_tensor(out=out, in0=in0, in1=in1, op=op), reads, writes)

        def ts(out, in0, s1, s2, op0, op1, reads, writes, eng='dve'):
            s.op(eng, lambda e: e.tensor_scalar(out=out, in0=in0, scalar1=s1, scalar2=s2, op0=op0, op1=op1), reads, writes)

        def stt(out, in0, scalar, in1, op0, op1, reads, writes):
            s.op('dve', lambda e: e.scalar_tensor_tensor(out=out, in0=in0, scalar=scalar, in1=in1, op0=op0, op1=op1), reads, writes)

        def cp(out, in_, reads, writes, eng='dve'):
            if eng == 'act':
                s.op('act', lambda e: e.activation(out=out, in_=in_, func=AF.Copy), reads, writes)
            else:
                s.op(eng, lambda e: e.tensor_copy(out, in_), reads, writes)

        def recip(out, in_, reads, writes, scratch=None):
            if scratch is None:
                s.op('dve', lambda e: e.reciprocal(out, in_), reads, writes)
            else:
                s.op('dve', lambda e: e.reciprocal_approx_accurate(out, in_, scratch), reads, writes)

        def memset(ap, val, writes, eng='dve'):
            s.op(eng, lambda e: e.memset(ap, val), (), writes)

        def scan(out, d0, d1, init, reads, writes):
            s.op('dve', lambda e: e.tensor_tensor_scan(out=out, data0=d0, data1=d1, initial=init, op0=ALU.mult, op1=ALU.add), reads, writes)

        def dma(eng, out, in_, sem, reads=(), writes=()):
            nd = 1
            for d_ in list(out.shape)[:-1]:
                nd *= d_
            s.dma(eng, lambda e: e.dma_start(out=out, in_=in_), sem, reads, writes, ndesc=nd)

        def av(off_words, shape, dt):
            n = 1
            for d_ in shape[1:]:
                n *= d_
            if dt == BF16:
                assert n % 2 == 0
                w = n // 2
                v = arena[:, off_words:off_words + w].bitcast(BF16)
            else:
                w = n
                v = arena[:, off_words:off_words + w]
            assert off_words + w <= ARW, (off_words, w, ARW)
            if len(shape) == 3:
                v = v.rearrange("p (a b) -> p a b", a=shape[1])
            elif len(shape) == 4:
                v = v.rearrange("p (a b c) -> p a b c", a=shape[1], b=shape[2])
            return v[0:shape[0]], off_words + w

        def wslot(i, shape):
            v = wst[:, i, :]
            n = 1
            for d_ in shape[1:]:
                n *= d_
            v = v[:, 0:n]
            if len(shape) == 3:
                v = v.rearrange("p (a b) -> p a b", a=shape[1])
            elif len(shape) == 4:
                v = v.rearrange("p (a b c) -> p a b c", a=shape[1], b=shape[2])
            return v

        K = 'const'
        for (dst, src) in [(ident_f, ident_d), (rmask, rmask_d), (onehot, onehot_d), (npre, normpre), (npost, normpost),
                           (adabs, adab), (cw, convw), (cb, convb), (gn, gnorm)]:
            dma('sp', dst[:], src, 'l_c_%s' % src.name, writes=[K])
        for (dst, src) in [(ident_b, ident_d), (amask, amask_d), (hmask, hmask_d), (smask, smask_d), (mcm, mc_d)]:
            dma('pool', dst[:], src, 'l_cb_%s' % src.name, writes=[K])
        memset(ones_bf[:], 1.0, [K])
        memset(epsc[:], EPS, [K])
        memset(onec[:], 1.0, [K])
        memset(BDa[:], 0.0, ['BD'])
        memset(BDx[:], 0.0, ['BD'])
        for k in range(8):
            c, hf = k // 2, k % 2
            dma('pool', BDa[hf * 64:(hf + 1) * 64, c, hf * 64:(hf + 1) * 64], rg_wa[k], 'l_bd', reads=(), writes=['BD'])
            dma('pool', BDx[hf * 64:(hf + 1) * 64, c, hf * 64:(hf + 1) * 64], rg_wx[k], 'l_bd', reads=(), writes=['BD'])
        dma('sp', sinkexp[:], sinkT, 'l_p1', writes=['p_sink'])
        act(sinkexp[:], sinkexp[:], AF.Exp, ['p_sink'], ['p_sink'])
        dma('sp', hba[:], rgba, 'l_p2', writes=['p_hba'])
        ts(hba[:], hba[:], 0.5, None, ALU.mult, ALU.bypass, ['p_hba'], ['p_hba'])
        dma('sp', hbx[:], rgbx, 'l_p3', writes=['p_hbx'])
        ts(hbx[:], hbx[:], 0.5, None, ALU.mult, ALU.bypass, ['p_hbx'], ['p_hbx'])
        dma('sp', hcoef[:], rglam, 'l_p4', writes=['p_hc'])
        act(hcoef[:], hcoef[:], AF.Exp, ['p_hc'], ['p_hc'], scale=-1.0)
        act(hcoef[:], hcoef[:], AF.Ln, ['p_hc', K], ['p_hc'], bias=onec[:], scale=1.0)
        ts(hcoef[:], hcoef[:], -4.0, None, ALU.mult, ALU.bypass, ['p_hc'], ['p_hc'])
        dma('sp', sm1[:], lbl, 'l_p5', writes=['p_lb'])
        tt(hc0[:], sm1[:, 1, :], sm1[:, 0, :], ALU.subtract, ['p_lb'], ['p_hc0'])
        act(hc0[:], hc0[:], AF.Tanh, ['p_hc0'], ['p_hc0'], scale=0.5)
        ts(hc1[:], hc0[:], -0.25, 0.25, ALU.mult, ALU.add, ['p_hc0'], ['p_hc1'])
        ts(hc0[:], hc0[:], 0.25, 0.75, ALU.mult, ALU.add, ['p_hc0', 'p_hc1'], ['p_hc0'])
        tt(lbv[:], hc0[:], hc1[:], ALU.subtract, ['p_hc0', 'p_hc1'], ['p_lbv'])
        ts(omlb[:], hc1[:], 2.0, None, ALU.mult, ALU.bypass, ['p_hc1'], ['p_lbv'])
        s.barrier()
        PK = [K, 'BD', 'p_sink', 'p_hba', 'p_hbx', 'p_hc', 'p_hc0', 'p_hc1']

        sc, o = av(0, [128, 8, 17], BF16)
        cts, o = av(o, [128, 8, 17], F32)
        modr, o = av(o, [128, 6, 24, 17], F32)
        dma('sp', cts, cT.rearrange("(c p) s -> p c s", p=128), 'l_p6', writes=['cts'])
        act(sc, cts, AF.Silu, ['cts'], ['sc'])
        for sub in range(6):
            l, k3 = sub // 3, sub % 3
            bank = nb()
            for blk in range(6):
                sl = nws()
                wv = wslot(sl, [128, 8, 512])
                dma('pool', wv, ada_w[l, k3, :, blk * 512:(blk + 1) * 512].rearrange("(c p) f -> p c f", p=128),
                    'w%d' % sl, writes=[('ws', sl)])
                for fl in range(4):
                    fc = blk * 4 + fl
                    for kc in range(8):
                        mm(ps[:, bank, fc * 17:(fc + 1) * 17], wv[:, kc, fl * 128:(fl + 1) * 128], sc[:, kc, :],
                           kc == 0, kc == 7, [('ws', sl), 'sc'], [('ps', bank)])
            tt(modr[:, sub, :, :], ps[:, bank, 0:408].rearrange("p (a b) -> p a b", a=24),
               adabs[:, sub, :].unsqueeze(2).to_broadcast([128, 24, 17]), ALU.add, [('ps', bank), K], [('modr', sub)])
            stt(Amod[:, sub], modr[:, sub, 8:16, :], 1.0, npre[:, sub, :].unsqueeze(2).to_broadcast([128, 8, 17]),
                ALU.add, ALU.mult, [('modr', sub), K], ['mods'])
            cp(Smod[:, sub], modr[:, sub, 0:8, :], [('modr', sub)], ['mods'])
            stt(Gmod[:, sub], modr[:, sub, 16:24, :], 1.0, npost[:, sub, :].unsqueeze(2).to_broadcast([128, 8, 17]),
                ALU.add, ALU.mult, [('modr', sub), K], ['mods'])
            if k3 != 1:
                ts(Gmod[:, sub], Gmod[:, sub], 0.5, None, ALU.mult, ALU.bypass, ['mods'], ['mods'])
        s.barrier()

        def rms_inv(src_fn, rkeys, n, scale):
            bank = nb()
            for c in range(8):
                q = n16()
                if True:
                    act(t16[:, q, :n], src_fn(c), AF.Square, rkeys(c), [('t16', q)])
                else:
                    tt(t16[:, q, :n], src_fn(c), src_fn(c), ALU.mult, rkeys(c), [('t16', q)], eng='pool')
                mm(ps[:, bank, :n], ones_bf[:], t16[:, q, :n], c == 0, c == 7, [K, ('t16', q)], [('ps', bank)])
            r = n32()
            iv = ninv()
            act(t32[:, r, :n], ps[:, bank, :n], AF.Ln, [('ps', bank), K], [('t32', r)], bias=epsc[:], scale=scale, force6=True)
            act(inv[:, iv, :n], t32[:, r, :n], AF.Exp, [('t32', r)], [('inv', iv)], scale=-0.5, force6=True)
            return iv

        def expand_mod(src, sub):
            r = n32()
            cp(t32[:, r, :].rearrange("p (c s i) -> p c s i", c=8, s=16),
               src[:, sub, :, 1:17].unsqueeze(3).to_broadcast([128, 8, 16, 4]), ['mods'], [('t32', r)])
            return r

        def prenorm_tile(sub, ti, tile):
            t0, n, kind = tile
            iv = rms_inv(lambda c: x[:, c, t0:t0 + n], lambda c: [('x', ti, c)], n, 1.0 / D)
            if kind == 'p':
                for c in range(8):
                    r = n32()
                    stt(t32[:, r, :n], x[:, c, t0:t0 + n], Amod[:, sub, c, 0:1], inv[:, iv, :n], ALU.mult, ALU.mult,
                        [('x', ti, c), 'mods', ('inv', iv)], [('t32', r)])
                    act(h[:, c, t0:t0 + n], t32[:, r, :n], AF.Identity, [('t32', r), 'mods'], [('h', ti, c)],
                        bias=Smod[:, sub, c, 0:1], scale=1.0)
            else:
                r = n32()
                v = t32[:, r, :].rearrange("p (c t) -> p c t", c=8)
                allx = [('x', ti, c) for c in range(8)]
                tt(v, x[:, :, t0:t0 + n], inv[:, iv, :n].unsqueeze(1).to_broadcast([128, 8, n]), ALU.mult,
                   allx + [('inv', iv)], [('t32', r)])
                ra = expand_mod(Amod, sub)
                tt(v, v, t32[:, ra, :].rearrange("p (c t) -> p c t", c=8), ALU.mult, [('t32', r), ('t32', ra)], [('t32', r)])
                rs = expand_mod(Smod, sub)
                tt(h[:, :, t0:t0 + n], v, t32[:, rs, :].rearrange("p (c t) -> p c t", c=8), ALU.add, [('t32', r), ('t32', rs)],
                   [('h', ti, c) for c in range(8)])

        def postnorm(sub, ti, tile, yb, ykey):
            t0, n, kind = tile
            iv = rms_inv(yb, lambda c: [ykey(c)], n, 1.0 / D)
            if kind == 'p':
                for c in range(8):
                    r = n32()
                    stt(t32[:, r, :n], yb(c), Gmod[:, sub, c, 0:1], inv[:, iv, :n], ALU.mult, ALU.mult,
                        [ykey(c), 'mods', ('inv', iv)], [('t32', r)])
                    tt(x[:, c, t0:t0 + n], x[:, c, t0:t0 + n], t32[:, r, :n], ALU.add, [('x', ti, c), ('t32', r)], [('x', ti, c)], eng='pool')
            else:
                rg = expand_mod(Gmod, sub)
                gv = t32[:, rg, :].rearrange("p (c t) -> p c t", c=8)
                n32.hold(rg)
                for c in range(8):
                    r = n32()
                    tt(t32[:, r, :n], yb(c), inv[:, iv, :n], ALU.mult, [ykey(c), ('inv', iv)], [('t32', r)])
                    tt(t32[:, r, :n], t32[:, r, :n], gv[:, c, :], ALU.mult, [('t32', r), ('t32', rg)], [('t32', r)])
                    tt(x[:, c, t0:t0 + n], x[:, c, t0:t0 + n], t32[:, r, :n], ALU.add, [('x', ti, c), ('t32', r)], [('x', ti, c)], eng='pool')
                n32.release(rg)

        def ffn(l, which, sub, tiles, hook):
            a_, o = av(0, [128, 22, TG], BF16)
            yb_, o = av(o, [128, 8, TG], F32)
            w_in = ffn_w_in[which][l]
            w_out = ffn_w_out[which][l]
            for blk in range(11):
                sl = nws()
                wv = wslot(sl, [128, 2, 8, 256])
                for gu in range(2):
                    c0 = gu * DFF + blk * 256
                    dma('pool', wv[:, gu], w_in[:, c0:c0 + 256].rearrange("(c p) f -> p c f", p=128), 'w%d' % sl,
                        writes=[('ws', sl)])
                for ti, (t0, n, kind) in enumerate(tiles):
                    for jj in range(2):
                        j = blk * 2 + jj
                        bg, bu = nb(), nb()
                        for gu, bk in ((0, bg), (1, bu)):
                            for kc in range(8):
                                mm(ps[:, bk, :n], wv[:, gu, kc, jj * 128:(jj + 1) * 128], h[:, kc, t0:t0 + n], kc == 0, kc == 7,
                                   [('ws', sl), ('h', ti, kc)], [('ps', bk)])
                        r = n32()
                        act(t32[:, r, :n], ps[:, bg, :n], AF.Silu, [('ps', bg)], [('t32', r)])
                        tt(a_[:, j, t0:t0 + n], t32[:, r, :n], ps[:, bu, :n], ALU.mult, [('t32', r), ('ps', bu)], [('a', j, ti)])
            jblocks = [(0, 4), (4, 4), (8, 4), (12, 4), (16, 4), (20, 2)]
            for bi, (j0, nj) in enumerate(jblocks):
                sl = nws()
                wv = wslot(sl, [128, 4, 1024])
                dma('pool', wv[:, 0:nj, :], w_out[j0 * 128:(j0 + nj) * 128, :].rearrange("(j p) f -> p j f", p=128), 'w%d' % sl,
                    writes=[('ws', sl)])
                for ti, (t0, n, kind) in enumerate(tiles):
                    for dc in range(8):
                        bank = nb()
                        for jl in range(nj):
                            mm(ps[:, bank, :n], wv[:, jl, dc * 128:(dc + 1) * 128], a_[:, j0 + jl, t0:t0 + n], jl == 0, jl == nj - 1,
                               [('ws', sl), ('a', j0 + jl, ti)], [('ps', bank)])
                        if bi == 0:
                            cp(yb_[:, dc, t0:t0 + n], ps[:, bank, :n], [('ps', bank)], [('yb', ti, dc)], eng='act')
                        else:
                            tt(yb_[:, dc, t0:t0 + n], yb_[:, dc, t0:t0 + n], ps[:, bank, :n], ALU.add, [('yb', ti, dc), ('ps', bank)], [('yb', ti, dc)])
            for ti, tile in enumerate(tiles):
                t0, n, kind = tile
                postnorm(sub, ti, tile, lambda c: yb_[:, c, t0:t0 + n], lambda c: ('yb', ti, c))
                hook(ti, tile)

        def outproj(sub, tiles, w_dram_chunk, src_fn, src_keys, ybt, hook):
            sls = []
            for half in range(2):
                sl = nws()
                wv = wslot(sl, [128, 8, 512])
                for (dst, ap) in w_dram_chunk(wv, half):
                    dma('pool', dst, ap, 'w%d' % sl, writes=[('ws', sl)])
                sls.append((sl, wv))
            for ti, tile in enumerate(tiles):
                t0, n, kind = tile
                for half in range(2):
                    sl, wv = sls[half]
                    for dcl in range(4):
                        dc = half * 4 + dcl
                        bank = nb()
                        for kc in range(8):
                            mm(ps[:, bank, :n], wv[:, kc, dcl * 128:(dcl + 1) * 128], src_fn(kc, t0, n), kc == 0, kc == 7,
                               [('ws', sl)] + src_keys(kc, ti), [('ps', bank)])
                        cp(ybt[:, dc, :n], ps[:, bank, :n], [('ps', bank)], [('ybt', dc)], eng='act')
                postnorm(sub, ti, tile, lambda c: ybt[:, c, :n], lambda c: ('ybt', c))
                hook(ti, tile)

        def even_mixer(sub, g, tiles, hook):
            has_s = (g == 0)
            o = 0
            oa, o = av(o, [128, 4, TG], BF16)
            ob, o = av(o, [128, 4, TG], BF16)
            oP = o
            qr, o = av(oP, [128, 4, TG], BF16)
            kr, o = av(o, [128, 1216], BF16)
            krf, o = av(o, [128, 192], F32)
            vtok, o = av(o, [128, 10, 128], BF16)
            vwf, o = av(o, [128, 2, 128], F32)
            cosT, o = av(o, [128, TG], F32)
            sinT, o = av(o, [128, TG], F32)
            kcache, o = av(o, [128, 2, 128], F32)
            vcache, o = av(o, [128, 2, 128], F32)
            KT, o = av(o, [128, 2, 128], BF16)
            Vb, o = av(o, [128, 16, 128], BF16)
            ktr, o = av(o, [64, 128], F32)
            gl, o = av(oP, [128, 4, TG], BF16)
            xr, o = av(o, [128, 4, 3 + 1024], F32)
            xrs, o = av(o, [128, 4, 16, 7], F32)
            xcL, xcbL, acL, bcL, hsbL = [], [], [], [], []
            for _p in range(2):
                v_, o = av(o, [128, TG], F32)
                xcL.append(v_)
                v_, o = av(o, [128, TG], BF16)
                xcbL.append(v_)
                v_, o = av(o, [128, TG], F32)
                acL.append(v_)
                v_, o = av(o, [128, TG], F32)
                bcL.append(v_)
                v_, o = av(o, [128, TG], F32)
                hsbL.append(v_)
            h0s, o = av(o, [128, 4, 16], F32)
            hsS, o = av(o, [128, 4, 16], F32)
            ybt, _ = av(oP, [128, 8, 512], F32)

            dma('sp', cosT[:, 0:1024], ropec_d[:, g * 1024:(g + 1) * 1024], 'l_rope', writes=['rope'])
            dma('sp', sinT[:, 0:1024], ropes_d[:, g * 1024:(g + 1) * 1024], 'l_rope', writes=['rope'])
            if has_s:
                dma('sp', cosT[:, 1024:1088], ropec_d[:, NP:NP + NS], 'l_rope', writes=['rope'])
                dma('sp', sinT[:, 1024:1088], ropes_d[:, NP:NP + NS], 'l_rope', writes=['rope'])
            W = even_w_in
            Wv_ = W.rearrange("(c p) f -> p c f", p=128)
            slq, slqs, slk, slst = nws(), nws(), nws(), nws()
            wq = wslot(slq, [128, 8, 512])
            wqs = wslot(slqs, [128, 8, 512])
            wk = wslot(slk, [128, 8, 384])
            wstg = wslot(slst, [128, 8, 512])
            dma('pool', wstg, Wv_[:, :, 0:512], 'w%d' % slst, writes=[('ws', slst)])
            dma('pool', wk[:, :, 0:128], Wv_[:, :, 512:640], 'w%d' % slk, writes=[('ws', slk)])
            dma('pool', wk[:, :, 256:384], Wv_[:, :, 640:768], 'w%d' % slk, writes=[('ws', slk)])
            src5 = wstg.rearrange("p c (hf j d) -> p c hf j d", hf=2, j=4)
            dq5 = wq.rearrange("p c (j hf d) -> p c j hf d", j=4, hf=2)
            dqs5 = wqs.rearrange("p c (j hf d) -> p c j hf d", j=4, hf=2)
            for hf in range(2):
                cp(dq5[:, :, :, hf, :], src5[:, :, hf, :, :], [('ws', slst)], [('ws', slq)], eng='pool')
                cp(dqs5[:, :, :, hf, 0:8], src5[:, :, hf, :, 8:16], [('ws', slst)], [('ws', slqs)], eng='pool')
                cp(dqs5[:, :, :, hf, 8:16], src5[:, :, hf, :, 0:8], [('ws', slst)], [('ws', slqs)], eng='pool')
                cp(dqs5[:, :, :, hf, 16:64], src5[:, :, hf, :, 16:64], [('ws', slst)], [('ws', slqs)], eng='pool')
            ks4 = wk[:, :, 0:128].rearrange("p c (kv d) -> p c kv d", kv=2)
            kd4 = wk[:, :, 128:256].rearrange("p c (kv d) -> p c kv d", kv=2)
            cp(kd4[:, :, :, 0:8], ks4[:, :, :, 8:16], [('ws', slk)], [('ws', slk)], eng='pool')
            cp(kd4[:, :, :, 8:16], ks4[:, :, :, 0:8], [('ws', slk)], [('ws', slk)], eng='pool')
            cp(kd4[:, :, :, 16:64], ks4[:, :, :, 16:64], [('ws', slk)], [('ws', slk)], eng='pool')

            def kcol(t0):
                return 128 + t0
            for ti, (t0, n, kind) in enumerate(tiles):
                hk = [('h', ti, kc) for kc in range(8)]
                for j in range(4):
                    b1, b2 = nb(), nb()
                    for kc in range(8):
                        mm(ps[:, b1, :n], wq[:, kc, j * 128:(j + 1) * 128], h[:, kc, t0:t0 + n], kc == 0, kc == 7,
                           [('ws', slq), ('h', ti, kc)], [('ps', b1)])
                    for kc in range(8):
                        mm(ps[:, b2, :n], wqs[:, kc, j * 128:(j + 1) * 128], h[:, kc, t0:t0 + n], kc == 0, kc == 7,
                           [('ws', slqs), ('h', ti, kc)], [('ps', b2)])
                    r1, r2 = n32(), n32()
                    tt(t32[:, r1, :n], ps[:, b1, :n], cosT[:, t0:t0 + n], ALU.mult, [('ps', b1), 'rope'], [('t32', r1)])
                    tt(t32[:, r2, :n], ps[:, b2, :n], sinT[:, t0:t0 + n], ALU.mult, [('ps', b2), 'rope'], [('t32', r2)])
                    tt(qr[:, j, t0:t0 + n], t32[:, r1, :n], t32[:, r2, :n], ALU.add, [('t32', r1), ('t32', r2)], [('qr', ti, j)])
                b1, b2 = nb(), nb()
                for kc in range(8):
                    mm(ps[:, b1, :n], wk[:, kc, 0:128], h[:, kc, t0:t0 + n], kc == 0, kc == 7, [('ws', slk), ('h', ti, kc)], [('ps', b1)])
                for kc in range(8):
                    mm(ps[:, b2, :n], wk[:, kc, 128:256], h[:, kc, t0:t0 + n], kc == 0, kc == 7, [('ws', slk), ('h', ti, kc)], [('ps', b2)])
                r1, r2 = n32(), n32()
                tt(t32[:, r1, :n], ps[:, b1, :n], cosT[:, t0:t0 + n], ALU.mult, [('ps', b1), 'rope'], [('t32', r1)])
                tt(t32[:, r2, :n], ps[:, b2, :n], sinT[:, t0:t0 + n], ALU.mult, [('ps', b2), 'rope'], [('t32', r2)])
                tt(kr[:, kcol(t0):kcol(t0) + n], t32[:, r1, :n], t32[:, r2, :n], ALU.add, [('t32', r1), ('t32', r2)], [('kr', ti)])
                if kind == 's':
                    tt(krf[:, 128:192], t32[:, r1, :n], t32[:, r2, :n], ALU.add, [('t32', r1), ('t32', r2)], ['krf_s'])
                elif g == 1 and ti == 1:
                    tt(krf[:, 0:128], t32[:, r1, 384:512], t32[:, r2, 384:512], ALU.add, [('t32', r1), ('t32', r2)], ['krf_p'])
                nblk = (n + 127) // 128
                for bl in range(nblk):
                    nt = min(128, n - bl * 128)
                    blk = (t0 // 128 + bl) if kind == 'p' else 8
                    bank = nb()
                    for kc in range(8):
                        mm(ps[:nt, bank, 0:128], h[:, kc, t0 + bl * 128:t0 + bl * 128 + nt], wk[:, kc, 256:384], kc == 0, kc == 7,
                           [('ws', slk), ('h', ti, kc)], [('ps', bank)])
                    cp(vtok[:nt, 1 + blk, :], ps[:nt, bank, 0:128], [('ps', bank)], [('vtok', 1 + blk)], eng='act')
                    if kind == 's':
                        cp(vwf[:nt, 1, :], ps[:nt, bank, 0:128], [('ps', bank)], ['vwf_s'], eng='act')
                    elif g == 1 and blk == 7:
                        cp(vwf[:, 0, :], ps[:, bank, 0:128], [('ps', bank)], ['vwf_p'], eng='act')
            if estop <= 1:
                return
            anorm = (int(estop * 100 + 0.5) % 10) != 5 and estop >= 2
            alvl = (int(estop * 100 + 0.5) % 10) if estop < 2 else 9
            if g == 1:
                cp(kr[:, 0:128], kcar[:], ['kcar'], [('krc',)])
                cp(vtok[:, 0, :], vcar[:], ['vcar'], [('vtok', 0)])
            def tile_of(col):
                return col // 512
            for b in range(8 if estop >= 2 else int((estop - 1) * 10 + 0.5)):
                has_prev = not (g == 0 and b == 0)
                tq = tile_of(b * 128)
                kkeys = [('kr', tile_of(b * 128))]
                if has_prev:
                    kkeys.append(('kr', tile_of((b - 1) * 128)) if b > 0 else ('krc',))
                bo, bd = nb(), nb()
                nb.hold(bo, bd)
                for jp in range(2):
                    banks = [nb(), nb()]
                    for hf in range(2):
                        p0 = hf * 64
                        for jl in range(2):
                            j = jp * 2 + jl
                            qa = qr[p0:p0 + 64, j, b * 128:(b + 1) * 128]
                            mm(ps[:, banks[hf], (jl * 2) * 128:(jl * 2 + 1) * 128], kr[p0:p0 + 64, 128 * (1 + b):128 * (2 + b)], qa, True, True,
                               kkeys + [('qr', tq, j)], [('ps', banks[hf])])
                            if has_prev:
                                mm(ps[:, banks[hf], (jl * 2 + 1) * 128:(jl * 2 + 2) * 128], kr[p0:p0 + 64, 128 * b:128 * (1 + b)], qa, True, True,
                                   kkeys + [('qr', tq, j)], [('ps', banks[hf])])
                    for hf in range(2):
                        p0 = hf * 64
                        bank = banks[hf]
                        q = n16()
                        if alvl < 1:
                            continue
                        if has_prev:
                            act(t16[:, q, :], ps[:, bank, :], AF.Exp, [('ps', bank)], [('t16', q)], scale=0.125)
                            ev = t16[:, q, :].rearrange("p (a b) -> p a b", a=2)
                            tt(ev, ev, amask[:].unsqueeze(1).to_broadcast([128, 2, 256]), ALU.mult, [('t16', q), K], [('t16', q)])
                        else:
                            ev4 = t16[:, q, :].rearrange("p (a b c) -> p a b c", a=2, b=2)
                            pv4 = ps[:, bank, :].rearrange("p (a b c) -> p a b c", a=2, b=2)
                            act(ev4[:, :, 0, :], pv4[:, :, 0, :], AF.Exp, [('ps', bank)], [('t16', q)], scale=0.125)
                            tt(ev4[:, :, 0, :], ev4[:, :, 0, :], amask[:, 0:128].unsqueeze(1).to_broadcast([128, 2, 128]), ALU.mult,
                               [('t16', q), K], [('t16', q)])
                        for jl in range(2 if alvl >= 2 else 0):
                            j = jp * 2 + jl
                            ed = t16[:, q, (jl * 2) * 128:(jl * 2 + 1) * 128]
                            ep = t16[:, q, (jl * 2 + 1) * 128:(jl * 2 + 2) * 128]
                            mm(ps[p0:p0 + 64, bo, j * 128:(j + 1) * 128], vtok[:, 1 + b, p0:p0 + 64], ed, True, not has_prev,
                               [('vtok', 1 + b), ('t16', q)], [('ps', bo)])
                            if has_prev:
                                mm(ps[p0:p0 + 64, bo, j * 128:(j + 1) * 128], vtok[:, b, p0:p0 + 64], ep, False, True,
                                   [('vtok', b), ('t16', q)], [('ps', bo)])
                            if alvl < 3:
                                continue
                            mm(ps[p0:p0 + 64, bd, j * 128:(j + 1) * 128], ones_bf[:, 0:64], ed, True, not has_prev, [K, ('t16', q)], [('ps', bd)])
                            if has_prev:
                                mm(ps[p0:p0 + 64, bd, j * 128:(j + 1) * 128], ones_bf[:, 0:64], ep, False, True, [K, ('t16', q)], [('ps', bd)])
                nb.release(bo, bd)
                if not anorm:
                    continue
                r = n32()
                rv = t32[:, r, :].rearrange("p (a b) -> p a b", a=4)
                tt(rv, ps[:, bd, :].rearrange("p (a b) -> p a b", a=4), sinkexp[:].unsqueeze(2).to_broadcast([128, 4, 128]), ALU.add,
                   [('ps', bd), 'p_sink'], [('t32', r)])
                r2 = n32()
                recip(t32[:, r2, :], t32[:, r, :], [('t32', r)], [('t32', r2)])
                tt(oa[:, :, b * 128:(b + 1) * 128], ps[:, bo, :].rearrange("p (a b) -> p a b", a=4),
                   t32[:, r2, :].rearrange("p (a b) -> p a b", a=4), ALU.mult, [('ps', bo), ('t32', r2)], [('oa', b // 4)])
            if g == 0:
                cp(kcar[:], kr[:, 128 * 8:128 * 9], [('kr', 1)], ['kcar'])
                cp(vcar[:], vtok[:, 8, :], [('vtok', 8)], ['vcar'])
            else:
                dma('sp', kwp, krf[:, 0:128], 'o_kwp', reads=['krf_p'])
                dma('sp', vwp, vwf[:, 0, :], 'o_vwp', reads=['vwf_p'])
            if estop <= 2:
                return
            if has_s:
                TS = 1024
                dma('sp', kws[:, 0:124, :], cachek[:, 4:128, :], 'o_kws')
                dma('sp', vws[:, 0:124, :], cachev[:, 4:128, :], 'o_vws')
                bsc = [nb(), nb()]
                nb.hold(*bsc)
                for sq in range(16):
                    rr = sq % 2
                    dma('sp', kcache[:, rr, :], cachek[sq], 'l_kc%d' % rr, writes=[('kcache', rr)])
                    dma('sp', vcache[:, rr, :], cachev[sq], 'l_vc%d' % rr, writes=[('vcache', rr)])
                    bt = nb()
                    tr(ps[:, bt, 0:128], kcache[:, rr, :], ident_f[:], [('kcache', rr), K], [('ps', bt)])
                    cp(KT[:, rr, :], ps[:, bt, 0:128], [('ps', bt)], [('KT', rr)], eng='act')
                    cp(Vb[:, sq, :], vcache[:, rr, :], [('vcache', rr)], [('Vb', sq)], eng='dve')
                    for hf in range(2):
                        p0 = hf * 64
                        for j in range(4):
                            c0 = (sq * 4 + j) * 4
                            mm(ps[:, bsc[hf], c0:c0 + 4], KT[p0:p0 + 64, rr, :], qr[p0:p0 + 64, j, TS + sq * 4:TS + sq * 4 + 4], True, True,
                               [('KT', rr), ('qr', 2, j)], [('ps', bsc[hf])])
                qc = [n16(), n16()]
                for hf in range(2):
                    act(t16[:, qc[hf], 0:256], ps[:, bsc[hf], 0:256], AF.Exp, [('ps', bsc[hf])], [('t16', qc[hf])], scale=0.125)
                    ecv = t16[:, qc[hf], 0:256].rearrange("p (a b) -> p a b", b=4)
                    tt(ecv, ecv, mcm[:].unsqueeze(1).to_broadcast([128, 64, 4]), ALU.mult, [('t16', qc[hf]), K], [('t16', qc[hf])])
                nb.release(*bsc)
                bn = [nb(), nb()]
                for hf in range(2):
                    p0 = hf * 64
                    for j in range(4):
                        mm(ps[0:64, bn[hf], j * 64:(j + 1) * 64], kr[p0:p0 + 64, 128 + TS:128 + TS + 64], qr[p0:p0 + 64, j, TS:TS + 64], True, True,
                           [('kr', 2), ('qr', 2, j)], [('ps', bn[hf])])
                qn = [n16(), n16()]
                for hf in range(2):
                    act(t16[0:64, qn[hf], 0:256], ps[0:64, bn[hf], 0:256], AF.Exp, [('ps', bn[hf])], [('t16', qn[hf])], scale=0.125)
                    env = t16[0:64, qn[hf], 0:256].rearrange("p (a b) -> p a b", a=4)
                    tt(env, env, smask[:].unsqueeze(1).to_broadcast([64, 4, 64]), ALU.mult, [('t16', qn[hf]), K], [('t16', qn[hf])])
                bo, bd = nb(), nb()
                for j in range(4):
                    for hf in range(2):
                        p0 = hf * 64
                        en_ = t16[0:64, qn[hf], j * 64:(j + 1) * 64]
                        mm(ps[p0:p0 + 64, bo, j * 64:(j + 1) * 64], vtok[0:64, 9, p0:p0 + 64], en_, True, False,
                           [('vtok', 9), ('t16', qn[hf])], [('ps', bo)], skip_group_check=True)
                        mm(ps[p0:p0 + 64, bd, j * 64:(j + 1) * 64], ones_bf[0:64, 0:64], en_, True, False, [K, ('t16', qn[hf])], [('ps', bd)],
                           skip_group_check=True)
                        for sq in range(16):
                            c1 = (sq * 4 + j) * 4
                            ec_ = t16[:, qc[hf], c1:c1 + 4]
                            mm(ps[p0:p0 + 64, bo, j * 64 + sq * 4:j * 64 + sq * 4 + 4], Vb[:, sq, p0:p0 + 64], ec_, False, True,
                               [('Vb', sq), ('t16', qc[hf])], [('ps', bo)], skip_group_check=True)
                            mm(ps[p0:p0 + 64, bd, j * 64 + sq * 4:j * 64 + sq * 4 + 4], ones_bf[:, 0:64], ec_, False, True,
                               [K, ('t16', qc[hf])], [('ps', bd)], skip_group_check=True)
                r = n32()
                rv = t32[:, r, 0:256].rearrange("p (a b) -> p a b", a=4)
                tt(rv, ps[:, bd, 0:256].rearrange("p (a b) -> p a b", a=4), sinkexp[:].unsqueeze(2).to_broadcast([128, 4, 64]), ALU.add,
                   [('ps', bd), 'p_sink'], [('t32', r)])
                r2 = n32()
                recip(t32[:, r2, 0:256], t32[:, r, 0:256], [('t32', r)], [('t32', r2)])
                tt(oa[:, :, TS:TS + 64], ps[:, bo, 0:256].rearrange("p (a b) -> p a b", a=4),
                   t32[:, r2, 0:256].rearrange("p (a b) -> p a b", a=4), ALU.mult, [('ps', bo), ('t32', r2)], [('oa', 2)])
                bt = nb()
                tr(ps[0:64, bt, 0:128], krf[:, 128:192], ident_f[:], ['krf_s', K], [('ps', bt)])
                cp(ktr[:], ps[0:64, bt, 0:128], [('ps', bt)], ['ktr'], eng='act')
                for sq in range(16):
                    dma('sp', kws[sq, 124:128, :], ktr[sq * 4:sq * 4 + 4, :], 'o_kws2', reads=['ktr'])
                    dma('sp', vws[sq, 124:128, :], vwf[sq * 4:sq * 4 + 4, 1, :], 'o_vws2', reads=['vwf_s'])

            if estop <= 3:
                return
            s.barrier()
            slg, slr = nws(), nws()
            wg = wslot(slg, [128, 8, 512])
            wr = wslot(slr, [128, 8, 512])
            dma('pool', wg, Wv_[:, :, 768:1280], 'w%d' % slg, writes=[('ws', slg)])
            dma('pool', wr, Wv_[:, :, 1280:1792], 'w%d' % slr, writes=[('ws', slr)])
            if g == 0:
                memset(xr[:, :, 0:3], 0.0, ['xr_c'])
                dma('sp', xrs[:, :, :, 0:3], convS, 'l_cs1', writes=['xrs_c'])
                dma('sp', h0s, h0S, 'l_cs2', writes=['h0s'])
            else:
                cp(xr[:, :, 0:3], convc[:], ['convc'], ['xr_c'])
            for ti, (t0, n, kind) in enumerate(tiles):
                for j in range(4):
                    bank = nb()
                    for kc in range(8):
                        mm(ps[:, bank, :n], wg[:, kc, j * 128:(j + 1) * 128], h[:, kc, t0:t0 + n], kc == 0, kc == 7,
                           [('ws', slg), ('h', ti, kc)], [('ps', bank)])
                    r1, r2 = n32(), n32()
                    act(t32[:, r1, :n], ps[:, bank, :n], AF.Square, [('ps', bank)], [('t32', r1)])
                    ts(t32[:, r1, :n], t32[:, r1, :n], 0.044715, 1.0, ALU.mult, ALU.add, [('t32', r1)], [('t32', r1)])
                    tt(t32[:, r2, :n], t32[:, r1, :n], ps[:, bank, :n], ALU.mult, [('t32', r1), ('ps', bank)], [('t32', r2)])
                    act(t32[:, r1, :n], t32[:, r2, :n], AF.Tanh, [('t32', r2)], [('t32', r1)], scale=0.7978845608028654)
                    stt(gl[:, j, t0:t0 + n], t32[:, r1, :n], 1.0, ps[:, bank, :n], ALU.add, ALU.mult, [('t32', r1), ('ps', bank)], [('gl', ti, j)])
                    bank = nb()
                    for kc in range(8):
                        mm(ps[:, bank, :n], wr[:, kc, j * 128:(j + 1) * 128], h[:, kc, t0:t0 + n], kc == 0, kc == 7,
                           [('ws', slr), ('h', ti, kc)], [('ps', bank)])
                    if kind == 'p':
                        cp(xr[:, j, 3 + t0:3 + t0 + n], ps[:, bank, :n], [('ps', bank)], [('xr', j, ti)], eng='act')
                    else:
                        cp(xrs[:, j, :, 3:7], ps[:, bank, 0:64].rearrange("p (a b) -> p a b", b=4), [('ps', bank)], [('xr', j, ti)], eng='act')
            nt_ = len(tiles)
            for c in range(4):
                pp = c % 2
                xc, xcb, ac, bc, hsb = xcL[pp], xcbL[pp], acL[pp], bcL[pp], hsbL[pp]
                xk = [('xr', c, ti) for ti in range(nt_)] + ['xr_c', 'xrs_c']
                ts(xc[:, 0:1024], xr[:, c, 3:1027], cw[:, c, 3:4], cb[:, c:c + 1], ALU.mult, ALU.add, xk + [K], [('xc', pp)])
                for jj in range(3):
                    stt(xc[:, 0:1024], xr[:, c, jj:jj + 1024], cw[:, c, jj:jj + 1], xc[:, 0:1024], ALU.mult, ALU.add, xk + [K, ('xc', pp)], [('xc', pp)])
                if has_s:
                    xcs = xc[:, 1024:1088].rearrange("p (a b) -> p a b", b=4)
                    ts(xcs, xrs[:, c, :, 3:7], cw[:, c, 3:4], cb[:, c:c + 1], ALU.mult, ALU.add, xk + [K], [('xcs', pp)])
                    for jj in range(3):
                        stt(xcs, xrs[:, c, :, jj:jj + 4], cw[:, c, jj:jj + 1], xcs, ALU.mult, ALU.add, xk + [K, ('xcs', pp)], [('xcs', pp)])
                ntot = 1088 if has_s else 1024
                cp(xcb[:, 0:ntot], xc[:, 0:ntot], [('xc', pp), ('xcs', pp)], [('xcb', pp)], eng='act')
                for ti, (t0, n, kind) in enumerate(tiles):
                    b1, b2 = nb(), nb()
                    mm(ps[:, b1, :n], BDa[:, c, :], xcb[:, t0:t0 + n], True, True, ['BD', ('xcb', pp)], [('ps', b1)])
                    mm(ps[:, b2, :n], BDx[:, c, :], xcb[:, t0:t0 + n], True, True, ['BD', ('xcb', pp)], [('ps', b2)])
                    r1 = n32()
                    act(t32[:, r1, :n], ps[:, b1, :n], AF.Tanh, [('ps', b1), 'p_hba'], [('t32', r1)], bias=hba[:, c:c + 1], scale=0.5)
                    act(ac[:, t0:t0 + n], t32[:, r1, :n], AF.Exp, [('t32', r1), 'p_hc'], [('ac', ti, pp)], bias=hcoef[:, c:c + 1], scale=hcoef[:, c:c + 1])
                    act(hsb[:, t0:t0 + n], ps[:, b2, :n], AF.Tanh, [('ps', b2), 'p_hbx', ('hsb', pp), ('hsbs', pp)], [('hsbT', ti, pp)],
                        bias=hbx[:, c:c + 1], scale=0.5)
                for ti, (t0, n, kind) in enumerate(tiles):
                    bt_ = bc[:, t0:t0 + n]
                    tt(bt_, ac[:, t0:t0 + n], ac[:, t0:t0 + n], ALU.mult, [('ac', ti, pp)], [('bc', ti, pp)])
                    ts(bt_, bt_, -1.0, 1.0, ALU.mult, ALU.add, [('bc', ti, pp)], [('bc', ti, pp)])
                    ts(bt_, bt_, 0.0, None, ALU.max, ALU.bypass, [('bc', ti, pp)], [('bc', ti, pp)])
                for ti, (t0, n, kind) in enumerate(tiles):
                    bt_ = bc[:, t0:t0 + n]
                    act(bt_, bt_, AF.Sqrt, [('bc', ti, pp)], [('bc', ti, pp)])
                for ti, (t0, n, kind) in enumerate(tiles):
                    bt_ = bc[:, t0:t0 + n]
                    r2 = n32()
                    stt(t32[:, r2, :n], hsb[:, t0:t0 + n], 1.0, xc[:, t0:t0 + n], ALU.add, ALU.mult,
                        [('hsbT', ti, pp), ('xc', pp), ('xcs', pp)], [('t32', r2)])
                    stt(bt_, t32[:, r2, :n], 0.5, bt_, ALU.mult, ALU.mult, [('t32', r2), ('bc', ti, pp)], [('bc', ti, pp)])
                allab = [('ac', ti, pp) for ti in range(nt_)] + [('bc', ti, pp) for ti in range(nt_)]
                if has_s:
                    a0 = ac[:, 1024:1088].rearrange("p (a b) -> p a b", b=4)[:, :, 0]
                    b0 = bc[:, 1024:1088].rearrange("p (a b) -> p a b", b=4)[:, :, 0]
                    r = n32()
                    tt(t32[:, r, 0:16], a0, h0s[:, c, :], ALU.mult, allab + ['h0s'], [('t32', r)])
                    tt(b0, b0, t32[:, r, 0:16], ALU.add, allab + [('t32', r)], [('bc', 2, pp)])
                    memset(a0, 0.0, [('ac', 2, pp)])
                init = 0.0 if g == 0 else hcar[:, c:c + 1]
                scan(hsb[:, 0:1024], ac[:, 0:1024], bc[:, 0:1024], init, allab + ['hcar'], [('hsb', pp)] + [('hsbT', ti, pp) for ti in range(nt_)])
                if has_s:
                    scan(hsb[:, 1024:1088], ac[:, 1024:1088], bc[:, 1024:1088], 0.0, allab, [('hsbs', pp)] + [('hsbT', ti, pp) for ti in range(nt_)])
                stt(ob[:, c, 0:ntot], gl[:, c, 0:ntot], 0.5, hsb[:, 0:ntot], ALU.mult, ALU.mult,
                    [('gl', ti, c) for ti in range(nt_)] + [('hsb', pp), ('hsbs', pp)], [('ob', c)])
                cp(hcar[:, c:c + 1], hsb[:, 1023:1024], [('hsb', pp)], ['hcar'])
                if has_s:
                    cp(hsS[:, c, :], hsb[:, 1024:1088].rearrange("p (a b) -> p a b", b=4)[:, :, 3], [('hsbs', pp)], ['hsS'])
            cp(convc[:], xr[:, :, 1024:1027], [('xr', c, 1) for c in range(4)], ['convc'])
            if g == 1:
                dma('sp', hp, hcar[:], 'o_hp', reads=['hcar'])
                dma('sp', convp, convc[:], 'o_cp', reads=['convc'])
            if has_s:
                dma('sp', hs_o, hsS, 'o_hs', reads=['hsS'])
                dma('sp', convs, xrs[:, :, :, 4:7], 'o_cs', reads=[('xr', c, 2) for c in range(4)])
            if estop <= 4:
                return
            s.barrier()
            Wo = even_w_out

            def wchunk(wv, half):
                cs = slice(half * 512, (half + 1) * 512)
                out = []
                for hf in range(2):
                    out.append((wv[hf * 64:(hf + 1) * 64, 0:4, :],
                                Wo[hf * 256:(hf + 1) * 256, cs].rearrange("(j d) f -> d j f", j=4)))
                out.append((wv[:, 4:8, :], Wo[512:1024, cs].rearrange("(c p) f -> p c f", p=128)))
                return out

            def src(kc, t0, n):
                return oa[:, kc, t0:t0 + n] if kc < 4 else ob[:, kc - 4, t0:t0 + n]

            def srck(kc, ti):
                return [('oa', ti)] if kc < 4 else [('ob', kc - 4)]
            outproj(sub, tiles, wchunk, src, srck, ybt, hook)

        def odd_mixer(sub, g, tiles, hook):
            has_s = (g == 0)
            o = 0
            qe, o = av(o, [128, 8, TG], BF16)
            ybt, _ = av(0, [128, 8, 512], F32)
            ke, o = av(o, [128, 8, TG], BF16)
            sg, o = av(o, [128, 8, TG], BF16)
            vtk, o = av(o, [128, 9, 1024], BF16)
            keT, o = av(o, [128, 2, 1024], BF16)
            Sbf, o = av(o, [128, 8, 128], BF16)
            tmpS, o = av(o, [128, 8, 128], F32)
            ebl, o = av(o, [128, 8, 32], F32)
            wsc, o = av(o, [128, 8, 32], F32)
            ebr, o = av(o, [128, 8, 32], F32)
            ebls, o = av(o, [128, 8, 16], F32)
            zT = h
            W = odd_w_in
            Wv_ = W.rearrange("(c p) f -> p c f", p=128)
            for hd in range(8):
                sl = nws()
                wv = wslot(sl, [128, 2, 8, 128])
                dma('pool', wv[:, 0], Wv_[:, :, hd * 128:(hd + 1) * 128], 'w%d' % sl, writes=[('ws', sl)])
                dma('pool', wv[:, 1], Wv_[:, :, 1024 + hd * 128:1024 + (hd + 1) * 128], 'w%d' % sl, writes=[('ws', sl)])
                for ti, (t0, n, kind) in enumerate(tiles):
                    bq, bf = nb(), nb()
                    for kc in range(8):
                        mm(ps[:, bq, :n], wv[:, 0, kc, :], h[:, kc, t0:t0 + n], kc == 0, kc == 7, [('ws', sl), ('h', ti, kc)], [('ps', bq)])
                    for kc in range(8):
                        mm(ps[:, bf, :n], wv[:, 1, kc, :], h[:, kc, t0:t0 + n], kc == 0, kc == 7, [('ws', sl), ('h', ti, kc)], [('ps', bf)])
                    rA, rB, rC, rD = n32(), n32(), n32(), n32()
                    A_, B_, C_, D_ = t32[:, rA, :n], t32[:, rB, :n], t32[:, rC, :n], t32[:, rD, :n]
                    act(A_, ps[:, bf, :n], AF.Exp, [('ps', bf)], [('t32', rA)], scale=-1.0)
                    act(C_, A_, AF.Ln, [('t32', rA), K], [('t32', rC)], bias=onec[:], scale=1.0)
                    act(D_, A_, AF.Ln, [('t32', rA), K, 'p_lbv'], [('t32', rD)], bias=onec[:], scale=lbv[:, hd:hd + 1])
                    act(B_, C_, AF.Exp, [('t32', rC)], [('t32', rB)], scale=-1.0)
                    stt(B_, A_, omlb[:, hd:hd + 1], B_, ALU.mult, ALU.mult, [('t32', rA), ('t32', rB), 'p_lbv'], [('t32', rB)])
                    tt(C_, D_, C_, ALU.subtract, [('t32', rD), ('t32', rC)], [('t32', rC)])
                    if kind == 'p':
                        scan(A_, rmask[:, 0:n], C_, 0.0, [K, ('t32', rC), ('t32', rA)], [('t32', rA)])
                        nch = n // 32
                        ch0 = t0 // 32
                        bv = A_.rearrange("p (c l) -> p c l", l=32)
                        tt(D_.rearrange("p (c l) -> p c l", l=32), bv, bv[:, :, 15].unsqueeze(2).to_broadcast([128, nch, 32]), ALU.subtract,
                           [('t32', rA)], [('t32', rD)])
                        act(ebl[:, hd, ch0:ch0 + nch], bv[:, :, 31], AF.Exp, [('t32', rA)], [('ebl', hd, ti)])
                        act(ebr[:, hd, ch0:ch0 + nch], bv[:, :, 15], AF.Exp, [('t32', rA)], [('ebr', hd, ti)])
                        act(wsc[:, hd, ch0:ch0 + nch], D_.rearrange("p (c l) -> p c l", l=32)[:, :, 31], AF.Exp, [('t32', rD)], [('wsc', hd, ti)])
                        dsrc, dk = D_, ('t32', rD)
                    else:
                        scan(A_, rmask[:, 512:512 + n], C_, 0.0, [K, ('t32', rC), ('t32', rA)], [('t32', rA)])
                        act(ebls[:, hd, :], A_.rearrange("p (c l) -> p c l", l=4)[:, :, 3], AF.Exp, [('t32', rA)], [('ebls', hd)])
                        dsrc, dk = A_, ('t32', rA)
                    act(C_, dsrc, AF.Exp, [dk, ('t32', rC)], [('t32', rC)])
                    tt(qe[:, hd, t0:t0 + n], ps[:, bq, :n], C_, ALU.mult, [('ps', bq), ('t32', rC)], [('qe', hd, ti)])
                    act(C_, dsrc, AF.Exp, [dk, ('t32', rC)], [('t32', rC)], scale=-1.0)
                    tt(ke[:, hd, t0:t0 + n], B_, C_, ALU.mult, [('t32', rB), ('t32', rC)], [('ke', hd, ti)])
            for half in range(2):
                sl = nws()
                wv = wslot(sl, [128, 8, 512])
                dma('pool', wv, Wv_[:, :, 2048 + half * 512:2048 + (half + 1) * 512], 'w%d' % sl, writes=[('ws', sl)])
                for ti, (t0, n, kind) in enumerate(tiles):
                    nblk = (n + 127) // 128
                    for bl in range(nblk):
                        nt = min(128, n - bl * 128)
                        blk = (t0 // 128 + bl) if kind == 'p' else 8
                        bank = nb()
                        for kc in range(8):
                            mm(ps[:nt, bank, :], h[:, kc, t0 + bl * 128:t0 + bl * 128 + nt], wv[:, kc, :], kc == 0, kc == 7,
                               [('ws', sl), ('h', ti, kc)], [('ps', bank)])
                        cp(vtk[:nt, blk, half * 512:(half + 1) * 512], ps[:nt, bank, :], [('ps', bank)], [('vtk', blk, half)], eng='act')
            for half in range(2):
                sl = nws()
                wv = wslot(sl, [128, 8, 512])
                dma('pool', wv, Wv_[:, :, 3072 + half * 512:3072 + (half + 1) * 512], 'w%d' % sl, writes=[('ws', sl)])
                for ti, (t0, n, kind) in enumerate(tiles):
                    for hl in range(4):
                        hd = half * 4 + hl
                        bank = nb()
                        for kc in range(8):
                            mm(ps[:, bank, :n], wv[:, kc, hl * 128:(hl + 1) * 128], h[:, kc, t0:t0 + n], kc == 0, kc == 7,
                               [('ws', sl), ('h', ti, kc)], [('ps', bank)])
                        act(sg[:, hd, t0:t0 + n], ps[:, bank, :n], AF.Silu, [('ps', bank)], [('sg', hd, ti)])
            s.barrier()
            allh = [('h', ti, c) for ti in range(len(tiles)) for c in range(8)]

            def finish_o(banks, ncols, zdst_fn, sg_fn, sgkeys, zkeys):
                for (bk, h0_, nh) in banks:
                    w = nh * ncols
                    q = n16()
                    act(t16[:, q, :w], ps[:, bk, :w], AF.Square, [('ps', bk)], [('t16', q)])
                    b2 = nb()
                    mm(ps[:, b2, :w], ones_bf[:], t16[:, q, :w], True, True, [K, ('t16', q)], [('ps', b2)])
                    r = n32()
                    act(t32[:, r, :w], ps[:, b2, :w], AF.Sqrt, [('ps', b2), K], [('t32', r)], bias=epsc[:], scale=1.0 / 128)
                    r2 = n32()
                    recip(t32[:, r2, :w], t32[:, r, :w], [('t32', r)], [('t32', r2)])
                    tt(t32[:, r, :w], ps[:, bk, :w], t32[:, r2, :w], ALU.mult, [('ps', bk), ('t32', r2), ('t32', r)], [('t32', r)])
                    stt(zdst_fn(h0_, nh), t32[:, r, :w].rearrange("p (a b) -> p a b", a=nh), gn[:, 0:1], sg_fn(h0_, nh), ALU.mult, ALU.mult,
                        [('t32', r), K] + sgkeys + allh, zkeys)

            if has_s:
                TS = 1024
                bs = nb()
                for hd in range(8):
                    mm(ps[0:64, bs, hd * 64:(hd + 1) * 64], ke[:, hd, TS:TS + 64], qe[:, hd, TS:TS + 64], True, True,
                       [('ke', hd, 2), ('qe', hd, 2)], [('ps', bs)])
                qp = n16()
                tt(t16[0:64, qp, :].rearrange("p (a b) -> p a b", a=8), ps[0:64, bs, :].rearrange("p (a b) -> p a b", a=8),
                   smask[:].unsqueeze(1).to_broadcast([64, 8, 64]), ALU.mult, [('ps', bs), K], [('t16', qp)])
                btr = nb()
                ptb = ps[:, btr, :].bitcast(BF16)
                for hd in range(8):
                    tr(ptb[0:64, hd * 128:(hd + 1) * 128], ke[:, hd, TS:TS + 64], ident_b[:], [('ke', hd, 2), K], [('ps', btr)])
                cp(keT[0:64, 0, :], ptb[0:64, :], [('ps', btr)], [('keT', 0)], eng='act')
                bo = nb()
                nb.hold(bo)
                for hd in range(8):
                    mm(ps[:, bo, hd * 64:(hd + 1) * 64], vtk[0:64, 8, hd * 128:(hd + 1) * 128], t16[0:64, qp, hd * 64:(hd + 1) * 64], hd == 0, False,
                       [('vtk', 8, hd // 4), ('t16', qp)], [('ps', bo)], skip_group_check=True)
                for sq in range(16):
                    rr = sq % 2
                    S0 = Sst if rr == 0 else tmpS
                    skey = 'Sst' if rr == 0 else 'tmpS'
                    dma('sp', S0, s0S[sq].rearrange("h k v -> k h v"), 'l_s0%d' % rr, writes=[skey])
                    cp(Sbf, S0, [skey], ['Sbf'], eng='act')
                    for hd in range(8):
                        c0 = hd * 64 + sq * 4
                        mm(ps[:, bo, c0:c0 + 4], Sbf[:, hd, :], qe[:, hd, TS + sq * 4:TS + sq * 4 + 4], False, True,
                           ['Sbf', ('qe', hd, 2)], [('ps', bo)], skip_group_check=True)
                    ts(keT[0:64, 1, :], keT[0:64, 0, :], onehot[:, sq:sq + 1], None, ALU.mult, ALU.bypass, [('keT', 0), K], [('keT', 1)])
                    u0, u1 = nb(), nb()
                    for hd in range(8):
                        ub = u0 if hd < 4 else u1
                        mm(ps[:, ub, (hd % 4) * 128:(hd % 4 + 1) * 128], keT[0:64, 1, hd * 128:(hd + 1) * 128], vtk[0:64, 8, hd * 128:(hd + 1) * 128],
                           True, True, [('keT', 1), ('vtk', 8, hd // 4)], [('ps', ub)])
                    for half, ub in ((0, u0), (1, u1)):
                        sv = S0[:, half * 4:(half + 1) * 4, :]
                        tt(sv, sv, ps[:, ub, :].rearrange("p (a b) -> p a b", a=4), ALU.add, [skey, ('ps', ub)], [skey])
                        tt(sv, sv, ebls[:, half * 4:(half + 1) * 4, sq:sq + 1].to_broadcast([128, 4, 128]), ALU.mult,
                           [skey] + [('ebls', hh) for hh in range(8)], [skey])
                    dma('sp', ss_o[sq].rearrange("h k v -> k h v"), S0, 'o_ss%d' % rr, reads=[skey])
                finish_o([(bo, 0, 8)], 64,
                         lambda h0_, nh: zT[:, h0_:h0_ + nh, TS:TS + 64],
                         lambda h0_, nh: sg[:, h0_:h0_ + nh, TS:TS + 64],
                         [('sg', hh, 2) for hh in range(8)], [('z', 2)])
                nb.release(bo)
                s.barrier(dma_prefix=('o_ss', 'l_s0'))
                memset(Sst, 0.0, ['Sst', ('Sst', 0), ('Sst', 1)])
            HS = [slice(0, 4), slice(4, 8)]
            for b in range(8):
                ti = b // 4
                cols = slice(b * 128, (b + 1) * 128)
                ek = [('ebr', hh, ti) for hh in range(8)] + [('ebl', hh, ti) for hh in range(8)] + [('wsc', hh, ti) for hh in range(8)]
                btr = nb()
                ptb = ps[:, btr, :].bitcast(BF16)
                for half in range(2):
                    qk = n16()
                    kdv = t16[:, qk, :].rearrange("p (h c l) -> p h c l", h=4, c=4)
                    tt(kdv, ke[:, HS[half], cols].rearrange("p h (c l) -> p h c l", c=4),
                       wsc[:, HS[half], b * 4:b * 4 + 4].unsqueeze(3).to_broadcast([128, 4, 4, 32]), ALU.mult,
                       [('ke', hh, ti) for hh in range(half * 4, half * 4 + 4)] + ek, [('t16', qk)], eng='pool')
                    for hl in range(4):
                        hd = half * 4 + hl
                        tr(ptb[:, hd * 128:(hd + 1) * 128], t16[:, qk, hl * 128:(hl + 1) * 128], ident_b[:], [('t16', qk), K], [('ps', btr)])
                kslot = b % 2
                cp(keT[:, kslot, :], ptb[:, :], [('ps', btr)], [('keT', kslot)], eng='act')
                pbanks = []
                for half in range(2):
                    bk = nb()
                    for hl in range(4):
                        hd = half * 4 + hl
                        mm(ps[:, bk, hl * 128:(hl + 1) * 128], ke[:, hd, cols], qe[:, hd, cols], True, True, [('ke', hd, ti), ('qe', hd, ti)], [('ps', bk)])
                    qp = n16()
                    tt(t16[:, qp, :].rearrange("p (a b) -> p a b", a=4), ps[:, bk, :].rearrange("p (a b) -> p a b", a=4),
                       hmask[:].unsqueeze(1).to_broadcast([128, 4, 128]), ALU.mult, [('ps', bk), K], [('t16', qp)])
                    pbanks.append(qp)
                obanks = [nb(), nb()]
                nb.hold(*obanks)
                for hd in range(8):
                    ob_ = obanks[hd // 4]
                    qp = pbanks[hd // 4]
                    hl = hd % 4
                    mm(ps[:, ob_, hl * 128:(hl + 1) * 128], vtk[:, b, hd * 128:(hd + 1) * 128], t16[:, qp, hl * 128:(hl + 1) * 128], hl == 0, False,
                       [('vtk', b, hd // 4), ('t16', qp)], [('ps', ob_)], skip_group_check=True)
                for c in range(4):
                    ch = b * 4 + c
                    for half in range(2):
                        tt(Sbf[:, HS[half], :], Sst[:, HS[half], :], ebr[:, HS[half], ch:ch + 1].to_broadcast([128, 4, 128]), ALU.mult,
                           [('Sst', half), 'Sst'] + ek, [('Sbf', half)])
                    us = [nb(), nb()]
                    for half in range(2):
                        for hl in range(4):
                            hd = half * 4 + hl
                            c0 = hl * 128 + c * 32
                            mm(ps[:, obanks[half], c0:c0 + 32], Sbf[:, hd, :], qe[:, hd, b * 128 + c * 32:b * 128 + (c + 1) * 32], False, True,
                               [('Sbf', half), ('qe', hd, ti)], [('ps', obanks[half])], skip_group_check=True)
                        for hl in range(4):
                            hd = half * 4 + hl
                            mm(ps[:, us[half], hl * 128:(hl + 1) * 128], keT[32 * c:32 * (c + 1), kslot, hd * 128:(hd + 1) * 128],
                               vtk[32 * c:32 * (c + 1), b, hd * 128:(hd + 1) * 128], True, True, [('keT', kslot), ('vtk', b, hd // 4)], [('ps', us[half])],
                               tile_position=(32 * c, 0))
                    for half in range(2):
                        tt(Sst[:, HS[half], :], Sst[:, HS[half], :], ebl[:, HS[half], ch:ch + 1].to_broadcast([128, 4, 128]), ALU.mult,
                           [('Sst', half), 'Sst'] + ek, [('Sst', half)], eng='pool')
                    for half in range(2):
                        tt(Sst[:, HS[half], :], Sst[:, HS[half], :], ps[:, us[half], :].rearrange("p (a b) -> p a b", a=4), ALU.add,
                           [('Sst', half), ('ps', us[half])], [('Sst', half)])
                finish_o([(obanks[0], 0, 4), (obanks[1], 4, 4)], 128,
                         lambda h0_, nh: zT[:, h0_:h0_ + nh, cols],
                         lambda h0_, nh: sg[:, h0_:h0_ + nh, cols],
                         [('sg', hh, ti) for hh in range(8)], [('z', ti)])
                nb.release(*obanks)
            if g == 1:
                dma('sp', sp_o.rearrange("h k v -> k h v"), Sst, 'o_sp', reads=['Sst', ('Sst', 0), ('Sst', 1)])
            s.barrier()
            Wo = odd_w_out

            def wchunk(wv, half):
                return [(wv, Wo[:, half * 512:(half + 1) * 512].rearrange("(c p) f -> p c f", p=128))]
            outproj(sub, tiles, wchunk, lambda kc, t0, n: zT[:, kc, t0:t0 + n], lambda kc, ti: [('z', ti), ('h', ti, kc)], ybt, hook)

        kcar = sb("kcar", [128, 128], BF16)
        vcar = sb("vcar", [128, 128], BF16)
        Sst = sb("Sst", [128, 8, 128], F32)[:]

        for g in groups:
            if g == 0:
                tiles = [(0, 512, 'p'), (512, 512, 'p'), (1024, 64, 's')]
            else:
                tiles = [(0, 512, 'p'), (512, 512, 'p')]
            xv = xT.rearrange("(c p) t -> p c t", p=128)
            for ti, (t0, n, kind) in enumerate(tiles):
                src_c0 = g * 1024 + t0 if kind == 'p' else NP
                dma('sp', x[:, :, t0:t0 + n], xv[:, :, src_c0:src_c0 + n], 'l_x%d' % ti, writes=[('x', ti, c) for c in range(8)])
            yv = yT.rearrange("(c p) t -> p c t", p=128)
            for l in range(2):
                for k3 in range(3):
                    sub = l * 3 + k3

                    def hook(ti, tile, sub=sub):
                        t0, n, kind = tile
                        if sub == 5:
                            dst_c0 = g * 1024 + t0 if kind == 'p' else NP
                            dma('sp', yv[:, :, dst_c0:dst_c0 + n], x[:, :, t0:t0 + n], 'o_y%d' % ti,
                                reads=[('x', ti, c) for c in range(8)])
                    enabled = do_ffn if k3 != 1 else (do_even if l == 0 else do_odd)
                    for ti, tile in enumerate(tiles):
                        prenorm_tile(sub, ti, tile)
                    if not enabled:
                        for ti, tile in enumerate(tiles):
                            hook(ti, tile)
                    elif k3 == 0 or k3 == 2:
                        ffn(l, 0 if k3 == 0 else 1, sub, tiles, hook)
                    elif l == 0:
                        even_mixer(sub, g, tiles, hook)
                    else:
                        odd_mixer(sub, g, tiles, hook)
                    s.barrier(wait_engs=('act', 'dve'))
            s.barrier()
        s.finish()
        global _LAST_COUNTS
        _LAST_COUNTS = (dict(s.cnt), {e: len(v) for e, v in s.q.items()}, s.nsem)
        with nc.Block() as block:
            s.emit(block)
    return nc


def _consts():
    c = {}
    c['ident'] = np.eye(128, dtype=np.float32)
    k = np.arange(128)[:, None]
    q = np.arange(128)[None, :]
    am = np.zeros((128, 2, 128), np.float32)
    am[:, 0, :] = (k <= q)
    am[:, 1, :] = (k > q)
    c['amask'] = am.reshape(128, 256)
    c['hmask'] = ((k // 32 == q // 32) & (k <= q)).astype(np.float32)
    k6 = np.arange(64)[:, None]
    q6 = np.arange(64)[None, :]
    c['smask'] = ((k6 // 4 == q6 // 4) & (k6 <= q6)).astype(np.float32)
    c['mcmask'] = (np.arange(128)[:, None] > np.arange(4)[None, :]).astype(np.float32)
    rm = np.ones((128, 576), np.float32)
    rm[:, 0:512:32] = 0.0
    rm[:, 512:576:4] = 0.0
    c['rmask'] = rm
    c['onehot'] = (np.arange(64)[:, None] // 4 == np.arange(16)[None, :]).astype(np.float32)
    pos = np.concatenate([np.arange(NP, dtype=np.int32), np.tile(16384 + np.arange(4, dtype=np.int32), 16)]).astype(np.float32)
    inv_freq = (np.float32(500000.0) ** (-(np.arange(0, 16, 2, dtype=np.float32)) / np.float32(16))).astype(np.float32)
    ang = (pos[None, :] * inv_freq[:, None]).astype(np.float32)
    cs = np.cos(ang).astype(np.float32)
    sn = np.sin(ang).astype(np.float32)
    rc = np.ones((64, NP + NS), np.float32)
    rs = np.zeros((64, NP + NS), np.float32)
    rc[0:8] = cs
    rc[8:16] = cs
    rs[0:8] = -sn
    rs[8:16] = sn
    c['ropec'] = np.concatenate([rc, rc], 0)
    c['ropes'] = np.concatenate([rs, rs], 0)
    return c


_NC_CACHE = {}


def kernel(x_prompt, x_sample, c_prompt, c_sample, cache_k_win, cache_v_win, state_conv_rglru,
           state_h_rglru, state_s_hgrn, norm_pre, norm_post, ada_w, ada_b, ffn1_w_in, ffn1_w_out,
           ffn2_w_in, ffn2_w_out, even_w_in, even_w_out, attn_sinks, rg_conv_w, rg_conv_b, rg_wa,
           rg_ba, rg_wx, rg_bx, rg_lambda, odd_w_in, odd_w_out, hgrn_lb_logits, hgrn_gnorm, _flags=None):
    f = lambda a: np.ascontiguousarray(np.asarray(a, dtype=np.float32))
    flags = dict(_flags or {})
    ncores_run = flags.pop('ncores', NCORES)
    key = tuple(sorted(flags.items()))
    if key not in _NC_CACHE:
        _NC_CACHE[key] = build_nc(**flags)
    nc = _NC_CACHE[key]
    consts = _consts()

    def fm(v, nchunk):
        v = np.asarray(v, np.float32)
        lead = v.shape[:-1]
        v = v.reshape(lead + (nchunk, 128))
        v = np.moveaxis(v, -1, 0)
        return np.ascontiguousarray(v)

    shared = {
        'normpre': fm(np.asarray(norm_pre).reshape(6, D), 8),
        'normpost': fm(np.asarray(norm_post).reshape(6, D), 8),
        'adab': fm(np.asarray(ada_b).reshape(6, 3 * D), 24),
        'ada_w': f(ada_w),
        'ffn1_w_in': f(ffn1_w_in), 'ffn2_w_in': f(ffn2_w_in), 'ffn1_w_out': f(ffn1_w_out), 'ffn2_w_out': f(ffn2_w_out),
        'even_w_in': f(even_w_in)[0], 'even_w_out': f(even_w_out)[0],
        'convw': np.ascontiguousarray(np.moveaxis(np.asarray(rg_conv_w, np.float32)[0].reshape(4, 4, 128), 2, 0).transpose(0, 2, 1)),
        'convb': fm(np.asarray(rg_conv_b)[0], 4),
        'rg_wa': f(rg_wa)[0], 'rg_wx': f(rg_wx)[0],
        'rgba': fm(np.asarray(rg_ba)[0], 4), 'rgbx': fm(np.asarray(rg_bx)[0], 4), 'rglam': fm(np.asarray(rg_lambda)[0], 4),
        'odd_w_in': f(odd_w_in)[0], 'odd_w_out': f(odd_w_out)[0],
        'lbl': fm(np.asarray(hgrn_lb_logits), 8),
        'gnorm': f(np.asarray(hgrn_gnorm)[0].reshape(128, 1)),
    }
    sk = np.asarray(attn_sinks, np.float32)[0]
    sT = np.zeros((128, 4), np.float32)
    sT[0:64, :] = sk[0:4][None, :]
    sT[64:128, :] = sk[4:8][None, :]
    shared['sinkT'] = sT
    shared.update(consts)
    cwv = np.asarray(rg_conv_w, np.float32)[0].reshape(4, 4, 128)
    shared['convw'] = np.ascontiguousarray(cwv.transpose(2, 1, 0))

    xp = np.asarray(x_prompt, np.float32)
    xs = np.asarray(x_sample, np.float32)
    in_maps = []
    for i in range(NCORES):
        m = dict(shared)
        sl = slice(16 * i, 16 * i + 16)
        xt = np.concatenate([xp[i], xs[sl].reshape(64, D)], 0)
        m['xT'] = np.ascontiguousarray(xt.T)
        ct = np.concatenate([np.asarray(c_prompt, np.float32)[i:i + 1], np.asarray(c_sample, np.float32)[sl]], 0)
        m['cT'] = np.ascontiguousarray(ct.T)
        m['cachek'] = f(np.asarray(cache_k_win)[0, sl].reshape(16, 128, 128))
        m['cachev'] = f(np.asarray(cache_v_win)[0, sl].reshape(16, 128, 128))
        cs_ = np.asarray(state_conv_rglru, np.float32)[0, sl]
        m['convS'] = np.ascontiguousarray(cs_.reshape(16, 3, 4, 128).transpose(3, 2, 0, 1))
        hs_ = np.asarray(state_h_rglru, np.float32)[0, sl]
        m['h0S'] = np.ascontiguousarray(hs_.reshape(16, 4, 128).transpose(2, 1, 0))
        m['s0S'] = f(np.asarray(state_s_hgrn)[0, sl])
        in_maps.append(m)
    res = run_bass_kernel_spmd(nc, in_maps[:ncores_run], core_ids=list(range(ncores_run)))
    R = list(res.results)
    while len(R) < NCORES:
        R.append(R[0])
    y_prompt = np.stack([R[i]['yT'][:, :NP].T for i in range(NCORES)], 0)
    y_sample = np.concatenate([R[i]['yT'][:, NP:].T.reshape(16, 4, D) for i in range(NCORES)], 0)
    k_win_p = np.stack([R[i]['kwp'].T.reshape(128, 2, 64) for i in range(NCORES)], 0)[None]
    v_win_p = np.stack([R[i]['vwp'].reshape(128, 2, 64) for i in range(NCORES)], 0)[None]
    conv_p = np.stack([R[i]['convp'].transpose(2, 1, 0).reshape(3, 512) for i in range(NCORES)], 0)[None]
    h_p = np.stack([R[i]['hp'].T.reshape(512) for i in range(NCORES)], 0)[None]
    s_p = np.stack([R[i]['sp_o'] for i in range(NCORES)], 0)[None]
    k_win_s = np.concatenate([R[i]['kws'].reshape(16, 128, 2, 64) for i in range(NCORES)], 0)[None]
    v_win_s = np.concatenate([R[i]['vws'].reshape(16, 128, 2, 64) for i in range(NCORES)], 0)[None]
    conv_s = np.concatenate([R[i]['convs'].transpose(2, 3, 1, 0).reshape(16, 3, 512) for i in range(NCORES)], 0)[None]
    h_s = np.concatenate([R[i]['hs_o'].transpose(2, 1, 0).reshape(16, 512) for i in range(NCORES)], 0)[None]
    s_s = np.concatenate([R[i]['ss_o'] for i in range(NCORES)], 0)[None]
    outs = (y_prompt, y_sample, k_win_p, v_win_p, conv_p, h_p, s_p, k_win_s, v_win_s, conv_s, h_s, s_s)
    return tuple(np.ascontiguousarray(o, dtype=np.float32) for o in outs)
```
